# Optimizing a Trainium2 kernel written in Bass

```python
import jax, jax.numpy as jnp
from jax import lax
import numpy as np

D_MODEL = 1024
BATCH = 2
SEQ = 8192
DEPTH = 1

CHUNK = 64
SUB_CHUNK = 16
N_SUB = CHUNK // SUB_CHUNK
N_MEM = 256
M_HEADS = 4
M_DK = 128
M_DV = 128
M_QK = M_HEADS * M_DK
M_WIDTH = M_HEADS * M_DV
CONV_W = 4
G_HEADS = 4
G_DK = 64
G_DV = 128
G_QK = G_HEADS * G_DK
G_WIDTH = G_HEADS * G_DV
G_RANK = 16
G_TAU = 16.0
MIX_WIDTH = M_WIDTH + G_WIDTH
X_HEADS = 4
X_DH = D_MODEL // X_HEADS
D_FF = 4 * D_MODEL
ALPHA = (2.0 * DEPTH) ** 0.25
BETA = (8.0 * DEPTH) ** -0.25
LN_EPS = 1e-5
IN_SIZES = (M_QK, M_QK, M_WIDTH, M_WIDTH, M_HEADS, M_HEADS, G_QK, G_QK, G_WIDTH, G_WIDTH, G_RANK)
IN_COLS = sum(IN_SIZES)
IN_SPLITS = tuple(int(s) for s in np.cumsum(IN_SIZES)[:-1])

kernel_name = "hymba_mlstm_gla_deepnorm_memxattn"


def layer_norm(x, g, b):
    xf = x.astype(jnp.float32)
    mu = jnp.mean(xf, -1, keepdims=True)
    var = jnp.mean(jnp.square(xf - mu), -1, keepdims=True)
    return ((xf - mu) * lax.rsqrt(var + LN_EPS) * g + b).astype(x.dtype)


def head_layer_norm(h, g):
    mu = jnp.mean(h, -1, keepdims=True)
    var = jnp.mean(jnp.square(h - mu), -1, keepdims=True)
    hn = (h - mu) * lax.rsqrt(var + LN_EPS)
    return hn.reshape(h.shape[0], h.shape[1], -1) * g


def head_rms_norm(h, g):
    hn = h * lax.rsqrt(jnp.mean(jnp.square(h), -1, keepdims=True) + LN_EPS)
    return hn.reshape(h.shape[0], h.shape[1], -1) * g


def causal_depthwise_conv(u, w, b):
    T = u.shape[1]
    up = jnp.pad(u, ((0, 0), (CONV_W - 1, 0), (0, 0)))
    return sum(up[:, j:j + T] * w[j] for j in range(CONV_W)) + b


def to_heads(u, n_heads):
    B_, T, C = u.shape
    return u.reshape(B_, T, n_heads, C // n_heads).transpose(0, 2, 1, 3)


def mlstm_chunkwise(q, k, v, i_pre, f_pre):
    B_, H, T, DK = q.shape
    DV = v.shape[-1]
    NC = T // CHUNK
    q = q.reshape(B_, H, NC, CHUNK, DK)
    k = k.reshape(B_, H, NC, CHUNK, DK) * (DK ** -0.5)
    v = v.reshape(B_, H, NC, CHUNK, DV)
    log_f = jax.nn.log_sigmoid(f_pre).reshape(B_, H, NC, CHUNK)
    log_i = i_pre.reshape(B_, H, NC, CHUNK)
    b = jnp.cumsum(log_f, -1)
    g = b[..., -1]
    a = g[..., None] - b + log_i
    a_max = jnp.max(a, -1)
    w = jnp.exp(a - a_max[..., None])
    U = jnp.einsum('bhcl,bhcld,bhcle->bhcde', w, k, v)
    u = jnp.einsum('bhcl,bhcld->bhcd', w, k)

    def step(carry, inp):
        C, n, m = carry
        U_c, u_c, g_c, am_c = inp
        m_new = jnp.maximum(g_c + m, am_c)
        dec = jnp.exp(g_c + m - m_new)
        inj = jnp.exp(am_c - m_new)
        C_new = dec[..., None, None] * C + inj[..., None, None] * U_c
        n_new = dec[..., None] * n + inj[..., None] * u_c
        return (C_new, n_new, m_new), (C, n, m)

    init = (jnp.zeros((B_, H, DK, DV), jnp.float32), jnp.zeros((B_, H, DK), jnp.float32),
            jnp.zeros((B_, H), jnp.float32))
    xs = (jnp.moveaxis(U, 2, 0), jnp.moveaxis(u, 2, 0), jnp.moveaxis(g, 2, 0), jnp.moveaxis(a_max, 2, 0))
    _, (C_prev, n_prev, m_prev) = lax.scan(step, init, xs)
    C_prev = jnp.moveaxis(C_prev, 0, 2)
    n_prev = jnp.moveaxis(n_prev, 0, 2)
    m_prev = jnp.moveaxis(m_prev, 0, 2)

    causal = jnp.tril(jnp.ones((CHUNK, CHUNK), bool))
    D = b[..., :, None] - b[..., None, :] + log_i[..., None, :]
    D = jnp.where(causal, D, -jnp.inf)
    inter = b + m_prev[..., None]
    m_t = jnp.maximum(inter, jnp.max(D, -1))
    Wts = jnp.exp(D - m_t[..., None])
    sc = jnp.exp(inter - m_t)
    S = jnp.einsum('bhcld,bhcsd->bhcls', q, k) * Wts
    num = (sc[..., None] * jnp.einsum('bhcld,bhcde->bhcle', q, C_prev)
           + jnp.einsum('bhcls,bhcse->bhcle', S, v))
    den = sc * jnp.einsum('bhcld,bhcd->bhcl', q, n_prev) + jnp.sum(S, -1)
    h = num / jnp.maximum(jnp.abs(den), jnp.exp(-m_t))[..., None]
    return h.reshape(B_, H, T, DV)


def gla_chunked(q, k, v, log_a):
    B_, H, T, DK = q.shape
    DV = v.shape[-1]
    NC = T // CHUNK
    shp = (B_, H, NC, N_SUB, SUB_CHUNK)
    q = q.reshape(*shp, DK) * (DK ** -0.5)
    k = k.reshape(*shp, DK)
    la = log_a.reshape(*shp, DK)
    bc = jnp.cumsum(la.reshape(B_, H, NC, CHUNK, DK), axis=3).reshape(*shp, DK)
    b_start = bc[..., :, :1, :] - la[..., :, :1, :]
    b_end = bc[..., :, -1, :]
    causal_s = jnp.tril(jnp.ones((SUB_CHUNK, SUB_CHUNK), bool))
    expo = bc[..., :, None, :] - bc[..., None, :, :]
    expo = jnp.where(causal_s[..., None], expo, -jnp.inf)
    a_diag = jnp.sum(q[..., :, None, :] * k[..., None, :, :] * jnp.exp(expo), -1)
    q_hat = q * jnp.exp(bc - b_start)
    k_hat = k * jnp.exp(b_end[..., :, None, :] - bc)
    mid = b_start[..., :, 0, :][..., :, None, :] - b_end[..., None, :, :]
    earlier = jnp.tril(jnp.ones((N_SUB, N_SUB), bool), -1)
    mid = jnp.exp(jnp.where(earlier[..., None], mid, -jnp.inf))
    a_off = jnp.einsum('bhcjtd,bhcjid,bhcisd->bhcjtis', q_hat, mid, k_hat)
    eye = jnp.eye(N_SUB, dtype=a_off.dtype)[:, None, :, None]
    A = (a_off + eye * a_diag[..., :, :, None, :]).reshape(B_, H, NC, CHUNK, CHUNK)
    vf = v.reshape(B_, H, NC, CHUNK, DV)
    o_intra = jnp.einsum('bhcts,bhcse->bhcte', A, vf)
    bcf = bc.reshape(B_, H, NC, CHUNK, DK)
    qf = q.reshape(B_, H, NC, CHUNK, DK)
    kf = k.reshape(B_, H, NC, CHUNK, DK)
    g = bcf[..., -1, :]
    U = jnp.einsum('bhcld,bhcle->bhcde', kf * jnp.exp(g[..., None, :] - bcf), vf)

    def step(S, inp):
        U_c, g_c = inp
        return jnp.exp(g_c)[..., None] * S + U_c, S

    _, S_prev = lax.scan(step, jnp.zeros((B_, H, DK, DV), jnp.float32),
                         (jnp.moveaxis(U, 2, 0), jnp.moveaxis(g, 2, 0)))
    S_prev = jnp.moveaxis(S_prev, 0, 2)
    o_inter = jnp.einsum('bhcld,bhcde->bhcle', qf * jnp.exp(bcf), S_prev)
    return (o_intra + o_inter).reshape(B_, H, T, DV)


def hybrid_mixer(x, w_in, conv_w, conv_b, m_i_bias, m_f_bias, m_norm_g, g_lr_w, g_lr_b, g_norm_g, w_out):
    proj = (x @ w_in).astype(jnp.float32)
    mq, mk, mv, mo, mi, mf, gq, gk, gv, gg, glr = jnp.split(proj, IN_SPLITS, axis=-1)
    qk = jax.nn.silu(causal_depthwise_conv(jnp.concatenate([mq, mk], -1),
                                           conv_w.astype(jnp.float32), conv_b.astype(jnp.float32)))
    mq, mk = jnp.split(qk, 2, axis=-1)
    i_pre = (mi + m_i_bias.astype(jnp.float32)).transpose(0, 2, 1)
    f_pre = (mf + m_f_bias.astype(jnp.float32)).transpose(0, 2, 1)
    hm = mlstm_chunkwise(to_heads(mq, M_HEADS), to_heads(mk, M_HEADS), to_heads(mv, M_HEADS), i_pre, f_pre)
    m_out = jax.nn.sigmoid(mo) * head_layer_norm(hm.transpose(0, 2, 1, 3), m_norm_g.astype(jnp.float32))
    log_a = jax.nn.log_sigmoid(glr @ g_lr_w.astype(jnp.float32) + g_lr_b.astype(jnp.float32)) / G_TAU
    hg = gla_chunked(to_heads(gq, G_HEADS), to_heads(gk, G_HEADS), to_heads(gv, G_HEADS), to_heads(log_a, G_HEADS))
    g_out = jax.nn.silu(gg) * head_rms_norm(hg.transpose(0, 2, 1, 3), g_norm_g.astype(jnp.float32))
    y = jnp.concatenate([m_out, g_out], -1).astype(x.dtype)
    return y @ w_out


def memory_cross_attention(x, mem, w_q, w_k, w_v, w_o):
    B_, T, _ = x.shape
    q = (x @ w_q).reshape(B_, T, X_HEADS, X_DH)
    k = (mem @ w_k).reshape(B_, N_MEM, X_HEADS, X_DH)
    v = (mem @ w_v).reshape(B_, N_MEM, X_HEADS, X_DH)
    s = jnp.einsum('bthd,bmhd->bhtm', q, k).astype(jnp.float32) * (X_DH ** -0.5)
    p = jax.nn.softmax(s, axis=-1).astype(v.dtype)
    o = jnp.einsum('bhtm,bmhd->bthd', p, v).reshape(B_, T, D_MODEL)
    return o @ w_o


def squared_relu_mlp(x, w_ff1, w_ff2):
    return jnp.square(jax.nn.relu(x @ w_ff1)) @ w_ff2


def setup_inputs(seed: int = 0) -> dict:
    key = jax.random.key(seed)
    ks = jax.random.split(key, 28)
    L = DEPTH

    def nrm(k, shape, scale):
        return jax.random.normal(k, shape, jnp.float32) * scale

    return {
        "x": nrm(ks[0], (BATCH, SEQ, D_MODEL), 1.0),
        "mem": nrm(ks[1], (BATCH, N_MEM, D_MODEL), 1.0),
        "ln_in_g": 1.0 + nrm(ks[2], (D_MODEL,), 0.02),
        "ln_in_b": nrm(ks[3], (D_MODEL,), 0.02),
        "w_in": nrm(ks[4], (L, D_MODEL, IN_COLS), D_MODEL ** -0.5),
        "conv_w": nrm(ks[5], (L, CONV_W, 2 * M_QK), CONV_W ** -0.5),
        "conv_b": nrm(ks[6], (L, 2 * M_QK), 0.02),
        "m_i_bias": nrm(ks[7], (L, M_HEADS), 0.1),
        "m_f_bias": jnp.linspace(3.0, 6.0, M_HEADS, dtype=jnp.float32)[None] + nrm(ks[8], (L, M_HEADS), 0.1),
        "m_norm_g": 1.0 + nrm(ks[9], (L, M_WIDTH), 0.02),
        "g_lr_w": nrm(ks[10], (L, G_RANK, G_QK), G_RANK ** -0.5),
        "g_lr_b": nrm(ks[11], (L, G_QK), 0.02),
        "g_norm_g": 1.0 + nrm(ks[12], (L, G_WIDTH), 0.02),
        "w_out": nrm(ks[13], (L, MIX_WIDTH, D_MODEL), MIX_WIDTH ** -0.5 * BETA),
        "ln1_g": 1.0 + nrm(ks[14], (L, D_MODEL), 0.02),
        "ln1_b": nrm(ks[15], (L, D_MODEL), 0.02),
        "x_wq": nrm(ks[16], (L, D_MODEL, D_MODEL), D_MODEL ** -0.5),
        "x_wk": nrm(ks[17], (L, D_MODEL, D_MODEL), D_MODEL ** -0.5),
        "x_wv": nrm(ks[18], (L, D_MODEL, D_MODEL), D_MODEL ** -0.5 * BETA),
        "x_wo": nrm(ks[19], (L, D_MODEL, D_MODEL), D_MODEL ** -0.5 * BETA),
        "ln2_g": 1.0 + nrm(ks[20], (L, D_MODEL), 0.02),
        "ln2_b": nrm(ks[21], (L, D_MODEL), 0.02),
        "w_ff1": nrm(ks[22], (L, D_MODEL, D_FF), D_MODEL ** -0.5 * BETA),
        "w_ff2": nrm(ks[23], (L, D_FF, D_MODEL), D_FF ** -0.5 * BETA),
        "ln3_g": 1.0 + nrm(ks[24], (L, D_MODEL), 0.02),
        "ln3_b": nrm(ks[25], (L, D_MODEL), 0.02),
    }


def reference(x, mem, ln_in_g, ln_in_b, w_in, conv_w, conv_b, m_i_bias, m_f_bias, m_norm_g,
              g_lr_w, g_lr_b, g_norm_g, w_out, ln1_g, ln1_b, x_wq, x_wk, x_wv, x_wo,
              ln2_g, ln2_b, w_ff1, w_ff2, ln3_g, ln3_b):
    h = layer_norm(x, ln_in_g, ln_in_b)
    for l in range(DEPTH):
        mix = hybrid_mixer(h, w_in[l], conv_w[l], conv_b[l], m_i_bias[l], m_f_bias[l], m_norm_g[l],
                           g_lr_w[l], g_lr_b[l], g_norm_g[l], w_out[l])
        h = layer_norm(ALPHA * h + mix, ln1_g[l], ln1_b[l])
        xa = memory_cross_attention(h, mem, x_wq[l], x_wk[l], x_wv[l], x_wo[l])
        h = layer_norm(ALPHA * h + xa, ln2_g[l], ln2_b[l])
        ff = squared_relu_mlp(h, w_ff1[l], w_ff2[l])
        h = layer_norm(ALPHA * h + ff, ln3_g[l], ln3_b[l])
    return h
```

```python
import math
import os
import numpy as np
import concourse.bass as bass
import concourse.mybir as mybir
from concourse.bass_utils import run_bass_kernel_spmd
from contextlib import ExitStack

F32 = mybir.dt.float32
BF16 = mybir.dt.bfloat16
AF = mybir.ActivationFunctionType
ALU = mybir.AluOpType
AX = mybir.AxisListType

D = 1024
NT_MAIN = 16
NT_PRE = 48
NT = NT_MAIN + NT_PRE
ALPHA = 2.0 ** 0.25
EPS = 1e-5
NMEM = 256
DFF = 4096
KDMA = 6
ENGS = ["pe", "act", "dve", "pool", "sp"]


class Sched:
    def __init__(self):
        self.ops = {e: [] for e in ENGS}
        self.cnt = {e: 0 for e in ENGS}
        self.lastw = {}
        self.readers = {}
        self.waited = {e: {} for e in ENGS}
        self.dma_n = {e: 0 for e in ENGS}
        self.rec = None

    def record(self, f, *a):
        self.rec = []
        f(*a)
        r, self.rec = self.rec, None
        return r

    def replay(self, *lists):
        pos = [0] * len(lists)
        while True:
            best, bf = None, 2.0
            for j, l in enumerate(lists):
                if pos[j] < len(l):
                    fr = pos[j] / len(l)
                    if fr < bf:
                        best, bf = j, fr
            if best is None:
                break
            kind, a = lists[best][pos[best]]
            pos[best] += 1
            getattr(self, kind)(*a)

    class _FakeIns:
        def then_inc(self, *a, **k):
            return self

    class _FakeEng:
        def __init__(self):
            self.calls = []

        def __getattr__(self, name):
            def f(*a, **k):
                self.calls.append((name, a, k))
                return Sched._FakeIns()
            return f

    @staticmethod
    def _free(ap):
        try:
            sh = list(ap.shape)
            n = 1
            for x in sh[1:]:
                n *= int(x)
            return n
        except Exception:
            return 128

    def est_cost(self, kind, eng, fn):
        if fn is None:
            return 0.0
        fe = Sched._FakeEng()
        try:
            fn(fe)
        except Exception:
            return 0.5
        c = 0.0
        for (name, a, k) in fe.calls:
            out = k.get("out", a[0] if a else None)
            n = Sched._free(out) if out is not None else 128
            if name == "matmul":
                rhs = k.get("rhs", a[2] if len(a) > 2 else None)
                nn = Sched._free(rhs) if rhs is not None else 128
                c += max(nn, 64) / 1500.0 + 0.015
            elif name == "transpose":
                c += 0.1
            elif name == "dma_start":
                c += 0.15
            elif eng == "act":
                c += 0.2 + n * 0.00087
            elif eng == "dve":
                c += 0.1 + n * 0.0011
            elif eng == "pool":
                c += 0.3 + n * 0.002
            else:
                c += 0.2
        return c + 0.08

    def schedule(self, *streams):
        if not hasattr(self, "t_eng"):
            self.t_eng = {e: 0.0 for e in ENGS}
            self.t_w = {}
            self.t_r = {}
        streams = [list(x) for x in streams if x]
        wsets, rsets = [], []
        for st in streams:
            ws, rs = set(), set()
            for (kind, a) in st:
                rs.update(self.norm(k) for k in a[2])
                ws.update(self.norm(k) for k in a[3])
            wsets.append(ws)
            rsets.append(rs)
        for i1 in range(len(streams)):
            for i2 in range(len(streams)):
                if i1 != i2:
                    bad = wsets[i1] & (wsets[i2] | rsets[i2])
                    assert not bad, ("streams share scratch", i1, i2, sorted(bad))
        costs = [[self.est_cost(kind, a[0], a[1]) for (kind, a) in st] for st in streams]
        pos = [0] * len(streams)
        while True:
            best, bt = None, None
            for j, st in enumerate(streams):
                if pos[j] >= len(st):
                    continue
                kind, a = st[pos[j]]
                eng, fn, reads, writes = a
                t = self.t_eng[eng]
                for k in reads:
                    t = max(t, self.t_w.get(self.norm(k), 0.0))
                for k in writes:
                    nk = self.norm(k)
                    t = max(t, self.t_w.get(nk, 0.0), self.t_r.get(nk, 0.0))
                if bt is None or t < bt - 1e-9:
                    best, bt = j, t
            if best is None:
                break
            kind, a = streams[best][pos[best]]
            eng, fn, reads, writes = a
            c = costs[best][pos[best]]
            pos[best] += 1
            fin = bt + c
            self.t_eng[eng] = bt + (0.15 if kind == "dma" else c)
            done = fin + (2.0 if kind == "dma" else 0.0)
            for k in writes:
                self.t_w[self.norm(k)] = done
            for k in reads:
                nk = self.norm(k)
                if self.t_r.get(nk, 0.0) < done:
                    self.t_r[nk] = done
            getattr(self, kind)(*a)

    @staticmethod
    def norm(k):
        if k == "PS_T":
            return "ps0"
        if k == "F0":
            return "ps1"
        if k == "F1":
            return "ps2"
        if k == "K0" or k.startswith("K0o"):
            return "ps3"
        if k == "K1" or k.startswith("K1o"):
            return "ps4"
        if k.startswith("G_"):
            return "ps5"
        if k.startswith("M0"):
            return "ps6"
        if k.startswith("M1"):
            return "ps7"
        return k

    def _deps(self, eng, reads, writes):
        deps = {}

        def add(tok, war=False):
            if tok is None:
                return
            sk, val = tok
            if sk == ("e", eng) and eng == "pe":
                return
            if deps.get(sk, 0) < val:
                deps[sk] = val

        for k in reads:
            add(self.lastw.get(k))
        for k in writes:
            add(self.lastw.get(k))
            for sk, val in self.readers.get(k, {}).items():
                add((sk, val), war=True)
        out = []
        w = self.waited[eng]
        for sk, val in deps.items():
            if w.get(sk, 0) < val:
                w[sk] = val
                out.append((sk, val))
        return out

    def _commit(self, tok, reads, writes):
        for k in writes:
            self.lastw[k] = tok
            self.readers[k] = {}
        for k in reads:
            r = self.readers.setdefault(k, {})
            if r.get(tok[0], 0) < tok[1]:
                r[tok[0]] = tok[1]

    def op(self, eng, fn, reads=(), writes=()):
        if self.rec is not None:
            self.rec.append(("op", (eng, fn, reads, writes)))
            return
        reads = tuple(dict.fromkeys(self.norm(k) for k in reads))
        writes = tuple(dict.fromkeys(self.norm(k) for k in writes))
        waits = self._deps(eng, reads, writes)
        self.cnt[eng] += 1
        tok = (("e", eng), self.cnt[eng])
        self.ops[eng].append((fn, waits, ("e", eng), 1))
        self._commit(tok, reads, writes)

    def dma(self, q, fn, reads=(), writes=()):
        if self.rec is not None:
            self.rec.append(("dma", (q, fn, reads, writes)))
            return
        reads = tuple(dict.fromkeys(self.norm(k) for k in reads))
        writes = tuple(dict.fromkeys(self.norm(k) for k in writes))
        j = self.dma_n[q]
        self.dma_n[q] += 1
        slot = j % KDMA
        val = 16 * (j // KDMA + 1)
        sk = ("d", q, slot)
        waits = self._deps(q, reads, writes)
        if val > 16 and self.waited[q].get(sk, 0) < val - 16:
            self.waited[q][sk] = val - 16
            waits.append((sk, val - 16))
        self.ops[q].append((fn, waits, sk, 16))
        self._commit((sk, val), reads, writes)

    def fence(self, eng, keys):
        keys = tuple(dict.fromkeys(self.norm(k) for k in keys))
        waits = self._deps(eng, keys, keys)
        self.ops[eng].append((None, waits, None, 0))


def build_program(nt_pre=48, nt_main=16, stop_after=None):
    global NT_MAIN, NT_PRE, NT
    NT_MAIN, NT_PRE = nt_main, nt_pre
    NT = NT_MAIN + NT_PRE
    TGN = max(1, NT_MAIN // 4)
    TPG = NT_MAIN // TGN
    nc = bass.Bass("TRN2", target_bir_lowering=False)

    def din(name, shape):
        return nc.dram_tensor(name, list(shape), F32, kind="ExternalInput").ap()

    xall = din("xall", [NT * 128, D])
    pmask_d = din("pmask", [128, NT])
    mem_d = din("mem", [NMEM, D])
    ln_g = {k: din(k + "_g", [D]) for k in ("ln_in", "ln1", "ln2", "ln3")}
    ln_b = {k: din(k + "_b", [D]) for k in ("ln_in", "ln1", "ln2", "ln3")}
    w_in = din("w_in", [D, 3608])
    conv_w = din("conv_w", [4, D])
    conv_b = din("conv_b", [D])
    m_i_bias = din("m_i_bias", [4])
    m_f_bias = din("m_f_bias", [4])
    m_norm_g = din("m_norm_g", [512])
    g_lr_w = din("g_lr_w", [16, 256])
    g_lr_b = din("g_lr_b", [256])
    g_norm_g = din("g_norm_g", [512])
    w_out = din("w_out", [D, D])
    x_wq = din("x_wq", [D, D])
    x_wk = din("x_wk", [D, D])
    x_wv = din("x_wv", [D, D])
    x_wo = din("x_wo", [D, D])
    w_ff1 = din("w_ff1", [D, DFF])
    w_ff2 = din("w_ff2", [DFF, D])
    out_d = nc.dram_tensor("out", [NT_MAIN * 128, D], F32, kind="ExternalOutput").ap()

    S = Sched()
    es = ExitStack()

    def sb(name, shape, dt=F32):
        return es.enter_context(nc.sbuf_tensor(name, list(shape), dt))

    NG4 = NT_PRE // 4 if (NT_PRE >= 4 and NT_PRE % 4 == 0) else 0
    HTILES = max(NT_MAIN, 16) if NG4 > 0 else NT_MAIN
    Hflat = sb("Hflat", [128, HTILES * D])
    H = Hflat[:, 0:NT_MAIN * D].rearrange("p (a b) -> p a b", a=NT_MAIN)
    Hbf = Hflat.bitcast(BF16)
    HTflat = sb("HTflat", [128, max(8 * NT_MAIN * 128, 16384)], BF16)
    HT = HTflat[:, 0:8 * NT_MAIN * 128].rearrange("p (a b) -> p a b", a=8)
    ARENA_BYTES = 73 * 1024
    S2_BYTES = 22 * 1024 + 768
    arena = sb("arena", [128, ARENA_BYTES // 2], BF16)
    s2 = sb("s2", [128, S2_BYTES // 2], BF16)
    regions = {"arena": (arena, ARENA_BYTES), "ht": (HTflat, 32 * 1024), "s2": (s2, S2_BYTES),
               "hp": (Hbf, HTILES * D * 4)}
    bump = {}

    def cv(region, phase, shape, dt=BF16):
        t, cap = regions[region]
        off = bump.get((region, phase), 0)
        off = (off + 31) // 32 * 32
        n = int(np.prod(shape[1:]))
        esz = 2 if dt == BF16 else 4
        assert off + n * esz <= cap, (region, phase, off, n * esz, cap)
        bump[(region, phase)] = off + n * esz
        a = t[0:shape[0], off // 2: off // 2 + n * esz // 2]
        if dt != BF16:
            a = a.bitcast(dt)
        if len(shape) == 3:
            a = a.rearrange("p (a b) -> p a b", a=shape[1])
        return a

    Wf = cv("arena", "A", [128, 8, 1536])
    WtA = cv("arena", "A", [128, 8, 1024])
    Ws = cv("arena", "A", [128, 8, 24])
    WTB_OFF = bump[("arena", "A")]
    WtB = cv("arena", "A", [128, 8, 1024])
    Wo_ = cv("arena", "A", [128, 8, 1024])
    Wq_ = cv("arena", "B", [128, 8, 1024])
    Wx_ = cv("arena", "B", [128, 8, 1024])
    Wk_ = cv("arena", "B", [128, 8, 1024])
    MEMT = cv("arena", "B", [128, 8, NMEM])
    KT = cv("arena", "B", [128, 8, NMEM])
    VV = cv("arena", "B", [128, 2, D])
    MHB = cv("arena", "B", [128, D])
    LNG2 = cv("arena", "B", [128, D], F32)
    LNB2 = cv("arena", "B", [128, D], F32)
    W1s, W2s = [], []
    for _ in range(2):
        W1s.append(cv("arena", "C", [128, 8, 1024]))
        W2s.append(cv("arena", "C", [128, 8, 1024]))
    RL2 = [cv("arena", "C", [128, 512], F32) for i in range(2)]
    XS_C = cv("arena", "C", [128, D], F32)

    ident_bf = sb("ident_bf", [128, 128], BF16)
    ident_f = sb("ident_f", [128, 128])
    ones_f = sb("ones_f", [128, 128])
    zeros_f = sb("zeros_f", [128, 128])
    mask_ut = sb("mask_ut", [128, 128], BF16)
    negh = sb("negh", [128, 4])
    LNG = sb("LNG", [128, D])
    LNB = sb("LNB", [128, D])
    CW = sb("CW", [128, 4, 8])
    CB = sb("CB", [128, 8])
    GB = sb("GB", [4, 4])
    GLW = sb("GLW", [16, 256])
    GLB = sb("GLB", [128, 2])
    PM = sb("PM", [128, NT])
    ST6 = sb("ST6", [128, 2, 6])
    MV = sb("MV", [128, 2])
    RS = sb("RS", [128, 4])
    EI = sb("EI", [4, 128])
    EF = sb("EF", [4, 128])
    Z = sb("Z", [4, 256])
    RL = sb("RL", [4, 1])
    D4 = sb("D4", [4, 4])
    SC = sb("SC", [128, 12])
    SM = sb("SM", [128, 16])
    ST6b = sb("ST6b", [128, 6])
    MVb = sb("MVb", [128, 2])
    ST6c = sb("ST6c", [128, 2, 6])
    MVc = sb("MVc", [128, 2])
    RSc = sb("RSc", [128, 4])
    ST6b2 = sb("ST6b2", [128, 2, 6])
    MVb2 = sb("MVb2", [128, 2, 2])
    GLR = sb("GLR", [16, 128])
    MX = sb("MX", [128, 4])
    NB = sb("NB", [128, 4])
    RSUM = sb("RSUM", [128, 4])
    RINVb = [sb(f"RINV{i}", [128, 4]) for i in range(2)]
    XB = [cv("ht", "A", [128, D], F32)]
    HB_A = cv("ht", "A", [128, D])
    XS_A = cv("ht", "A", [128, D], F32)
    RAW = cv("ht", "A", [128, 8, 131], F32)
    CV = cv("ht", "A", [128, 4, 128], F32)
    OGT = cv("ht", "A", [128, 512], F32)
    TH = OGT.rearrange("p (a b) -> p a b", a=4)
    OGb = [cv("ht", "A", [128, D], F32) for _ in range(2)]
    QKb = [cv("s2", "A", [128, 8, 128]), cv("ht", "A", [128, 8, 128])]
    VEb = [cv("s2", "A", [128, 4, 129]), cv("ht", "A", [128, 4, 129])]
    GVb = [cv("s2", "A", [128, 512]), cv("ht", "A", [128, 512])]
    KHb = [cv("s2", "A", [128, 2, 128]), cv("ht", "A", [128, 2, 128])]
    QHb = [cv("s2", "A", [128, 2, 128]), cv("ht", "A", [128, 2, 128])]
    HTt = [cv("s2", "A", [128, 8, 128])]
    YTt = cv("s2", "A", [128, 8, 128])
    KW = cv("s2", "A", [128, 4, 128])
    Cf = cv("s2", "A", [128, 4, 129], F32)
    Cb = cv("s2", "A", [128, 4, 129])
    SW = [cv("s2", "A", [128, 128]) for i in range(2)]
    TT2 = cv("s2", "A", [128, 2, 128], F32)
    TT = TT2[:, 0, :]
    E2 = cv("s2", "A", [128, 2, 128], F32)
    AC = cv("s2", "A", [128, 2, 128], F32)
    AI = cv("s2", "A", [128, 2, 128], F32)
    KHt = cv("s2", "A", [128, 2, 128])
    ATb = cv("s2", "A", [128, 4, 128])
    Sf = cv("s2", "A", [128, 2, 128], F32)
    Sb_ = cv("s2", "A", [128, 2, 128])
    Y = cv("s2", "A", [128, D])
    if NG4 > 0:
        XGall = cv("hp", "P", [128, 2 * D], F32)
        XG = [XGall[:, 0:D], XGall[:, D:2 * D]]
        HTG1 = cv("hp", "P", [128, 8, 512])
        RAWG = cv("hp", "P", [128, 4, 516], F32)
        CVG = cv("hp", "P", [128, 4, 512], F32)
        bump[("arena", "P")] = WTB_OFF
        HTG2 = cv("arena", "P", [128, 8, 512])
        THG = cv("arena", "P", [128, 4, 512], F32)
        QKG = [cv("hp", "P", [128, 4, 512]), cv("arena", "P", [128, 4, 512])]
        VEG = [cv("hp", "P", [128, 16, 129]), cv("arena", "P", [128, 16, 129])]
        GVG = [cv("hp", "P", [128, 4, 512]), cv("arena", "P", [128, 4, 512])]
        KHG = [cv("hp", "P", [128, 2, 512]), cv("arena", "P", [128, 2, 512])]
        E2G = cv("hp", "P", [128, 2, 512], F32)
        BCG = cv("hp", "P", [128, 2, 512], F32)
        ZG = cv("hp", "P", [4, 2, 512], F32)
        EIG = ZG[:, 0, :]
        EFG = cv("hp", "P", [4, 512], F32)
        GLRG = cv("hp", "P", [16, 512], F32)
        RLG = sb("RLG", [4, 4])
        D4G = sb("D4G", [4, 4, 4])
        SUFG = sb("SUFG", [128, 2, 4])
        HTGs = [HTG1, HTG2]
        SCG = [sb(f"SCG{i}", [128, 4, 16]) for i in range(2)]
        ACLG = [sb(f"ACLG{i}", [128, 4, 2]) for i in range(2)]
    SCb = [sb(f"SCb{i}", [128, 16]) for i in range(2)]
    ACL = [sb(f"ACL{i}", [128, 2]) for i in range(2)]
    NGT = sb("NGT", [128, 8])
    XS_B = cv("s2", "B", [128, D], F32)
    HB_B = cv("s2", "B", [128, D])
    HTt_B = cv("s2", "B", [128, 8, 128])
    QT = cv("s2", "B", [128, 8, 128])
    PEX = cv("s2", "B", [128, 4, NMEM])
    PTb = [cv("s2", "B", [128, 8, 128]) for _ in range(2)]
    OB = cv("s2", "B", [128, D])
    XBB = [cv("s2", "B", [128, D], F32)]
    HID = [cv("s2", "C", [128, 8, 512]), ]
    cur = {"XS": None, "HB": HB_A}

    PS = [es.enter_context(nc.psum_tensor(f"ps{i}", [128, 512], F32)) for i in range(8)]
    PSb = [p.bitcast(BF16) for p in PS]
    B_T, B_F0, B_F1, B_K0, B_K1, B_G, B_M0, B_M1 = range(8)

    sems = {}

    def getsem(sk):
        if sk not in sems:
            sems[sk] = es.enter_context(nc.semaphore("s_" + "_".join(str(x) for x in sk)))
        return sems[sk]

    def wload(dst, src_rows_ap, ncols_chunks, key, eng="pool"):
        for (d0, s0, n) in ncols_chunks:
            S.dma(eng,
                  lambda e, d0=d0, s0=s0, n=n: e.dma_start(
                      out=dst[:, :, d0:d0 + n],
                      in_=src_rows_ap.rearrange("(kc p) n -> p kc n", p=128)[:, :, s0:s0 + n]),
                  reads=(), writes=(key,))

    def bcast_load(dst, src1d, n, key):
        S.dma("sp", lambda e: e.dma_start(out=dst, in_=src1d[0:n].partition_broadcast(128)),
              reads=(), writes=(key,))

    def layer_norm(src, dst, keys_r, keys_w, xs=None, xskey="XS", lng=None, lnb=None, gkey="LNG", bkey="LNB",
                   sc=None, sfx=""):
        XS = cur["XS"] if xs is None else xs
        G_ = LNG if lng is None else lng
        B_ = LNB if lnb is None else lnb
        st6, mv, rs = (ST6, MV, RS) if sc is None else sc
        k6, kmv, kr0, kr1 = "ST6" + sfx, "MV" + sfx, "RS0" + sfx, "RS1" + sfx
        S.op("dve", lambda e: (e.bn_stats(out=st6[:, 0, :], in_=src[:, 0:512]),
                               e.bn_stats(out=st6[:, 1, :], in_=src[:, 512:1024]))[-1],
             reads=keys_r, writes=(k6,))
        S.op("dve", lambda e: e.bn_aggr(out=mv[:, :], in_=st6[:, :, :].rearrange("p a b -> p (a b)")),
             reads=(k6,), writes=(kmv,))
        S.op("dve", lambda e: e.tensor_scalar(out=rs[:, 0:1], in0=mv[:, 1:2], scalar1=EPS, scalar2=None,
                                              op0=ALU.add), reads=(kmv,), writes=(kr0,))
        S.op("pool", lambda e: e.tensor_tensor(out=rs[:, 1:2], in0=rs[:, 0:1], in1=negh[:, 0:1], op=ALU.pow),
             reads=(kr0,), writes=(kr1,))
        S.op("dve", lambda e: e.scalar_tensor_tensor(out=XS[:, :], in0=src, scalar=mv[:, 0:1], in1=G_[:, :],
                                                     op0=ALU.subtract, op1=ALU.mult),
             reads=tuple(keys_r) + (kmv, gkey), writes=(xskey,))
        S.op("dve", lambda e: e.scalar_tensor_tensor(out=dst, in0=XS[:, :], scalar=rs[:, 1:2], in1=B_[:, :],
                                                     op0=ALU.mult, op1=ALU.add),
             reads=(xskey, kr1, bkey), writes=keys_w)

    def transpose8(src_bf, src_keys, dst_ap, dst_keys, bank=0, bkey="PS_T"):
        def f(e):
            last = None
            for kc in range(8):
                last = e.transpose(out=PSb[bank][:, kc * 128:(kc + 1) * 128],
                                   in_=src_bf[:, kc * 128:(kc + 1) * 128], identity=ident_bf[:, :])
            return last
        S.op("pe", f, reads=tuple(src_keys), writes=(bkey,))
        S.op("act", lambda e: e.activation(out=dst_ap,
                                           in_=PSb[bank][:, :].rearrange("p (a b) -> p a b", a=8),
                                           func=AF.Copy),
             reads=(bkey,), writes=dst_keys)

    def to_feature_major(src_f32, src_keys, dst_ap, dst_keys, bank=0, bkey="PS_T", hb=None, hbkey="HB"):
        HB = cur["HB"] if hb is None else hb
        S.op("act", lambda e: e.activation(out=HB[:, :], in_=src_f32, func=AF.Copy),
             reads=src_keys, writes=(hbkey,))
        transpose8(HB, (hbkey,), dst_ap, dst_keys, bank, bkey)

    def proj_resid(ti, src_ht, src_keys, W, wkey):
        for half in range(2):
            bank = B_K0 if half == 0 else B_K1
            bkey = "K0" if half == 0 else "K1"
            def f(e, half=half, bank=bank):
                last = None
                for kc in range(8):
                    last = e.matmul(PS[bank][:, 0:512], lhsT=src_ht[:, kc, :],
                                    rhs=W[:, kc, half * 512:(half + 1) * 512], start=(kc == 0), stop=(kc == 7))
                return last
            S.op("pe", f, reads=(wkey,) + tuple(src_keys), writes=(bkey,))
            S.op("dve", lambda e, half=half, bank=bank: e.scalar_tensor_tensor(
                out=H[:, ti, half * 512:(half + 1) * 512], in0=H[:, ti, half * 512:(half + 1) * 512], scalar=ALPHA,
                in1=PS[bank][:, 0:512], op0=ALU.mult, op1=ALU.add), reads=(bkey, f"H{ti}"), writes=(f"H{ti}",))

    def barrier():
        allk = list(S.lastw.keys())
        for eng in ENGS:
            S.fence(eng, allk)

    def finish():
        for e in ENGS:
            for (fn, waits, sk, inc) in S.ops[e]:
                for (wk, val) in waits:
                    getsem(wk)
                if sk is not None:
                    getsem(sk)
        block = es.enter_context(nc.Block())

        def emit(engname):
            def body(eng):
                for (fn, waits, sk, inc) in S.ops[engname]:
                    for (wk, val) in waits:
                        eng.wait_ge(sems[wk], val)
                    if fn is not None:
                        ins = fn(eng)
                        ins.then_inc(sems[sk], inc)
            return body

        block.tensor(emit("pe"))
        block.scalar(emit("act"))
        block.vector(emit("dve"))
        block.gpsimd(emit("pool"))
        block.sync(emit("sp"))
        es.close()
        return nc

    def dump_H_and_finish():
        for ti in range(NT_MAIN):
            S.dma("sp", lambda e, ti=ti: e.dma_start(out=out_d[ti * 128:(ti + 1) * 128, :], in_=H[:, ti, :]),
                  reads=(f"H{ti}",), writes=(f"OUT{ti}",))
        S.fence("sp", [f"OUT{t}" for t in range(NT_MAIN)])
        return finish()

    S.op("pool", lambda e: e.memset(ones_f[:, :], 1.0), writes=("c_ones",))
    S.op("pool", lambda e: e.memset(zeros_f[:, :], 0.0), writes=("c_zeros",))
    S.op("pool", lambda e: e.memset(negh[:, :], -0.5), writes=("c_negh",))
    S.op("pool", lambda e: e.affine_select(out=ident_f[:, :], in_=ones_f[:, :], pattern=[[-1, 128]],
                                           compare_op=ALU.is_equal, fill=0.0, base=0, channel_multiplier=1),
         reads=("c_ones",), writes=("c_identf",))
    S.op("pool", lambda e: e.tensor_copy(out=ident_bf[:, :], in_=ident_f[:, :]),
         reads=("c_identf",), writes=("c_identb",))
    S.op("pool", lambda e: e.affine_select(out=TT[:, :], in_=ones_f[:, :], pattern=[[1, 128]],
                                           compare_op=ALU.is_ge, fill=0.0, base=0, channel_multiplier=-1),
         reads=("c_ones",), writes=("TT",))
    S.op("pool", lambda e: e.tensor_copy(out=mask_ut[:, :], in_=TT[:, :]), reads=("TT",), writes=("c_mask",))
    S.op("pool", lambda e: e.memset(VEb[0][:, :, :], 1.0), writes=("VE0",))
    S.op("pool", lambda e: e.memset(VEb[1][:, :, :], 1.0), writes=("VE1",))
    S.op("pool", lambda e: e.memset(Cf[:, :, :], 0.0), writes=("Cf0", "Cf1", "Cf2", "Cf3"))
    S.op("pool", lambda e: e.memset(Cb[:, :, :], 0.0), writes=("Cb0", "Cb1", "Cb2", "Cb3"))
    S.op("pool", lambda e: e.memset(Sf[:, :, :], 0.0), writes=("Sf0", "Sf1"))
    S.op("pool", lambda e: e.memset(Sb_[:, :, :], 0.0), writes=("Sb0", "Sb1"))
    S.op("pool", lambda e: e.memset(RAW[:, :, :], 0.0), writes=("RAWq", "RAWk"))

    bcast_load(LNG[:, :], ln_g["ln_in"], D, "LNG")
    bcast_load(LNB[:, :], ln_b["ln_in"], D, "LNB")
    S.dma("sp", lambda e: e.dma_start(out=NGT[:, 0:4], in_=m_norm_g.rearrange("(c p) -> p c", p=128),
                                      allow_slow_non_contiguous=True), writes=("NGTa",))
    S.dma("sp", lambda e: e.dma_start(out=NGT[:, 4:8], in_=g_norm_g.rearrange("(c p) -> p c", p=128),
                                      allow_slow_non_contiguous=True), writes=("NGTb",))
    S.op("dve", lambda e: e.tensor_scalar(out=NGT[:, :], in0=NGT[:, :], scalar1=0.5, scalar2=None, op0=ALU.mult),
         reads=("NGTa", "NGTb"), writes=("NGT",))
    S.dma("sp", lambda e: e.dma_start(out=PM[:, :], in_=pmask_d[:, :]), writes=("PM",))
    with nc.allow_non_contiguous_dma(reason="tiny param loads"):
        for j in range(4):
            S.dma("sp", lambda e, j=j: e.dma_start(out=CW[:, j, :], in_=conv_w[j, :].rearrange("(c p) -> p c", p=128),
                                                   allow_slow_non_contiguous=True), writes=("CW",))
        S.dma("sp", lambda e: e.dma_start(out=CB[:, :], in_=conv_b.rearrange("(c p) -> p c", p=128), allow_slow_non_contiguous=True),
              writes=("CB",))
        S.dma("sp", lambda e: e.dma_start(out=GB[:, 0:1], in_=m_i_bias.rearrange("(p o) -> p o", o=1), allow_slow_non_contiguous=True),
              writes=("GBa",))
        S.dma("sp", lambda e: e.dma_start(out=GB[:, 1:2], in_=m_f_bias.rearrange("(p o) -> p o", o=1), allow_slow_non_contiguous=True),
              writes=("GBb",))
        S.dma("sp", lambda e: e.dma_start(out=GLB[:, :], in_=g_lr_b.rearrange("(c p) -> p c", p=128), allow_slow_non_contiguous=True),
              writes=("GLBa",))
    S.dma("sp", lambda e: e.dma_start(out=GLW[:, :], in_=g_lr_w[:, :]), writes=("GLW",))
    S.op("dve", lambda e: e.tensor_scalar(out=GB[:, 0:1], in0=GB[:, 0:1],
                                          scalar1=math.log(128.0 ** -0.5 / 4.0), scalar2=None, op0=ALU.add),
         reads=("GBa",), writes=("GB0",))
    S.op("dve", lambda e: e.tensor_scalar(out=GB[:, 1:2], in0=GB[:, 1:2], scalar1=-1.0, scalar2=None,
                                          op0=ALU.mult), reads=("GBb",), writes=("GB1",))
    S.op("dve", lambda e: e.tensor_scalar(out=GLB[:, :], in0=GLB[:, :], scalar1=-1.0, scalar2=None,
                                          op0=ALU.mult), reads=("GLBa",), writes=("GLB",))

    wload(Ws, w_in, [(0, 2048, 8), (8, 3592, 16)], "Ws")
    wload(Wf, w_in, [(512, 512, 512)], "Wfk")
    wload(Wf, w_in, [(1024, 2056, 512)], "Wfg")
    wload(WtA, w_in, [(0, 1024, 512), (512, 2568, 512)], "Wt")
    wload(Wf, w_in, [(0, 0, 512)], "Wfq")

    def load_main_weights():
        wload(WtB, w_in, [(0, 1536, 512), (512, 3080, 512)], "WtB")
        wload(Wo_, w_out, [(0, 0, 1024)], "Wo")
    if NG4 == 0:
        load_main_weights()
    pass

    htt = HTt[0]
    hk = "HTt0"

    def ln_A(src, dst, keys_r, keys_w):
        S.op("dve", lambda e: (e.bn_stats(out=ST6[:, 0, :], in_=src[:, 0:512]),
                               e.bn_stats(out=ST6[:, 1, :], in_=src[:, 512:1024]))[-1],
             reads=keys_r, writes=("ST6",))
        S.op("dve", lambda e: e.bn_aggr(out=MV[:, :], in_=ST6[:, :, :].rearrange("p a b -> p (a b)")),
             reads=("ST6",), writes=("MV",))
        S.op("dve", lambda e: e.tensor_scalar(out=RS[:, 0:1], in0=MV[:, 1:2], scalar1=EPS, scalar2=None,
                                              op0=ALU.add), reads=("MV",), writes=("RS0",))
        S.op("pool", lambda e: e.tensor_tensor(out=RS[:, 1:2], in0=RS[:, 0:1], in1=negh[:, 0:1], op=ALU.pow),
             reads=("RS0",), writes=("RS1",))
        S.op("dve", lambda e: e.scalar_tensor_tensor(out=XS_A[:, :], in0=src, scalar=MV[:, 0:1], in1=LNG[:, :],
                                                     op0=ALU.subtract, op1=ALU.mult),
             reads=tuple(keys_r) + ("MV", "LNG"), writes=("XS",))
        S.op("dve", lambda e: e.scalar_tensor_tensor(out=dst, in0=XS_A[:, :], scalar=RS[:, 1:2], in1=LNB[:, :],
                                                     op0=ALU.mult, op1=ALU.add),
             reads=("XS", "RS1", "LNB"), writes=keys_w)

    def stageX(i):
        main = i >= NT_PRE
        ti = i - NT_PRE
        need_q = main or (i == NT_PRE - 1)
        par = i % 2
        xb = XB[0]
        xk = "X0"
        pm = PM[:, i:i + 1]
        QK, VE, GV, KH, QH, SC, OG = QKb[par], VEb[par], GVb[par], KHb[par], QHb[par], SCb[par], OGb[par]
        S.dma("sp", lambda e: e.dma_start(out=xb[:, :], in_=xall[i * 128:(i + 1) * 128, :]), writes=(xk,))
        if main:
            dst = H[:, ti, :]
            dk_ = (f"H{ti}",)
        else:
            dst = xb[:, :]
            dk_ = (xk,)
        ln_A(xb[:, :], dst, (xk,), dk_)
        to_feature_major(dst, dk_, htt[:, :, :], (hk,))

        def proj_feat(bank, col0, nchunk, key):
            def f(e):
                last = None
                for c in range(nchunk):
                    for kc in range(8):
                        last = e.matmul(PS[bank][:, c * 128:(c + 1) * 128],
                                        lhsT=Wf[:, kc, col0 + c * 128: col0 + (c + 1) * 128],
                                        rhs=htt[:, kc, :], start=(kc == 0), stop=(kc == 7))
                return last
            S.op("pe", f, reads=(hk, {0: "Wfq", 512: "Wfk", 1024: "Wfg"}[col0]), writes=(key,))

        def proj_tok(bank, g, key):
            W_, c_, wk_ = {0: (WtA, 0, "Wt"), 1: (WtB, 0, "WtB"), 2: (WtA, 512, "Wt"), 3: (WtB, 512, "WtB")}[g]

            def f(e):
                last = None
                for kc in range(8):
                    last = e.matmul(PS[bank][:, 0:512], lhsT=htt[:, kc, :],
                                    rhs=W_[:, kc, c_:c_ + 512], start=(kc == 0), stop=(kc == 7))
                return last
            S.op("pe", f, reads=(hk, wk_), writes=(key,))

        def f_gates(e):
            last = None
            for (o0, n, c0) in ((0, 4, 0), (128, 4, 4)):
                for kc in range(8):
                    last = e.matmul(PS[B_G][0:n, o0:o0 + 128], lhsT=Ws[:, kc, c0:c0 + n], rhs=htt[:, kc, :],
                                    start=(kc == 0), stop=(kc == 7))
            for kc in range(8):
                last = e.matmul(PS[B_G][0:16, 256:384], lhsT=Ws[:, kc, 8:24], rhs=htt[:, kc, :],
                                start=(kc == 0), stop=(kc == 7))
            return last
        S.op("pe", f_gates, reads=(hk, "Ws"), writes=("G_a", "G_b"))
        if need_q:
            proj_feat(B_F0, 0, 4, "F0")
        proj_feat(B_F1, 512, 4, "F1")
        proj_tok(B_K0, 0, "K0")

        S.op("act", lambda e: e.activation(out=EI[:, :], in_=PS[B_G][0:4, 0:128], func=AF.Exp,
                                           bias=GB[:, 0:1], scale=1.0),
             reads=("G_a", "GB0"), writes=("EI",))
        S.op("act", lambda e: e.activation(out=EF[:, :], in_=PS[B_G][0:4, 128:256], func=AF.Exp,
                                           bias=GB[:, 1:2], scale=-1.0),
             reads=("G_a", "GB1"), writes=("EF",))
        S.op("act", lambda e: e.activation(out=GLR[:, :], in_=PS[B_G][0:16, 256:384], func=AF.Copy),
             reads=("G_b",), writes=("GLR",))
        S.op("dve", lambda e: e.tensor_scalar(out=EF[:, :], in0=EF[:, :], scalar1=1.0, scalar2=None, op0=ALU.add),
             reads=("EF",), writes=("EF",))
        S.op("dve", lambda e: e.tensor_tensor_scan(out=Z[:, 128:256], data0=EF[:, :], data1=zeros_f[0:4, :],
                                                   initial=1.0, op0=ALU.mult, op1=ALU.add),
             reads=("EF",), writes=("Zb",))
        S.op("dve", lambda e: e.tensor_tensor(out=Z[:, 0:128], in0=EI[:, :], in1=Z[:, 128:256], op=ALU.mult),
             reads=("EI", "Zb"), writes=("Za",))
        S.op("dve", lambda e: e.reciprocal(out=RL[:, :], in_=Z[:, 255:256]), reads=("Zb",), writes=("RL",))
        S.op("dve", lambda e: e.tensor_scalar(out=D4[:, :], in0=ident_f[0:4, 0:4], scalar1=RL[:, 0:1],
                                              scalar2=None, op0=ALU.mult), reads=("RL",), writes=("D4",))

        def f_sc(e):
            e.matmul(PS[B_G][:, 384:388], lhsT=Z[0:4, 0:128], rhs=ident_f[0:4, 0:4], start=True, stop=True)
            e.matmul(PS[B_G][:, 388:392], lhsT=Z[0:4, 128:256], rhs=ident_f[0:4, 0:4], start=True, stop=True)
            return e.matmul(PS[B_G][:, 392:396], lhsT=ones_f[0:4, :], rhs=D4[0:4, 0:4], start=True, stop=True)
        S.op("pe", f_sc, reads=("Za", "Zb", "D4"), writes=("G_c",))
        S.op("dve", lambda e: e.tensor_copy(out=SC[:, 0:12], in_=PS[B_G][:, 384:396]), reads=("G_c",),
             writes=(f"SC{par}",))
        S.op("dve", lambda e: e.tensor_scalar(out=SC[:, 0:4], in0=SC[:, 0:4], scalar1=pm, scalar2=None,
                                              op0=ALU.mult), reads=(f"SC{par}", "PM"), writes=(f"SC{par}",))
        S.op("dve", lambda e: e.tensor_tensor(out=SC[:, 12:16], in0=SC[:, 0:4], in1=SC[:, 8:12], op=ALU.mult),
             reads=(f"SC{par}",), writes=(f"SC{par}",))

        def f_z(e):
            e.matmul(PS[B_G][:, 0:128], lhsT=GLW[0:16, 0:128], rhs=GLR[0:16, :], start=True, stop=True)
            return e.matmul(PS[B_G][:, 128:256], lhsT=GLW[0:16, 128:256], rhs=GLR[0:16, :], start=True, stop=True)
        S.op("pe", f_z, reads=("GLR", "GLW"), writes=("G_a",))
        S.op("act", lambda e: (e.activation(out=E2[:, 0, :], in_=PS[B_G][:, 0:128], func=AF.Exp,
                                            bias=GLB[:, 0:1], scale=-1.0),
                               e.activation(out=E2[:, 1, :], in_=PS[B_G][:, 128:256], func=AF.Exp,
                                            bias=GLB[:, 1:2], scale=-1.0))[-1],
             reads=("G_a", "GLB"), writes=("E2",))
        S.op("act", lambda e: e.activation(out=E2[:, :, :], in_=E2[:, :, :], func=AF.Ln, bias=1.0, scale=1.0),
             reads=("E2",), writes=("E2",))
        S.op("dve", lambda e: e.tensor_tensor_scan(out=AI[:, 0, :], data0=ones_f[:, :], data1=E2[:, 0, :],
                                                   initial=0.0, op0=ALU.mult, op1=ALU.add),
             reads=("E2",), writes=("BC0",))
        S.op("dve", lambda e: e.tensor_tensor_scan(out=AI[:, 1, :], data0=ones_f[:, :], data1=E2[:, 1, :],
                                                   initial=0.0, op0=ALU.mult, op1=ALU.add),
             reads=("E2",), writes=("BC1",))
        S.op("act", lambda e: e.activation(out=AC[:, :, :], in_=AI[:, :, :], func=AF.Exp, scale=-1.0 / 16.0),
             reads=("BC0", "BC1"), writes=("AC",))
        S.op("act", lambda e: e.activation(out=AI[:, :, :], in_=AI[:, :, :], func=AF.Exp, scale=1.0 / 16.0),
             reads=("BC0", "BC1", "AC"), writes=("AI", "BC0", "BC1"))
        S.op("act", lambda e: e.activation(out=ACL[par][:, :], in_=AC[:, :, 127], func=AF.Copy),
             reads=("AC",), writes=(f"ACL{par}",))

        groups = []
        if need_q:
            groups.append((0, B_F0, "F0", "RAWq"))
        groups.append((4, B_F1, "F1", "RAWk"))
        for (c0, bank, bkey, rk) in groups:
            S.op("act", lambda e, c0=c0, bank=bank: e.activation(
                out=RAW[:, c0:c0 + 4, 3:131], in_=PS[bank][:, :].rearrange("p (a b) -> p a b", a=4),
                func=AF.Identity, scale=pm), reads=(bkey, "PM"), writes=(rk,))
            if c0 == 0:
                proj_feat(B_F0, 1024, 4, "F0")
            else:
                if main:
                    proj_tok(B_F1, 1, "F1")
            if main or c0 == 4:
                for cc in range(4):
                    c = c0 + cc
                    S.op("dve", lambda e, c=c, cc=cc: e.tensor_scalar(out=CV[:, cc, :], in0=RAW[:, c, 3:131],
                                                                      scalar1=CW[:, 3, c:c + 1], scalar2=CB[:, c:c + 1],
                                                                      op0=ALU.mult, op1=ALU.add),
                         reads=(rk, "CW", "CB"), writes=(f"CV{cc}",))
                for j in range(3):
                    for cc in range(4):
                        c = c0 + cc
                        S.op("dve", lambda e, c=c, cc=cc, j=j: e.scalar_tensor_tensor(
                            out=CV[:, cc, :], in0=RAW[:, c, j:j + 128], scalar=CW[:, j, c:c + 1], in1=CV[:, cc, :],
                            op0=ALU.mult, op1=ALU.add), reads=(rk, f"CV{cc}"), writes=(f"CV{cc}",))
                cvk = tuple(f"CV{cc}" for cc in range(4))
                S.op("act", lambda e: e.activation(out=TH[:, :, :], in_=CV[:, :, :], func=AF.Tanh, scale=0.5),
                     reads=cvk, writes=("OGT",))
                S.op("dve", lambda e, c0=c0: e.scalar_tensor_tensor(
                    out=QK[:, c0:c0 + 4, :], in0=TH[:, :, :], scalar=1.0, in1=CV[:, :, :],
                    op0=ALU.add, op1=ALU.mult), reads=("OGT",) + cvk,
                    writes=(f"QKq{par}" if c0 == 0 else f"QKk{par}",))
            S.op("pool", lambda e, c0=c0: e.tensor_copy(out=RAW[:, c0:c0 + 4, 0:3], in_=RAW[:, c0:c0 + 4, 128:131]),
                 reads=(rk,), writes=(rk,))
        if not need_q:
            proj_feat(B_F0, 1024, 4, "F0")

        if main:
            S.op("dve", lambda e: e.scalar_tensor_tensor(
                out=QH[:, :, :], in0=PS[B_F0][:, 0:256].rearrange("p (a b) -> p a b", a=2), scalar=0.125,
                in1=AC[:, :, :], op0=ALU.mult, op1=ALU.mult), reads=("F0", "AC"), writes=(f"QH{par}",))
        S.op("dve", lambda e: e.tensor_tensor(out=KH[:, :, :],
                                              in0=PS[B_F0][:, 256:512].rearrange("p (a b) -> p a b", a=2),
                                              in1=AI[:, :, :], op=ALU.mult), reads=("F0", "AI"), writes=(f"KH{par}",))

        S.op("act", lambda e: e.activation(out=VE[:, :, 0:128],
                                           in_=PS[B_K0][:, :].rearrange("p (a b) -> p a b", a=4), func=AF.Copy),
             reads=("K0",), writes=(f"VE{par}",))
        proj_tok(B_K0, 2, "K0")
        if main:
            S.op("act", lambda e: e.activation(out=OGT[:, :], in_=PS[B_F1][:, :], func=AF.Tanh, scale=0.5),
                 reads=("F1",), writes=("OGT",))
            S.op("dve", lambda e: e.tensor_scalar(out=OG[:, 0:512], in0=OGT[:, :], scalar1=1.0, scalar2=None,
                                                  op0=ALU.add), reads=("OGT",), writes=(f"OGa{par}",))
            proj_tok(B_F1, 3, "F1")
        S.op("act", lambda e: e.activation(out=GV[:, :], in_=PS[B_K0][:, :], func=AF.Copy),
             reads=("K0",), writes=(f"GV{par}",))
        if main:
            S.op("act", lambda e: e.activation(out=OGT[:, :], in_=PS[B_F1][:, :], func=AF.Tanh, scale=0.5),
                 reads=("F1",), writes=("OGT",))
            S.op("dve", lambda e: e.scalar_tensor_tensor(out=OG[:, 512:1024], in0=OGT[:, :], scalar=1.0,
                                                         in1=PS[B_F1][:, :], op0=ALU.add, op1=ALU.mult),
                 reads=("OGT", "F1"), writes=(f"OGb{par}",))

    def stageY(i, grp=None):
        main = i >= NT_PRE
        ti = i - NT_PRE
        par = i % 2
        pm = PM[:, i:i + 1]
        if grp is None:
            QK, VE, GV, KH, QH, SC, OG = QKb[par], VEb[par], GVb[par], KHb[par], QHb[par], SCb[par], OGb[par]
            kq, kk, kve, kgv, kkh, kqh, ksc = (f"QKq{par}", f"QKk{par}", f"VE{par}", f"GV{par}", f"KH{par}",
                                              f"QH{par}", f"SC{par}")
            kT = lambda h: QK[:, 4 + h, :]
            qT = lambda h: QK[:, h, :]
            VEh = lambda h: VE[:, h, :]
            GVh = lambda h: GV[:, h * 128:(h + 1) * 128]
            KHp = lambda p, part=slice(0, 128): KH[part, p, :]
            QHp = lambda p, part=slice(0, 128): QH[part, p, :]
            SCc = lambda j: SC[:, j:j + 1]
            ACLp = lambda p: ACL[par][:, p:p + 1]
            kacl = f"ACL{par}"
        else:
            t = grp
            gp = (i // 4) % 2
            tk = slice(t * 128, (t + 1) * 128)
            kq = kk = f"QKG{gp}"
            kve, kgv, kkh, kqh, ksc, kacl = f"VEG{gp}", f"GVG{gp}", f"KHG{gp}", f"KHG{gp}", f"SCG{gp}", f"ACLG{gp}"
            kT = lambda h: QKG[gp][:, h, tk]
            qT = None
            VEh = lambda h: VEG[gp][:, t * 4 + h, :]
            GVh = lambda h: GVG[gp][:, t, h * 128:(h + 1) * 128]
            KHp = lambda p, part=slice(0, 128): KHG[gp][part, p, tk]
            QHp = None
            SCc = lambda j: SCG[gp][:, t, j:j + 1]
            ACLp = lambda p: ACLG[gp][:, t, p:p + 1]
            OG = None
        if grp is not None:
            SCrow = lambda a, b: SCG[gp][:, t, a:b]

            def f_T(e):
                last = None
                for h in range(4):
                    e.transpose(out=PSb[B_M0][:, h * 128:(h + 1) * 128], in_=kT(h), identity=ident_bf[:, :])
                for p in range(2):
                    last = e.transpose(out=PSb[B_M0][:, 512 + p * 128:512 + (p + 1) * 128], in_=KHp(p),
                                       identity=ident_bf[:, :])
                return last
            S.op("pe", f_T, reads=(kk, kkh), writes=("M0T",))
            S.op("dve", lambda e: e.tensor_tensor(
                out=KW[:, :, :], in0=PSb[B_M0][:, 0:512].rearrange("p (a b) -> p a b", a=4),
                in1=SCrow(12, 16).rearrange("p (h o) -> p h o", o=1).broadcast_to([128, 4, 128]), op=ALU.mult),
                reads=("M0T", ksc), writes=("KW0", "KW1", "KW2", "KW3"))
            S.op("dve", lambda e: e.tensor_scalar(
                out=KHt[:, :, :], in0=PSb[B_M0][:, 512:768].rearrange("p (a b) -> p a b", a=2),
                scalar1=pm, scalar2=None, op0=ALU.mult), reads=("M0T", "PM"), writes=("KHt0", "KHt1"))

            def f_U(e):
                first = (t == 0)
                lastt = (t == 3)
                for h in range(3):
                    e.matmul(PS[B_M1][:, h * 129:(h + 1) * 129], lhsT=KW[:, h, :], rhs=VEh(h),
                             start=(first and h == 0), stop=lastt, skip_group_check=True)
                e.matmul(PS[B_K1][:, 0:129], lhsT=KW[:, 3, :], rhs=VEh(3), start=first, stop=lastt,
                         skip_group_check=True)
                last = None
                for p in range(2):
                    for hh in range(2):
                        part = slice(hh * 64, (hh + 1) * 64)
                        last = e.matmul(PS[B_K1][part, 129 + p * 128:129 + (p + 1) * 128],
                                        lhsT=KHt[:, p, hh * 64:(hh + 1) * 64], rhs=GVh(2 * p + hh),
                                        start=False, stop=lastt, skip_group_check=True)
                return last
            S.op("pe", f_U, reads=("KW0", "KW1", "KW2", "KW3", "KHt0", "KHt1", kve, kgv),
                 writes=("M1U", "K1"))
            if t == 3:
                for h in range(4):
                    U_ = PS[B_K1][:, 0:129] if h == 3 else PS[B_M1][:, h * 129:(h + 1) * 129]
                    S.op("dve", lambda e, h=h, U_=U_: e.scalar_tensor_tensor(
                        out=Cf[:, h, :], in0=Cf[:, h, :], scalar=SCG[gp][:, 0, 8 + h:9 + h], in1=U_,
                        op0=ALU.mult, op1=ALU.add),
                        reads=("K1" if h == 3 else "M1U", ksc, f"Cf{h}"), writes=(f"Cf{h}",))
                S.op("dve", lambda e: e.tensor_tensor(
                    out=Sf[:, :, :], in0=Sf[:, :, :],
                    in1=ACLG[gp][:, 0, :].rearrange("p (h o) -> p h o", o=1).broadcast_to([128, 2, 128]),
                    op=ALU.mult), reads=(kacl, "Sf0", "Sf1"), writes=("Sf0", "Sf1"))
                S.op("dve", lambda e: e.tensor_tensor(
                    out=Sf[:, :, :], in0=Sf[:, :, :],
                    in1=PS[B_K1][:, 129:385].rearrange("p (a b) -> p a b", a=2), op=ALU.add),
                    reads=("K1", "Sf0", "Sf1"), writes=("Sf0", "Sf1"))
                if i == 4 * NG4 - 1:
                    S.op("act", lambda e: e.activation(out=Cb[:, :, :], in_=Cf[:, :, :], func=AF.Copy),
                         reads=("Cf0", "Cf1", "Cf2", "Cf3"), writes=("Cb0", "Cb1", "Cb2", "Cb3"))
                    S.op("act", lambda e: e.activation(out=Sb_[:, :, :], in_=Sf[:, :, :], func=AF.Copy),
                         reads=("Sf0", "Sf1"), writes=("Sb0", "Sb1"))
            return

        if not main:
            if grp is None:
                SCrow = lambda a, b: SC[:, a:b]
                ACLrow = ACL[par][:, 0:2]
            else:
                SCrow = lambda a, b: SCG[gp][:, t, a:b]
                ACLrow = ACLG[gp][:, t, :]

            def f_T(e):
                last = None
                for h in range(4):
                    e.transpose(out=PSb[B_M0][:, h * 128:(h + 1) * 128], in_=kT(h), identity=ident_bf[:, :])
                for p in range(2):
                    last = e.transpose(out=PSb[B_K1][:, p * 128:(p + 1) * 128], in_=KHp(p), identity=ident_bf[:, :])
                return last
            S.op("pe", f_T, reads=(kk, kkh), writes=("M0T", "K1"))
            S.op("dve", lambda e: e.tensor_tensor(
                out=KW[:, :, :], in0=PSb[B_M0][:, 0:512].rearrange("p (a b) -> p a b", a=4),
                in1=SCrow(12, 16).rearrange("p (h o) -> p h o", o=1).broadcast_to([128, 4, 128]), op=ALU.mult),
                reads=("M0T", ksc), writes=("KW0", "KW1", "KW2", "KW3"))
            S.op("dve", lambda e: e.tensor_scalar(
                out=KHt[:, :, :], in0=PSb[B_K1][:, 0:256].rearrange("p (a b) -> p a b", a=2),
                scalar1=pm, scalar2=None, op0=ALU.mult), reads=("K1", "PM"), writes=("KHt0", "KHt1"))

            def f_U(e):
                e.matmul(PS[B_M0][:, 256:385], lhsT=KW[:, 0, :], rhs=VEh(0), start=True, stop=True)
                for h in range(1, 4):
                    e.matmul(PS[B_M1][:, (h - 1) * 129:h * 129], lhsT=KW[:, h, :], rhs=VEh(h), start=True, stop=True)
                last = None
                for p in range(2):
                    for hh in range(2):
                        part = slice(hh * 64, (hh + 1) * 64)
                        last = e.matmul(PS[B_K1][part, 128 + p * 128:128 + (p + 1) * 128],
                                        lhsT=KHt[:, p, hh * 64:(hh + 1) * 64], rhs=GVh(2 * p + hh),
                                        start=True, stop=True)
                return last
            S.op("pe", f_U, reads=("KW0", "KW1", "KW2", "KW3", "KHt0", "KHt1", kve, kgv),
                 writes=("M0U", "M1U", "K1"))
            for h in range(4):
                U_ = PS[B_M0][:, 256:385] if h == 0 else PS[B_M1][:, (h - 1) * 129:h * 129]
                S.op("dve", lambda e, h=h, U_=U_: e.scalar_tensor_tensor(
                    out=Cf[:, h, :], in0=Cf[:, h, :], scalar=SCc(8 + h), in1=U_, op0=ALU.mult, op1=ALU.add),
                    reads=("M0U" if h == 0 else "M1U", ksc, f"Cf{h}"), writes=(f"Cf{h}",))
            S.op("act", lambda e: e.activation(out=Cb[:, :, :], in_=Cf[:, :, :], func=AF.Copy),
                 reads=("Cf0", "Cf1", "Cf2", "Cf3"), writes=("Cb0", "Cb1", "Cb2", "Cb3"))
            S.op("dve", lambda e: e.tensor_tensor(
                out=Sf[:, :, :], in0=Sf[:, :, :], in1=PS[B_K1][:, 128:384].rearrange("p (a b) -> p a b", a=2),
                op=ALU.add), reads=("K1", "Sf0", "Sf1"), writes=("Sf0", "Sf1"))
            S.op("dve", lambda e: e.tensor_tensor(
                out=Sf[:, :, :], in0=Sf[:, :, :],
                in1=ACLrow.rearrange("p (h o) -> p h o", o=1).broadcast_to([128, 2, 128]), op=ALU.mult),
                reads=(kacl, "Sf0", "Sf1"), writes=("Sf0", "Sf1"))
            S.op("act", lambda e: e.activation(out=Sb_[:, :, :], in_=Sf[:, :, :], func=AF.Copy),
                 reads=("Sf0", "Sf1"), writes=("Sb0", "Sb1"))
            return

        assert main
        bc = lambda ap, n, w: ap.rearrange("p (h o) -> p h o", o=1).broadcast_to([128, n, w])
        mask2 = mask_ut[:, :].rearrange("p (o l) -> p o l", o=1).broadcast_to([128, 2, 128])
        for pr in range(2):
            ha, hb = 2 * pr, 2 * pr + 1

            def f1(e, ha=ha, hb=hb):
                e.matmul(PS[B_M0][:, 0:128], lhsT=kT(ha), rhs=qT(ha), start=True, stop=True)
                e.matmul(PS[B_M0][:, 128:256], lhsT=kT(hb), rhs=qT(hb), start=True, stop=True)
                e.transpose(out=PSb[B_M0][:, 512:640], in_=kT(ha), identity=ident_bf[:, :])
                return e.transpose(out=PSb[B_M0][:, 640:768], in_=kT(hb), identity=ident_bf[:, :])
            S.op("pe", f1, reads=(kq, kk), writes=("M0",))
            for j, h in enumerate((ha, hb)):
                S.op("dve", lambda e, j=j, h=h: e.scalar_tensor_tensor(
                    out=SW[j][:, :], in0=PS[B_M0][:, j * 128:(j + 1) * 128], scalar=SCc(h), in1=mask_ut[:, :],
                    op0=ALU.mult, op1=ALU.mult), reads=("M0", ksc), writes=(f"SW{j}",))
            S.op("dve", lambda e, ha=ha: e.tensor_tensor(
                out=KW[:, ha:ha + 2, :], in0=PSb[B_M0][:, 512:768].rearrange("p (a b) -> p a b", a=2),
                in1=bc(SC[:, 12 + ha:14 + ha], 2, 128), op=ALU.mult), reads=("M0", ksc), writes=(f"KW{ha}", f"KW{hb}"))

            def f2(e, ha=ha, hb=hb):
                for j, h in enumerate((ha, hb)):
                    e.matmul(PS[B_M1][:, j * 129:(j + 1) * 129], lhsT=SW[j][:, :], rhs=VEh(h), start=True, stop=False)
                    e.matmul(PS[B_M1][:, j * 129:(j + 1) * 129], lhsT=qT(h), rhs=Cb[:, h, :], start=False, stop=True)
                last = None
                for j, h in enumerate((ha, hb)):
                    last = e.matmul(PS[B_K1][:, j * 129:(j + 1) * 129], lhsT=KW[:, h, :], rhs=VEh(h),
                                    start=True, stop=True)
                return last
            S.op("pe", f2, reads=("SW0", "SW1", kve, kq, f"Cb{ha}", f"Cb{hb}", f"KW{ha}", f"KW{hb}"),
                 writes=("M1", "K1"))
            for j, h in enumerate((ha, hb)):
                S.op("dve", lambda e, j=j, h=h: e.scalar_tensor_tensor(
                    out=Cf[:, h, :], in0=Cf[:, h, :], scalar=SCc(8 + h), in1=PS[B_K1][:, j * 129:(j + 1) * 129],
                    op0=ALU.mult, op1=ALU.add), reads=("K1", ksc, f"Cf{h}"), writes=(f"Cf{h}",))
            S.op("act", lambda e, ha=ha: e.activation(out=Cb[:, ha:ha + 2, :], in_=Cf[:, ha:ha + 2, :], func=AF.Copy),
                 reads=(f"Cf{ha}", f"Cf{hb}"), writes=(f"Cb{ha}", f"Cb{hb}"))
            Pv = PS[B_M1][:, 0:258].rearrange("p (h c) -> p h c", c=129)
            den = Pv[:, :, 128]
            S.op("dve", lambda e, ha=ha: e.tensor_tensor(out=SM[:, 0:2], in0=den, in1=SC[:, 4 + ha:6 + ha], op=ALU.max),
                 reads=("M1", ksc), writes=("SMa",))
            S.op("dve", lambda e: e.scalar_tensor_tensor(out=SM[:, 2:4], in0=den, scalar=-1.0, in1=SM[:, 0:2],
                                                         op0=ALU.mult, op1=ALU.max), reads=("M1", "SMa"), writes=("SMb",))
            S.op("dve", lambda e: e.reciprocal(out=SM[:, 4:6], in_=SM[:, 2:4]), reads=("SMb",), writes=("SMc",))
            for j in range(2):
                S.op("dve", lambda e, j=j: e.bn_stats(out=ST6b2[:, j, :], in_=Pv[:, j, 0:128]), reads=("M1",),
                     writes=(f"ST6b{j}",))
                S.op("dve", lambda e, j=j: e.bn_aggr(out=MVb2[:, j, :], in_=ST6b2[:, j, :]), reads=(f"ST6b{j}",),
                     writes=(f"MVb{j}",))
            S.op("dve", lambda e: e.tensor_tensor(out=SM[:, 6:8], in0=SM[:, 4:6], in1=SM[:, 4:6], op=ALU.mult),
                 reads=("SMc",), writes=("SMd",))
            S.op("dve", lambda e: e.tensor_tensor(out=SM[:, 8:10], in0=MVb2[:, :, 1], in1=SM[:, 6:8], op=ALU.mult),
                 reads=("MVb0", "MVb1", "SMd"), writes=("SMe",))
            S.op("dve", lambda e: e.tensor_scalar(out=SM[:, 8:10], in0=SM[:, 8:10], scalar1=EPS, scalar2=None,
                                                  op0=ALU.add), reads=("SMe",), writes=("SMe",))
            S.op("pool", lambda e: e.tensor_tensor(out=SM[:, 10:12], in0=SM[:, 8:10], in1=negh[:, 0:2], op=ALU.pow),
                 reads=("SMe",), writes=("SMf",))
            S.op("dve", lambda e: e.tensor_tensor(out=SM[:, 12:14], in0=SM[:, 10:12], in1=SM[:, 4:6], op=ALU.mult),
                 reads=("SMf", "SMc"), writes=("SMg",))
            for j in range(2):
                S.op("dve", lambda e, j=j: e.tensor_scalar(out=TT2[:, j, :], in0=Pv[:, j, 0:128], scalar1=MVb2[:, j, 0:1],
                                                           scalar2=SM[:, 12 + j:13 + j], op0=ALU.subtract, op1=ALU.mult),
                     reads=("M1", f"MVb{j}", "SMg"), writes=(f"TT2{j}",))
            S.op("dve", lambda e, ha=ha: e.tensor_tensor(
                out=Y[:, ha * 128:(ha + 2) * 128], in0=TT2[:, :, :].rearrange("p a b -> p (a b)"),
                in1=OG[:, ha * 128:(ha + 2) * 128], op=ALU.mult),
                reads=("TT20", "TT21", f"OGa{par}"), writes=(f"Y{ha}", f"Y{hb}"))

        for p in range(2):
            h0, h1 = 2 * p, 2 * p + 1

            def g1(e, p=p):
                e.matmul(PS[B_M0][:, 0:128], lhsT=KHp(p, slice(0, 64)), rhs=QHp(p, slice(0, 64)), start=True, stop=True)
                e.matmul(PS[B_M1][:, 128:256], lhsT=KHp(p, slice(64, 128)), rhs=QHp(p, slice(64, 128)),
                         start=True, stop=True)
                return e.transpose(out=PSb[B_M0][:, 512:640], in_=KHp(p), identity=ident_bf[:, :])
            S.op("pe", g1, reads=(kkh, kqh), writes=("M0", "M1"))
            S.op("dve", lambda e, h0=h0: e.tensor_tensor(out=ATb[:, h0, :], in0=PS[B_M0][:, 0:128], in1=mask_ut[:, :],
                                                         op=ALU.mult), reads=("M0",), writes=(f"ATb{h0}",))
            S.op("dve", lambda e, h1=h1: e.tensor_tensor(out=ATb[:, h1, :], in0=PS[B_M1][:, 128:256], in1=mask_ut[:, :],
                                                         op=ALU.mult), reads=("M1",), writes=(f"ATb{h1}",))
            S.op("dve", lambda e, p=p: e.tensor_scalar(out=KHt[:, p, :], in0=PSb[B_M0][:, 512:640], scalar1=pm,
                                                       scalar2=None, op0=ALU.mult), reads=("M0", "PM"), writes=(f"KHt{p}",))

            def g2(e, p=p, h0=h0):
                e.matmul(PS[B_M1][:, 0:128], lhsT=ATb[:, h0, :], rhs=GVh(h0), start=True, stop=False)
                e.matmul(PS[B_M1][:, 0:128], lhsT=QHp(p, slice(0, 64)), rhs=Sb_[0:64, p, :], start=False, stop=True)
                e.matmul(PS[B_K1][:, 0:128], lhsT=ATb[:, h0 + 1, :], rhs=GVh(h0 + 1), start=True, stop=False)
                e.matmul(PS[B_K1][:, 0:128], lhsT=QHp(p, slice(64, 128)), rhs=Sb_[64:128, p, :], start=False, stop=True)
                last = None
                for hh in range(2):
                    part = slice(hh * 64, (hh + 1) * 64)
                    last = e.matmul(PS[B_K1][part, 128:256], lhsT=KHt[:, p, hh * 64:(hh + 1) * 64], rhs=GVh(h0 + hh),
                                    start=True, stop=True)
                return last
            S.op("pe", g2, reads=(f"ATb{h0}", f"ATb{h1}", kgv, kqh, f"Sb{p}", f"KHt{p}"), writes=("M1", "K1"))
            Ovs = [PS[B_M1][:, 0:128], PS[B_K1][:, 0:128]]
            Okeys = ["M1", "K1"]
            for j in range(2):
                S.op("dve", lambda e, j=j: e.bn_stats(out=ST6b2[:, j, :], in_=Ovs[j]), reads=(Okeys[j],),
                     writes=(f"ST6b{j}",))
                S.op("dve", lambda e, j=j: e.bn_aggr(out=MVb2[:, j, :], in_=ST6b2[:, j, :]), reads=(f"ST6b{j}",),
                     writes=(f"MVb{j}",))
            S.op("dve", lambda e: e.tensor_tensor(out=SM[:, 0:2], in0=MVb2[:, :, 0], in1=MVb2[:, :, 0], op=ALU.mult),
                 reads=("MVb0", "MVb1"), writes=("SMa",))
            S.op("dve", lambda e: e.scalar_tensor_tensor(out=SM[:, 2:4], in0=SM[:, 0:2], scalar=EPS, in1=MVb2[:, :, 1],
                                                         op0=ALU.add, op1=ALU.add),
                 reads=("SMa", "MVb0", "MVb1"), writes=("SMb",))
            S.op("pool", lambda e: e.tensor_tensor(out=SM[:, 4:6], in0=SM[:, 2:4], in1=negh[:, 0:2], op=ALU.pow),
                 reads=("SMb",), writes=("SMc",))
            for j in range(2):
                h = h0 + j
                S.op("dve", lambda e, j=j, h=h: e.scalar_tensor_tensor(
                    out=Y[:, 512 + h * 128:512 + (h + 1) * 128], in0=Ovs[j], scalar=SM[:, 4 + j:5 + j],
                    in1=OG[:, 512 + h * 128:512 + (h + 1) * 128], op0=ALU.mult, op1=ALU.mult),
                    reads=(Okeys[j], "SMc", f"OGb{par}"), writes=(f"Y{4 + h}",))
            S.op("dve", lambda e, p=p: e.tensor_tensor(out=Sf[:, p, :], in0=Sf[:, p, :], in1=PS[B_K1][:, 128:256],
                                                       op=ALU.add), reads=("K1", f"Sf{p}"), writes=(f"Sf{p}",))
            S.op("dve", lambda e, p=p: e.tensor_scalar(out=Sf[:, p, :], in0=Sf[:, p, :], scalar1=ACLp(p), scalar2=None,
                                                       op0=ALU.mult), reads=(kacl, f"Sf{p}"), writes=(f"Sf{p}",))
            S.op("act", lambda e, p=p: e.activation(out=Sb_[:, p, :], in_=Sf[:, p, :], func=AF.Copy),
                 reads=(f"Sf{p}",), writes=(f"Sb{p}",))

        if main:
            yk = tuple(f"Y{j}" for j in range(8))

            def f_t(e):
                last = None
                for kc in range(8):
                    last = e.transpose(out=PSb[B_K1][:, kc * 128:(kc + 1) * 128],
                                       in_=Y[:, kc * 128:(kc + 1) * 128], identity=ident_bf[:, :])
                return last
            S.op("pe", f_t, reads=yk, writes=("K1",))

            def f_e(e):
                last = None
                for kc in range(8):
                    last = e.activation(out=YTt[:, kc, :], in_=PSb[B_K1][:, kc * 128:(kc + 1) * 128],
                                        func=AF.Identity, scale=NGT[:, kc:kc + 1])
                return last
            S.op("act", f_e, reads=("K1", "NGT"), writes=("YTt",))
            for half in range(2):
                bank = B_M0 if half == 0 else B_M1
                bkey = "M0" if half == 0 else "M1"

                def f(e, half=half, bank=bank):
                    last = None
                    for kc in range(8):
                        last = e.matmul(PS[bank][:, 0:512], lhsT=YTt[:, kc, :],
                                        rhs=Wo_[:, kc, half * 512:(half + 1) * 512], start=(kc == 0), stop=(kc == 7))
                    return last
                S.op("pe", f, reads=("Wo", "YTt"), writes=(bkey,))
                S.op("dve", lambda e, half=half, bank=bank: e.scalar_tensor_tensor(
                    out=H[:, ti, half * 512:(half + 1) * 512], in0=H[:, ti, half * 512:(half + 1) * 512],
                    scalar=ALPHA, in1=PS[bank][:, 0:512], op0=ALU.mult, op1=ALU.add),
                    reads=(bkey, f"H{ti}"), writes=(f"H{ti}",))

    def groupL(g):
        i0 = 4 * g
        gp = g % 2
        HTG = HTGs[gp]
        for t in range(4):
            i = i0 + t
            xb = XG[t % 2]
            xk = f"XG{t % 2}"
            S.dma("sp", lambda e, xb=xb, i=i: e.dma_start(out=xb[:, :], in_=xall[i * 128:(i + 1) * 128, :]),
                  writes=(xk,))
            ln_A(xb[:, :], xb[:, :], (xk,), (xk,))
            to_feature_major(xb[:, :], (xk,), HTG[:, :, t * 128:(t + 1) * 128], (f"HTG{gp}_{t}",))

    def groupX(g):
        i0 = 4 * g
        gp = g % 2
        HTG = HTGs[gp]
        pm = PM[:, i0:i0 + 1]
        htk = tuple(f"HTG{gp}_{t}" for t in range(4))

        def mm512(out_ap, wsel, M=None):
            def f(e):
                last = None
                for kc in range(8):
                    last = e.matmul(out_ap, lhsT=wsel(kc), rhs=HTG[:, kc, :], start=(kc == 0), stop=(kc == 7))
                return last
            return f
        S.op("pe", mm512(PS[B_G][0:4, 0:512], lambda kc: Ws[:, kc, 0:4]), reads=htk + ("Ws",), writes=("G_a",))
        S.op("pe", mm512(PS[B_K0][0:4, 0:512], lambda kc: Ws[:, kc, 4:8]), reads=htk + ("Ws",), writes=("K0",))
        S.op("pe", mm512(PS[B_F0][0:16, 0:512], lambda kc: Ws[:, kc, 8:24]), reads=htk + ("Ws",), writes=("F0",))
        S.op("pe", mm512(PS[B_F1][:, 0:512], lambda kc: Wf[:, kc, 512 + 128: 512 + 256]),
             reads=htk + ("Wfk",), writes=("F1",))
        S.op("act", lambda e: e.activation(out=ZG[:, 0, :], in_=PS[B_G][0:4, 0:512], func=AF.Exp,
                                           bias=GB[:, 0:1], scale=1.0), reads=("G_a", "GB0", "ZGa"), writes=("EIG", "ZGa"))
        S.op("act", lambda e: e.activation(out=EFG[:, :], in_=PS[B_K0][0:4, 0:512], func=AF.Exp,
                                           bias=GB[:, 1:2], scale=-1.0), reads=("K0", "GB1"), writes=("EFG",))
        S.op("act", lambda e: e.activation(out=GLRG[:, :], in_=PS[B_F0][0:16, 0:512], func=AF.Copy),
             reads=("F0",), writes=("GLRG",))
        for p in range(2):
            bank = B_G if p == 0 else B_K0
            bkey = "G_a" if p == 0 else "K0"
            S.op("pe", lambda e, p=p, bank=bank: e.matmul(PS[bank][:, 0:512], lhsT=GLW[0:16, p * 128:(p + 1) * 128],
                                                          rhs=GLRG[0:16, :], start=True, stop=True),
                 reads=("GLRG", "GLW"), writes=(bkey,))
            S.op("act", lambda e, p=p, bank=bank: e.activation(out=E2G[:, p, :], in_=PS[bank][:, 0:512], func=AF.Exp,
                                                               bias=GLB[:, p:p + 1], scale=-1.0),
                 reads=(bkey, "GLB"), writes=(f"E2G{p}",))
        S.op("act", lambda e: e.activation(out=E2G[:, :, :], in_=E2G[:, :, :], func=AF.Ln, bias=1.0, scale=1.0),
             reads=("E2G0", "E2G1"), writes=("E2G0", "E2G1"))
        S.op("dve", lambda e: e.tensor_scalar(out=EFG[:, :], in0=EFG[:, :], scalar1=1.0, scalar2=None, op0=ALU.add),
             reads=("EFG",), writes=("EFG",))
        for t in range(4):
            tk = slice(t * 128, (t + 1) * 128)
            S.op("dve", lambda e, tk=tk: e.tensor_tensor_scan(out=ZG[:, 1, tk], data0=EFG[:, tk],
                                                              data1=zeros_f[0:4, :], initial=1.0,
                                                              op0=ALU.mult, op1=ALU.add),
                 reads=("EFG",), writes=(f"ZGb{t}",))
        zgb = tuple(f"ZGb{t}" for t in range(4))
        S.op("dve", lambda e: e.tensor_tensor(out=ZG[:, 0, :], in0=ZG[:, 0, :], in1=ZG[:, 1, :], op=ALU.mult),
             reads=("EIG",) + zgb, writes=("ZGa", "EIG"))
        S.op("dve", lambda e: e.reciprocal(out=RLG[:, :],
                                           in_=ZG[:, 1, :].rearrange("p (t l) -> p t l", l=128)[:, :, 127]),
             reads=zgb, writes=("RLG",))
        for t in (2, 1, 0):
            S.op("dve", lambda e, t=t: e.tensor_tensor(out=RLG[:, t:t + 1], in0=RLG[:, t:t + 1],
                                                       in1=RLG[:, t + 1:t + 2], op=ALU.mult),
                 reads=("RLG",), writes=("RLG",))
        for t in range(4):
            tk = slice(t * 128, (t + 1) * 128)
            S.op("dve", lambda e, t=t: e.tensor_scalar(out=D4G[:, t, :], in0=ident_f[0:4, 0:4], scalar1=RLG[:, t:t + 1],
                                                       scalar2=None, op0=ALU.mult), reads=("RLG",), writes=(f"D4G{t}",))

            def f_sc(e, t=t, tk=tk):
                o = t * 12
                e.matmul(PS[B_G][:, o:o + 4], lhsT=ZG[0:4, 0, tk], rhs=ident_f[0:4, 0:4], start=True, stop=True)
                e.matmul(PS[B_G][:, o + 4:o + 8], lhsT=ZG[0:4, 1, tk], rhs=ident_f[0:4, 0:4], start=True, stop=True)
                return e.matmul(PS[B_G][:, o + 8:o + 12], lhsT=ones_f[0:4, :], rhs=D4G[0:4, t, :],
                                start=True, stop=True)
            S.op("pe", f_sc, reads=("ZGa", f"D4G{t}", "E2G0") + zgb, writes=("G_c",))
        S.op("dve", lambda e: e.tensor_copy(out=SCG[gp][:, :, 0:12],
                                            in_=PS[B_G][:, 0:48].rearrange("p (a b) -> p a b", a=4)),
             reads=("G_c",), writes=(f"SCG{gp}",))
        S.op("dve", lambda e: e.tensor_scalar(out=SCG[gp][:, :, 0:4], in0=SCG[gp][:, :, 0:4], scalar1=pm, scalar2=None,
                                              op0=ALU.mult), reads=(f"SCG{gp}", "PM"), writes=(f"SCG{gp}",))
        S.op("dve", lambda e: e.tensor_tensor(out=SCG[gp][:, :, 12:16], in0=SCG[gp][:, :, 0:4],
                                              in1=SCG[gp][:, :, 8:12], op=ALU.mult),
             reads=(f"SCG{gp}",), writes=(f"SCG{gp}",))
        for p in range(2):
            for t in range(4):
                tk = slice(t * 128, (t + 1) * 128)
                S.op("dve", lambda e, p=p, tk=tk: e.tensor_tensor_scan(out=BCG[:, p, tk], data0=ones_f[:, :],
                                                                       data1=E2G[:, p, tk], initial=0.0,
                                                                       op0=ALU.mult, op1=ALU.add),
                     reads=(f"E2G{p}",), writes=(f"BCG{p}{t}",))
        bck = tuple(f"BCG{p}{t}" for p in range(2) for t in range(4))
        S.op("dve", lambda e: e.tensor_copy(out=SUFG[:, :, :],
                                            in_=BCG[:, :, :].rearrange("p a (t l) -> p a t l", l=128)[:, :, :, 127]),
             reads=bck, writes=("SUFG",))
        for t in (2, 1, 0):
            S.op("dve", lambda e, t=t: e.tensor_tensor(out=SUFG[:, :, t], in0=SUFG[:, :, t], in1=SUFG[:, :, t + 1],
                                                       op=ALU.add), reads=("SUFG",), writes=("SUFG",))
        S.op("dve", lambda e: e.tensor_scalar(out=SUFG[:, :, :], in0=SUFG[:, :, :], scalar1=-1.0 / 16.0, scalar2=None,
                                              op0=ALU.mult), reads=("SUFG",), writes=("SUFG",))

        def f_ai(e):
            last = None
            for p in range(2):
                for t in range(4):
                    tk = slice(t * 128, (t + 1) * 128)
                    last = e.activation(out=E2G[:, p, tk], in_=BCG[:, p, tk], func=AF.Exp,
                                        bias=SUFG[:, p, t:t + 1], scale=1.0 / 16.0)
            return last
        S.op("act", f_ai, reads=bck + ("SUFG",), writes=("E2G0", "E2G1", "AIG"))
        S.op("act", lambda e: e.activation(out=ACLG[gp][:, :, :].rearrange("p t a -> p a t"), in_=SUFG[:, :, :],
                                           func=AF.Exp), reads=("SUFG",), writes=(f"ACLG{gp}",))
        for cp0 in (0, 2):
            for c in (cp0, cp0 + 1):
                bank = B_F0 if c % 2 == 0 else B_F1
                bkey = "F0" if c % 2 == 0 else "F1"
                if c != 1:
                    S.op("pe", mm512(PS[bank][:, 0:512], lambda kc, c=c: Wf[:, kc, 512 + c * 128: 512 + (c + 1) * 128]),
                         reads=htk + ("Wfk",), writes=(bkey,))
                S.op("act", lambda e, c=c, bank=bank: e.activation(out=RAWG[:, c, 3:515], in_=PS[bank][:, 0:512],
                                                                   func=AF.Identity, scale=pm),
                     reads=(bkey, "PM"), writes=(f"RAWG{c}",))
            for c in (cp0, cp0 + 1):
                cg = 4 + c
                S.op("dve", lambda e, c=c, cg=cg: e.tensor_scalar(out=CVG[:, c, :], in0=RAWG[:, c, 3:515],
                                                                  scalar1=CW[:, 3, cg:cg + 1], scalar2=CB[:, cg:cg + 1],
                                                                  op0=ALU.mult, op1=ALU.add),
                     reads=(f"RAWG{c}", "CW", "CB"), writes=(f"CVG{c}",))
            for j in range(3):
                for c in (cp0, cp0 + 1):
                    cg = 4 + c
                    S.op("dve", lambda e, c=c, cg=cg, j=j: e.scalar_tensor_tensor(
                        out=CVG[:, c, :], in0=RAWG[:, c, j:j + 512], scalar=CW[:, j, cg:cg + 1], in1=CVG[:, c, :],
                        op0=ALU.mult, op1=ALU.add), reads=(f"RAWG{c}", f"CVG{c}"), writes=(f"CVG{c}",))
            for c in (cp0, cp0 + 1):
                S.op("pool", lambda e, c=c: e.tensor_copy(out=RAWG[:, c, 0:3], in_=RAWG[:, c, 512:515]),
                     reads=(f"RAWG{c}",), writes=(f"RAWG{c}",))
        cvk = tuple(f"CVG{c}" for c in range(4))
        S.op("act", lambda e: e.activation(out=THG[:, :, :], in_=CVG[:, :, :], func=AF.Tanh, scale=0.5),
             reads=cvk, writes=("THG",))
        S.op("dve", lambda e: e.scalar_tensor_tensor(out=QKG[gp][:, :, :], in0=THG[:, :, :], scalar=1.0, in1=CVG[:, :, :],
                                                     op0=ALU.add, op1=ALU.mult), reads=("THG",) + cvk, writes=(f"QKG{gp}",))
        if g == NG4 - 1:
            def f_mq(e):
                last = None
                for c in range(4):
                    for kc in range(8):
                        last = e.matmul(PS[B_F0][:, c * 128:(c + 1) * 128], lhsT=Wf[:, kc, c * 128:(c + 1) * 128],
                                        rhs=HTG[:, kc, 384:512], start=(kc == 0), stop=(kc == 7))
                return last
            S.op("pe", f_mq, reads=htk + ("Wfq",), writes=("F0",))
            S.op("act", lambda e: e.activation(out=RAW[:, 0:4, 3:131],
                                               in_=PS[B_F0][:, :].rearrange("p (a b) -> p a b", a=4),
                                               func=AF.Identity, scale=pm), reads=("F0", "PM"), writes=("RAWq",))
            S.op("pool", lambda e: e.tensor_copy(out=RAW[:, 0:4, 0:3], in_=RAW[:, 0:4, 128:131]),
                 reads=("RAWq",), writes=("RAWq",))
        for t in range(4):
            tk = slice(t * 128, (t + 1) * 128)
            bank = B_K0 if t % 2 == 0 else B_G
            bkey = "K0" if t % 2 == 0 else "G_a"

            def f(e, tk=tk, bank=bank):
                last = None
                for kc in range(8):
                    last = e.matmul(PS[bank][:, 0:512], lhsT=HTG[:, kc, tk], rhs=WtA[:, kc, 0:512],
                                    start=(kc == 0), stop=(kc == 7))
                return last
            S.op("pe", f, reads=htk + ("Wt",), writes=(bkey,))
            S.op("act", lambda e, t=t, bank=bank: e.activation(
                out=VEG[gp][:, 4 * t:4 * t + 4, 0:128], in_=PS[bank][:, :].rearrange("p (a b) -> p a b", a=4),
                func=AF.Copy), reads=(bkey,), writes=(f"VEG{gp}",))
        for t in range(4):
            tk = slice(t * 128, (t + 1) * 128)
            bank = B_F0 if t % 2 == 0 else B_F1
            bkey = "F0" if t % 2 == 0 else "F1"

            def f(e, tk=tk, bank=bank):
                last = None
                for kc in range(8):
                    last = e.matmul(PS[bank][:, 0:512], lhsT=HTG[:, kc, tk], rhs=WtA[:, kc, 512:1024],
                                    start=(kc == 0), stop=(kc == 7))
                return last
            S.op("pe", f, reads=htk + ("Wt",), writes=(bkey,))
            S.op("act", lambda e, t=t, bank=bank: e.activation(out=GVG[gp][:, t, :], in_=PS[bank][:, :], func=AF.Copy),
                 reads=(bkey,), writes=(f"GVG{gp}",))

        for p in range(2):
            bank = B_K0 if p == 0 else B_G
            bkey = "K0" if p == 0 else "G_a"
            S.op("pe", mm512(PS[bank][:, 0:512], lambda kc, p=p: Wf[:, kc, 1280 + p * 128: 1280 + (p + 1) * 128]),
                 reads=htk + ("Wfg",), writes=(bkey,))
            S.op("dve", lambda e, p=p, bank=bank: e.tensor_tensor(out=KHG[gp][:, p, :], in0=PS[bank][:, 0:512],
                                                                  in1=E2G[:, p, :], op=ALU.mult),
                 reads=(bkey, "AIG"), writes=(f"KHG{gp}",))

    I_START = 4 * NG4
    if NG4 > 0:
        S.op("pool", lambda e: e.memset(VEG[0][:, :, :], 1.0), writes=("VEG0",))
        S.op("pool", lambda e: e.memset(VEG[1][:, :, :], 1.0), writes=("VEG1",))
        S.op("pool", lambda e: e.memset(RAWG[:, :, :], 0.0), writes=tuple(f"RAWG{c}" for c in range(4)))

        def groupY(g):
            for t in range(4):
                stageY(4 * g + t, grp=t)
        S.replay(S.record(groupL, 0))
        lists = [S.record(groupX, 0)]
        if NG4 > 1:
            lists.append(S.record(groupL, 1))
        S.replay(*lists)
        for g in range(NG4):
            lists = [S.record(groupY, g)]
            if g + 1 < NG4:
                lists.append(S.record(groupX, g + 1))
            if g + 2 < NG4:
                lists.append(S.record(groupL, g + 2))
            mode = os.environ.get("GMODE", "prop")
            if mode == "prop":
                S.replay(*lists)
            elif mode == "seq":
                for l_ in lists:
                    S.replay(l_)
            elif mode == "xfirst":
                if len(lists) > 1:
                    S.replay(lists[1])
                S.replay(lists[0], *lists[2:])
            elif mode == "xl":
                S.replay(*lists[1:])
                S.replay(lists[0])
        S.op("pool", lambda e: e.tensor_copy(out=RAW[:, 4:8, 0:3], in_=RAWG[:, :, 512:515]),
             reads=tuple(f"RAWG{c}" for c in range(4)) + ("RAWk",), writes=("RAWk",))
        barrier()
        load_main_weights()

    PIPE = True
    if PIPE:
        S.replay(S.record(stageX, I_START))
        for i in range(I_START, NT):
            ry = S.record(stageY, i)
            if os.environ.get("YSTOP"):
                ry = ry[:int(os.environ["YSTOP"])]
            if i + 1 < NT:
                S.replay(S.record(stageX, i + 1), ry)
            else:
                S.replay(ry)
    else:
        for i in range(I_START, NT):
            stageX(i)
            stageY(i)

    if stop_after == "A":
        return dump_H_and_finish()
    barrier()
    cur["XS"] = XS_B
    cur["HB"] = HB_B
    wload(Wk_, x_wk, [(0, 0, 1024)], "Wk")
    wload(Wq_, x_wq, [(0, 0, 1024)], "Wq")
    wload(Wx_, x_wo, [(0, 0, 1024)], "Wx")
    bcast_load(LNG[:, :], ln_g["ln1"], D, "LNG")
    bcast_load(LNB[:, :], ln_b["ln1"], D, "LNB")

    def pre_kv():
        for mc in range(2):
            xb = XBB[0]
            S.dma("sp", lambda e, xb=xb, mc=mc: e.dma_start(out=xb[:, :], in_=mem_d[mc * 128:(mc + 1) * 128, :]),
                  writes=("XBB",))
            S.op("act", lambda e, xb=xb: e.activation(out=MHB[:, :], in_=xb[:, :], func=AF.Copy),
                 reads=("XBB",), writes=("MHB",))
            transpose8(MHB, ("MHB",), MEMT[:, :, mc * 128:(mc + 1) * 128], (f"MEMT{mc}",))
        for c in range(8):
            bank = B_K0 if c % 2 == 0 else B_K1
            bkey = "K0" if c % 2 == 0 else "K1"

            def f_k(e, c=c, bank=bank):
                last = None
                for kc in range(8):
                    last = e.matmul(PS[bank][:, 0:256], lhsT=Wk_[:, kc, c * 128:(c + 1) * 128], rhs=MEMT[:, kc, :],
                                    start=(kc == 0), stop=(kc == 7))
                return last
            S.op("pe", f_k, reads=("Wk", "MEMT0", "MEMT1"), writes=(bkey,))
            S.op("act", lambda e, c=c, bank=bank: e.activation(out=KT[:, c, :], in_=PS[bank][:, 0:256], func=AF.Copy),
                 reads=(bkey,), writes=(f"KT{c}",))
        wload(Wk_, x_wv, [(0, 0, 1024)], "Wk")
        for mc in range(2):
            for half in range(2):
                bank = B_K0 if half == 0 else B_K1
                bkey = "K0" if half == 0 else "K1"

                def f_v(e, mc=mc, half=half, bank=bank):
                    last = None
                    for kc in range(8):
                        last = e.matmul(PS[bank][:, 0:512], lhsT=MEMT[:, kc, mc * 128:(mc + 1) * 128],
                                        rhs=Wk_[:, kc, half * 512:(half + 1) * 512], start=(kc == 0), stop=(kc == 7))
                    return last
                S.op("pe", f_v, reads=("Wk", "MEMT0", "MEMT1"), writes=(bkey,))
                S.op("act", lambda e, mc=mc, half=half, bank=bank: e.activation(
                    out=VV[:, mc, half * 512:(half + 1) * 512], in_=PS[bank][:, 0:512], func=AF.Copy),
                    reads=(bkey,), writes=(f"VV{mc}{half}",))

    def stageB1(ti):
        tok = slice(ti * 128, (ti + 1) * 128)
        layer_norm(H[:, ti, :], H[:, ti, :], (f"H{ti}",), (f"H{ti}",), xs=XS_B, xskey="XS1", sc=(ST6c, MVc, RSc),
                   sfx="c")
        to_feature_major(H[:, ti, :], (f"H{ti}",), HT[:, :, tok], (f"HT{ti}",), B_T, "PS_T", hb=MHB, hbkey="MHB")

    KTK = tuple(f"KT{c}" for c in range(8))
    VVK = ("VV00", "VV01", "VV10", "VV11")
    S.dma("sp", lambda e: e.dma_start(out=LNG2[:, :], in_=ln_g["ln2"][0:D].partition_broadcast(128)), writes=("LNG2",))
    S.dma("sp", lambda e: e.dma_start(out=LNB2[:, :], in_=ln_b["ln2"][0:D].partition_broadcast(128)), writes=("LNB2",))
    S.replay(S.record(pre_kv))
    S.replay(S.record(stageB1, 0))
    if NT_MAIN > 1:
        S.replay(S.record(stageB1, 1))
    def stageP(ti):
        tok = slice(ti * 128, (ti + 1) * 128)
        par = ti % 2
        PT, RINV = PTb[par], RINVb[par]
        for g in range(2):
            bank = B_F0 if g == 0 else B_F1
            bkey = "F0" if g == 0 else "F1"

            def f_q(e, g=g, bank=bank):
                last = None
                for c in range(4):
                    for kc in range(8):
                        last = e.matmul(PS[bank][:, c * 128:(c + 1) * 128],
                                        lhsT=Wq_[:, kc, (g * 4 + c) * 128:(g * 4 + c + 1) * 128],
                                        rhs=HT[:, kc, tok], start=(kc == 0), stop=(kc == 7))
                return last
            S.op("pe", f_q, reads=("Wq", f"HT{ti}"), writes=(bkey,))
            S.op("act", lambda e, g=g, bank=bank: e.activation(
                out=QT[:, g * 4:(g + 1) * 4, :], in_=PS[bank][:, :].rearrange("p (a b) -> p a b", a=4),
                func=AF.Copy), reads=(bkey,), writes=(f"QT{g}",))

        def f_s(e):
            last = None
            for hd in range(4):
                bank = B_M0 if hd < 2 else B_M1
                for c in range(2):
                    last = e.matmul(PS[bank][:, (hd % 2) * 256:(hd % 2 + 1) * 256], lhsT=QT[:, 2 * hd + c, :],
                                    rhs=KT[:, 2 * hd + c, :], start=(c == 0), stop=(c == 1))
            return last
        S.op("pe", f_s, reads=("QT0", "QT1") + KTK, writes=("M0", "M1"))
        for half in range(2):
            bank = B_M0 if half == 0 else B_M1
            bkey = "M0" if half == 0 else "M1"
            S.op("dve", lambda e, half=half, bank=bank: e.tensor_reduce(
                out=MX[:, 2 * half:2 * half + 2], in_=PS[bank][:, :].rearrange("p (a b) -> p a b", a=2), axis=AX.X,
                op=ALU.max), reads=(bkey,), writes=(f"MX{half}",))
        S.op("dve", lambda e: e.tensor_scalar(out=NB[:, :], in0=MX[:, :], scalar1=-1.0 / 16.0, scalar2=None,
                                              op0=ALU.mult), reads=("MX0", "MX1"), writes=("NB",))
        for hd in range(4):
            bank = B_M0 if hd < 2 else B_M1
            bkey = "M0" if hd < 2 else "M1"
            sc_ap = PS[bank][:, (hd % 2) * 256:(hd % 2 + 1) * 256]
            S.op("act", lambda e, hd=hd, sc_ap=sc_ap: e.activation(
                out=PEX[:, hd, :], in_=sc_ap, func=AF.Exp, bias=NB[:, hd:hd + 1], scale=1.0 / 16.0,
                accum_out=RSUM[:, hd:hd + 1]), reads=(bkey, "NB"), writes=(f"PEX{hd}", f"RSUM{hd}"))
        S.op("dve", lambda e: e.reciprocal(out=RINV[:, :], in_=RSUM[:, :]),
             reads=tuple(f"RSUM{hd}" for hd in range(4)), writes=(f"RINV{par}",))

        def f_pt(e):
            last = None
            for hd in range(4):
                for mc in range(2):
                    j = hd * 2 + mc
                    last = e.transpose(out=PSb[B_F0][:, j * 128:(j + 1) * 128],
                                       in_=PEX[:, hd, mc * 128:(mc + 1) * 128], identity=ident_bf[:, :])
            return last
        S.op("pe", f_pt, reads=tuple(f"PEX{hd}" for hd in range(4)), writes=("F0",))
        S.op("act", lambda e: e.activation(out=PT[:, :, :], in_=PSb[B_F0][:, :].rearrange("p (a b) -> p a b", a=8),
                                           func=AF.Copy), reads=("F0",), writes=(f"PT{par}",))

    def stageQ(ti):
        tok = slice(ti * 128, (ti + 1) * 128)
        par = ti % 2
        PT, RINV = PTb[par], RINVb[par]

        def f_o2(e):
            last = None
            for hd in range(4):
                bank = B_K0 if hd < 2 else B_K1
                for mc in range(2):
                    last = e.matmul(PS[bank][:, (hd % 2) * 256:(hd % 2 + 1) * 256], lhsT=PT[:, hd * 2 + mc, :],
                                    rhs=VV[:, mc, hd * 256:(hd + 1) * 256], start=(mc == 0), stop=(mc == 1))
            return last
        S.op("pe", f_o2, reads=(f"PT{par}",) + VVK, writes=("K0", "K1"))
        for half in range(2):
            bank = B_K0 if half == 0 else B_K1
            bkey = "K0" if half == 0 else "K1"
            S.op("dve", lambda e, half=half, bank=bank: e.tensor_tensor(
                out=OB[:, half * 512:(half + 1) * 512].rearrange("p (a b) -> p a b", a=2),
                in0=PS[bank][:, :].rearrange("p (a b) -> p a b", a=2),
                in1=RINV[:, 2 * half:2 * half + 2].rearrange("p (h o) -> p h o", o=1).broadcast_to([128, 2, 256]),
                op=ALU.mult), reads=(bkey, f"RINV{par}"), writes=(f"OB{half}",))
        transpose8(OB, ("OB0", "OB1"), HTt_B[:, :, :], ("HTtB",), B_G, "G_a")
        proj_resid(ti, HTt_B, ("HTtB",), Wx_, "Wx")
        layer_norm(H[:, ti, :], H[:, ti, :], (f"H{ti}",), (f"H{ti}",), xs=XBB[0], xskey="XBB", lng=LNG2, lnb=LNB2,
                   gkey="LNG2", bkey="LNB2")
        to_feature_major(H[:, ti, :], (f"H{ti}",), HT[:, :, tok], (f"HT{ti}",), B_G, "G_a", hb=HB_B, hbkey="HB")

    S.replay(S.record(stageP, 0))
    for ti in range(NT_MAIN):
        lists = [S.record(stageQ, ti)]
        if ti + 1 < NT_MAIN:
            lists.append(S.record(stageP, ti + 1))
        if ti + 2 < NT_MAIN:
            lists.append(S.record(stageB1, ti + 2))
        S.replay(*lists)

    if stop_after == "B":
        return dump_H_and_finish()
    barrier()
    cur["XS"] = XS_C
    bcast_load(LNG[:, :], ln_g["ln3"], D, "LNG")
    bcast_load(LNB[:, :], ln_b["ln3"], D, "LNB")
    HTK = tuple(f"HT{t}" for t in range(NT_MAIN))
    hid = HID[0]
    for q in range(4):
        W1 = W1s[q % 2]
        W2 = W2s[q % 2]
        k1 = f"W1_{q % 2}"
        k2 = f"W2_{q % 2}"
        wload(W1, w_ff1, [(0, q * 1024, 512)], k1 + "a")
        wload(W1, w_ff1, [(512, q * 1024 + 512, 512)], k1 + "b")
        S.dma("pool", lambda e, W2=W2, q=q: e.dma_start(
            out=W2[:, :, :], in_=w_ff2[q * 1024:(q + 1) * 1024, :].rearrange("(kc p) n -> p kc n", p=128)),
            writes=(k2,))
        for tg in range(TGN):
            for fc in range(8):
                bank = B_F0 if fc % 2 == 0 else B_F1
                bkey = "F0" if fc % 2 == 0 else "F1"
                rl = RL2[fc % 2]
                rlk = f"RL2_{fc % 2}"
                def f1(e, fc=fc, bank=bank, W1=W1, tg=tg):
                    last = None
                    for kc in range(8):
                        last = e.matmul(PS[bank][:, 0:TPG * 128], lhsT=W1[:, kc, fc * 128:(fc + 1) * 128],
                                        rhs=HT[:, kc, tg * TPG * 128:(tg + 1) * TPG * 128], start=(kc == 0), stop=(kc == 7))
                    return last
                S.op("pe", f1, reads=(k1 + ("a" if fc < 4 else "b"),) + HTK[tg * TPG:(tg + 1) * TPG], writes=(bkey,))
                S.op("act", lambda e, bank=bank, rl=rl: e.activation(out=rl[:, 0:TPG * 128], in_=PS[bank][:, 0:TPG * 128],
                                                                     func=AF.Relu), reads=(bkey,), writes=(rlk,))
                S.op("dve", lambda e, rl=rl, fc=fc: e.tensor_tensor(out=hid[:, fc, 0:TPG * 128], in0=rl[:, 0:TPG * 128],
                                                                    in1=rl[:, 0:TPG * 128], op=ALU.mult),
                     reads=(rlk,), writes=(f"HID_{fc}",))
            hk_all = tuple(f"HID_{fc}" for fc in range(8))
            for t in range(TPG):
                ti = tg * TPG + t
                for half in range(2):
                    bank = B_K0 if half == 0 else B_K1
                    bkey = "K0" if half == 0 else "K1"
                    def f2(e, t=t, half=half, bank=bank, W2=W2):
                        last = None
                        for fc in range(8):
                            last = e.matmul(PS[bank][:, 0:512], lhsT=hid[:, fc, t * 128:(t + 1) * 128],
                                            rhs=W2[:, fc, half * 512:(half + 1) * 512],
                                            start=(fc == 0), stop=(fc == 7))
                        return last
                    S.op("pe", f2, reads=(k2,) + hk_all, writes=(bkey,))
                    hs = H[:, ti, half * 512:(half + 1) * 512]
                    if q == 0:
                        S.op("dve", lambda e, hs=hs, bank=bank: e.scalar_tensor_tensor(
                            out=hs, in0=hs, scalar=ALPHA, in1=PS[bank][:, 0:512], op0=ALU.mult, op1=ALU.add),
                            reads=(bkey, f"H{ti}"), writes=(f"H{ti}",))
                    else:
                        S.op("dve", lambda e, hs=hs, bank=bank: e.tensor_tensor(
                            out=hs, in0=hs, in1=PS[bank][:, 0:512], op=ALU.add),
                            reads=(bkey, f"H{ti}"), writes=(f"H{ti}",))
                if q == 3:
                    layer_norm(H[:, ti, :], H[:, ti, :], (f"H{ti}",), (f"H{ti}",))
                    S.dma("sp", lambda e, ti=ti: e.dma_start(out=out_d[ti * 128:(ti + 1) * 128, :], in_=H[:, ti, :]),
                          reads=(f"H{ti}",), writes=(f"OUT{ti}",))
    S.fence("sp", [f"OUT{t}" for t in range(NT_MAIN)])

    return finish()


_NC_CACHE = {}


def kernel(**inputs):
    f = lambda k: np.ascontiguousarray(np.asarray(inputs[k], dtype=np.float32))
    x = f("x")
    mem = f("mem")
    if "nc" not in _NC_CACHE:
        _NC_CACHE["nc"] = build_program()
    nc = _NC_CACHE["nc"]
    shared = {
        "w_in": f("w_in")[0], "conv_w": f("conv_w")[0], "conv_b": f("conv_b")[0],
        "m_i_bias": f("m_i_bias")[0], "m_f_bias": f("m_f_bias")[0], "m_norm_g": f("m_norm_g")[0],
        "g_lr_w": f("g_lr_w")[0], "g_lr_b": f("g_lr_b")[0], "g_norm_g": f("g_norm_g")[0],
        "w_out": f("w_out")[0], "x_wq": f("x_wq")[0], "x_wk": f("x_wk")[0], "x_wv": f("x_wv")[0],
        "x_wo": f("x_wo")[0], "w_ff1": f("w_ff1")[0], "w_ff2": f("w_ff2")[0],
        "ln_in_g": f("ln_in_g"), "ln_in_b": f("ln_in_b"),
        "ln1_g": f("ln1_g")[0], "ln1_b": f("ln1_b")[0], "ln2_g": f("ln2_g")[0], "ln2_b": f("ln2_b")[0],
        "ln3_g": f("ln3_g")[0], "ln3_b": f("ln3_b")[0],
    }
    in_maps = []
    SEG = NT_MAIN * 128
    for c in range(8):
        b, s = c // 4, c % 4
        start = s * SEG
        xa = np.zeros((NT * 128, D), np.float32)
        pm = np.zeros((128, NT), np.float32)
        npre = NT_PRE * 128
        have = min(start, npre)
        if have > 0:
            xa[npre - have:npre] = x[b, start - have:start]
            pm[:, NT_PRE - have // 128:NT_PRE] = 1.0
        xa[npre:] = x[b, start:start + SEG]
        pm[:, NT_PRE:] = 1.0
        m = dict(shared)
        m["xall"] = xa
        m["pmask"] = pm
        m["mem"] = mem[b]
        in_maps.append(m)
    res = run_bass_kernel_spmd(nc, in_maps, core_ids=list(range(8)))
    out = np.zeros((2, 8192, D), np.float32)
    for c in range(8):
        b, s = c // 4, c % 4
        out[b, s * SEG:(s + 1) * SEG] = np.asarray(res.results[c]["out"], dtype=np.float32)
    return out
```

```python
import math
import os
import numpy as np
import concourse.bass as bass
import concourse.mybir as mybir
from concourse.bass_utils import run_bass_kernel_spmd
from contextlib import ExitStack

F32 = mybir.dt.float32
BF16 = mybir.dt.bfloat16
AF = mybir.ActivationFunctionType
ALU = mybir.AluOpType
AX = mybir.AxisListType

D = 1024
NT_MAIN = 16
NT_PRE = 48
NT = NT_MAIN + NT_PRE
ALPHA = 2.0 ** 0.25
EPS = 1e-5
NMEM = 256
DFF = 4096
KDMA = 6
ENGS = ["pe", "act", "dve", "pool", "sp"]


class Sched:
    def __init__(self):
        self.ops = {e: [] for e in ENGS}
        self.cnt = {e: 0 for e in ENGS}
        self.lastw = {}
        self.readers = {}
        self.waited = {e: {} for e in ENGS}
        self.dma_n = {e: 0 for e in ENGS}
        self.rec = None

    def record(self, f, *a):
        self.rec = []
        f(*a)
        r, self.rec = self.rec, None
        return r

    def replay(self, *lists):
        pos = [0] * len(lists)
        while True:
            best, bf = None, 2.0
            for j, l in enumerate(lists):
                if pos[j] < len(l):
                    fr = pos[j] / len(l)
                    if fr < bf:
                        best, bf = j, fr
            if best is None:
                break
            kind, a = lists[best][pos[best]]
            pos[best] += 1
            getattr(self, kind)(*a)

    class _FakeIns:
        def then_inc(self, *a, **k):
            return self

    class _FakeEng:
        def __init__(self):
            self.calls = []

        def __getattr__(self, name):
            def f(*a, **k):
                self.calls.append((name, a, k))
                return Sched._FakeIns()
            return f

    @staticmethod
    def _free(ap):
        try:
            sh = list(ap.shape)
            n = 1
            for x in sh[1:]:
                n *= int(x)
            return n
        except Exception:
            return 128

    def est_cost(self, kind, eng, fn):
        if fn is None:
            return 0.0
        fe = Sched._FakeEng()
        try:
            fn(fe)
        except Exception:
            return 0.5
        c = 0.0
        for (name, a, k) in fe.calls:
            out = k.get("out", a[0] if a else None)
            n = Sched._free(out) if out is not None else 128
            if name == "matmul":
                rhs = k.get("rhs", a[2] if len(a) > 2 else None)
                nn = Sched._free(rhs) if rhs is not None else 128
                c += max(nn, 64) / 1500.0 + 0.015
            elif name == "transpose":
                c += 0.1
            elif name == "dma_start":
                c += 0.15
            elif eng == "act":
                c += 0.2 + n * 0.00087
            elif eng == "dve":
                c += 0.1 + n * 0.0011
            elif eng == "pool":
                c += 0.3 + n * 0.002
            else:
                c += 0.2
        return c + 0.08

    def schedule(self, *streams):
        if not hasattr(self, "t_eng"):
            self.t_eng = {e: 0.0 for e in ENGS}
            self.t_w = {}
            self.t_r = {}
        streams = [list(x) for x in streams if x]
        wsets, rsets = [], []
        for st in streams:
            ws, rs = set(), set()
            for (kind, a) in st:
                rs.update(self.norm(k) for k in a[2])
                ws.update(self.norm(k) for k in a[3])
            wsets.append(ws)
            rsets.append(rs)
        for i1 in range(len(streams)):
            for i2 in range(len(streams)):
                if i1 != i2:
                    bad = wsets[i1] & (wsets[i2] | rsets[i2])
                    assert not bad, ("streams share scratch", i1, i2, sorted(bad))
        costs = [[self.est_cost(kind, a[0], a[1]) for (kind, a) in st] for st in streams]
        pos = [0] * len(streams)
        while True:
            best, bt = None, None
            for j, st in enumerate(streams):
                if pos[j] >= len(st):
                    continue
                kind, a = st[pos[j]]
                eng, fn, reads, writes = a
                t = self.t_eng[eng]
                for k in reads:
                    t = max(t, self.t_w.get(self.norm(k), 0.0))
                for k in writes:
                    nk = self.norm(k)
                    t = max(t, self.t_w.get(nk, 0.0), self.t_r.get(nk, 0.0))
                if bt is None or t < bt - 1e-9:
                    best, bt = j, t
            if best is None:
                break
            kind, a = streams[best][pos[best]]
            eng, fn, reads, writes = a
            c = costs[best][pos[best]]
            pos[best] += 1
            fin = bt + c
            self.t_eng[eng] = bt + (0.15 if kind == "dma" else c)
            done = fin + (2.0 if kind == "dma" else 0.0)
            for k in writes:
                self.t_w[self.norm(k)] = done
            for k in reads:
                nk = self.norm(k)
                if self.t_r.get(nk, 0.0) < done:
                    self.t_r[nk] = done
            getattr(self, kind)(*a)

    @staticmethod
    def norm(k):
        if k == "PS_T":
            return "ps0"
        if k == "F0":
            return "ps1"
        if k == "F1":
            return "ps2"
        if k == "K0" or k.startswith("K0o"):
            return "ps3"
        if k == "K1" or k.startswith("K1o"):
            return "ps4"
        if k.startswith("G_"):
            return "ps5"
        if k.startswith("M0"):
            return "ps6"
        if k.startswith("M1"):
            return "ps7"
        return k

    def _deps(self, eng, reads, writes):
        deps = {}

        def add(tok, war=False):
            if tok is None:
                return
            sk, val = tok
            if sk == ("e", eng) and eng == "pe":
                return
            if deps.get(sk, 0) < val:
                deps[sk] = val

        for k in reads:
            add(self.lastw.get(k))
        for k in writes:
            add(self.lastw.get(k))
            for sk, val in self.readers.get(k, {}).items():
                add((sk, val), war=True)
        out = []
        w = self.waited[eng]
        for sk, val in deps.items():
            if w.get(sk, 0) < val:
                w[sk] = val
                out.append((sk, val))
        return out

    def _commit(self, tok, reads, writes):
        for k in writes:
            self.lastw[k] = tok
            self.readers[k] = {}
        for k in reads:
            r = self.readers.setdefault(k, {})
            if r.get(tok[0], 0) < tok[1]:
                r[tok[0]] = tok[1]

    def op(self, eng, fn, reads=(), writes=()):
        if self.rec is not None:
            self.rec.append(("op", (eng, fn, reads, writes)))
            return
        reads = tuple(dict.fromkeys(self.norm(k) for k in reads))
        writes = tuple(dict.fromkeys(self.norm(k) for k in writes))
        waits = self._deps(eng, reads, writes)
        self.cnt[eng] += 1
        tok = (("e", eng), self.cnt[eng])
        self.ops[eng].append((fn, waits, ("e", eng), 1))
        self._commit(tok, reads, writes)

    def dma(self, q, fn, reads=(), writes=()):
        if self.rec is not None:
            self.rec.append(("dma", (q, fn, reads, writes)))
            return
        reads = tuple(dict.fromkeys(self.norm(k) for k in reads))
        writes = tuple(dict.fromkeys(self.norm(k) for k in writes))
        j = self.dma_n[q]
        self.dma_n[q] += 1
        slot = j % KDMA
        val = 16 * (j // KDMA + 1)
        sk = ("d", q, slot)
        waits = self._deps(q, reads, writes)
        if val > 16 and self.waited[q].get(sk, 0) < val - 16:
            self.waited[q][sk] = val - 16
            waits.append((sk, val - 16))
        self.ops[q].append((fn, waits, sk, 16))
        self._commit((sk, val), reads, writes)

    def fence(self, eng, keys):
        keys = tuple(dict.fromkeys(self.norm(k) for k in keys))
        waits = self._deps(eng, keys, keys)
        self.ops[eng].append((None, waits, None, 0))


def build_program(nt_pre=48, nt_main=16, stop_after=None):
    global NT_MAIN, NT_PRE, NT
    NT_MAIN, NT_PRE = nt_main, nt_pre
    NT = NT_MAIN + NT_PRE
    TGN = max(1, NT_MAIN // 4)
    TPG = NT_MAIN // TGN
    nc = bass.Bass("TRN2", target_bir_lowering=False)

    def din(name, shape):
        return nc.dram_tensor(name, list(shape), F32, kind="ExternalInput").ap()

    xall = din("xall", [NT * 128, D])
    pmask_d = din("pmask", [128, NT])
    mem_d = din("mem", [NMEM, D])
    ln_g = {k: din(k + "_g", [D]) for k in ("ln_in", "ln1", "ln2", "ln3")}
    ln_b = {k: din(k + "_b", [D]) for k in ("ln_in", "ln1", "ln2", "ln3")}
    w_in = din("w_in", [D, 3608])
    conv_w = din("conv_w", [4, D])
    conv_b = din("conv_b", [D])
    m_i_bias = din("m_i_bias", [4])
    m_f_bias = din("m_f_bias", [4])
    m_norm_g = din("m_norm_g", [512])
    g_lr_w = din("g_lr_w", [16, 256])
    g_lr_b = din("g_lr_b", [256])
    g_norm_g = din("g_norm_g", [512])
    w_out = din("w_out", [D, D])
    x_wq = din("x_wq", [D, D])
    x_wk = din("x_wk", [D, D])
    x_wv = din("x_wv", [D, D])
    x_wo = din("x_wo", [D, D])
    w_ff1 = din("w_ff1", [D, DFF])
    w_ff2 = din("w_ff2", [DFF, D])
    out_d = nc.dram_tensor("out", [NT_MAIN * 128, D], F32, kind="ExternalOutput").ap()

    S = Sched()
    es = ExitStack()

    def sb(name, shape, dt=F32):
        return es.enter_context(nc.sbuf_tensor(name, list(shape), dt))

    NG4 = NT_PRE // 4 if (NT_PRE >= 4 and NT_PRE % 4 == 0) else 0
    HTILES = max(NT_MAIN, 16) if NG4 > 0 else NT_MAIN
    Hflat = sb("Hflat", [128, HTILES * D])
    H = Hflat[:, 0:NT_MAIN * D].rearrange("p (a b) -> p a b", a=NT_MAIN)
    Hbf = Hflat.bitcast(BF16)
    HTflat = sb("HTflat", [128, max(8 * NT_MAIN * 128, 16384)], BF16)
    HT = HTflat[:, 0:8 * NT_MAIN * 128].rearrange("p (a b) -> p a b", a=8)
    ARENA_BYTES = 73 * 1024
    S2_BYTES = 22 * 1024 + 768
    arena = sb("arena", [128, ARENA_BYTES // 2], BF16)
    s2 = sb("s2", [128, S2_BYTES // 2], BF16)
    regions = {"arena": (arena, ARENA_BYTES), "ht": (HTflat, 32 * 1024), "s2": (s2, S2_BYTES),
               "hp": (Hbf, HTILES * D * 4)}
    bump = {}

    def cv(region, phase, shape, dt=BF16):
        t, cap = regions[region]
        off = bump.get((region, phase), 0)
        off = (off + 31) // 32 * 32
        n = int(np.prod(shape[1:]))
        esz = 2 if dt == BF16 else 4
        assert off + n * esz <= cap, (region, phase, off, n * esz, cap)
        bump[(region, phase)] = off + n * esz
        a = t[0:shape[0], off // 2: off // 2 + n * esz // 2]
        if dt != BF16:
            a = a.bitcast(dt)
        if len(shape) == 3:
            a = a.rearrange("p (a b) -> p a b", a=shape[1])
        return a

    Wf = cv("arena", "A", [128, 8, 1536])
    WtA = cv("arena", "A", [128, 8, 1024])
    Ws = cv("arena", "A", [128, 8, 24])
    WTB_OFF = bump[("arena", "A")]
    WtB = cv("arena", "A", [128, 8, 1024])
    Wo_ = cv("arena", "A", [128, 8, 1024])
    Wq_ = cv("arena", "B", [128, 8, 1024])
    Wx_ = cv("arena", "B", [128, 8, 1024])
    Wk_ = cv("arena", "B", [128, 8, 1024])
    MEMT = cv("arena", "B", [128, 8, NMEM])
    KT = cv("arena", "B", [128, 8, NMEM])
    VV = cv("arena", "B", [128, 2, D])
    MHB = cv("arena", "B", [128, D])
    LNG2 = cv("arena", "B", [128, D], F32)
    LNB2 = cv("arena", "B", [128, D], F32)
    W1s, W2s = [], []
    for _ in range(2):
        W1s.append(cv("arena", "C", [128, 8, 1024]))
        W2s.append(cv("arena", "C", [128, 8, 1024]))
    RL2 = [cv("arena", "C", [128, 512], F32) for i in range(2)]
    XS_C = cv("arena", "C", [128, D], F32)

    ident_bf = sb("ident_bf", [128, 128], BF16)
    ident_f = sb("ident_f", [128, 128])
    ones_f = sb("ones_f", [128, 128])
    zeros_f = sb("zeros_f", [128, 128])
    mask_ut = sb("mask_ut", [128, 128], BF16)
    negh = sb("negh", [128, 4])
    LNG = sb("LNG", [128, D])
    LNB = sb("LNB", [128, D])
    CW = sb("CW", [128, 4, 8])
    CB = sb("CB", [128, 8])
    GB = sb("GB", [4, 4])
    GLW = sb("GLW", [16, 256])
    GLB = sb("GLB", [128, 2])
    PM = sb("PM", [128, NT])
    ST6 = sb("ST6", [128, 2, 6])
    MV = sb("MV", [128, 2])
    RS = sb("RS", [128, 4])
    EI = sb("EI", [4, 128])
    EF = sb("EF", [4, 128])
    Z = sb("Z", [4, 256])
    RL = sb("RL", [4, 1])
    D4 = sb("D4", [4, 4])
    SC = sb("SC", [128, 12])
    SM = sb("SM", [128, 16])
    ST6b = sb("ST6b", [128, 6])
    MVb = sb("MVb", [128, 2])
    ST6c = sb("ST6c", [128, 2, 6])
    MVc = sb("MVc", [128, 2])
    RSc = sb("RSc", [128, 4])
    ST6b2 = sb("ST6b2", [128, 2, 6])
    MVb2 = sb("MVb2", [128, 2, 2])
    GLR = sb("GLR", [16, 128])
    MX = sb("MX", [128, 4])
    NB = sb("NB", [128, 4])
    RSUM = sb("RSUM", [128, 4])
    RINVb = [sb(f"RINV{i}", [128, 4]) for i in range(2)]
    XB = [cv("ht", "A", [128, D], F32)]
    HB_A = cv("ht", "A", [128, D])
    XS_A = cv("ht", "A", [128, D], F32)
    RAW = cv("ht", "A", [128, 8, 131], F32)
    CV = cv("ht", "A", [128, 4, 128], F32)
    OGT = cv("ht", "A", [128, 512], F32)
    TH = OGT.rearrange("p (a b) -> p a b", a=4)
    OGb = [cv("ht", "A", [128, D], F32) for _ in range(2)]
    QKb = [cv("s2", "A", [128, 8, 128]), cv("ht", "A", [128, 8, 128])]
    VEb = [cv("s2", "A", [128, 4, 129]), cv("ht", "A", [128, 4, 129])]
    GVb = [cv("s2", "A", [128, 512]), cv("ht", "A", [128, 512])]
    KHb = [cv("s2", "A", [128, 2, 128]), cv("ht", "A", [128, 2, 128])]
    QHb = [cv("s2", "A", [128, 2, 128]), cv("ht", "A", [128, 2, 128])]
    HTt = [cv("s2", "A", [128, 8, 128])]
    YTt = cv("s2", "A", [128, 8, 128])
    KW = cv("s2", "A", [128, 4, 128])
    Cf = cv("s2", "A", [128, 4, 129], F32)
    Cb = cv("s2", "A", [128, 4, 129])
    SW = [cv("s2", "A", [128, 128]) for i in range(2)]
    TT2 = cv("s2", "A", [128, 2, 128], F32)
    TT = TT2[:, 0, :]
    E2 = cv("s2", "A", [128, 2, 128], F32)
    AC = cv("s2", "A", [128, 2, 128], F32)
    AI = cv("s2", "A", [128, 2, 128], F32)
    KHt = cv("s2", "A", [128, 2, 128])
    ATb = cv("s2", "A", [128, 4, 128])
    Sf = cv("s2", "A", [128, 2, 128], F32)
    Sb_ = cv("s2", "A", [128, 2, 128])
    Y = cv("s2", "A", [128, D])
    if NG4 > 0:
        XGall = cv("hp", "P", [128, 2 * D], F32)
        XG = [XGall[:, 0:D], XGall[:, D:2 * D]]
        HTG1 = cv("hp", "P", [128, 8, 512])
        RAWG = cv("hp", "P", [128, 4, 516], F32)
        CVG = cv("hp", "P", [128, 4, 512], F32)
        bump[("arena", "P")] = WTB_OFF
        HTG2 = cv("arena", "P", [128, 8, 512])
        THG = cv("arena", "P", [128, 4, 512], F32)
        QKG = [cv("hp", "P", [128, 4, 512]), cv("arena", "P", [128, 4, 512])]
        VEG = [cv("hp", "P", [128, 16, 129]), cv("arena", "P", [128, 16, 129])]
        GVG = [cv("hp", "P", [128, 4, 512]), cv("arena", "P", [128, 4, 512])]
        KHG = [cv("hp", "P", [128, 2, 512]), cv("arena", "P", [128, 2, 512])]
        E2G = cv("hp", "P", [128, 2, 512], F32)
        BCG = cv("hp", "P", [128, 2, 512], F32)
        ZG = cv("hp", "P", [4, 2, 512], F32)
        EIG = ZG[:, 0, :]
        EFG = cv("hp", "P", [4, 512], F32)
        GLRG = cv("hp", "P", [16, 512], F32)
        RLG = sb("RLG", [4, 4])
        D4G = sb("D4G", [4, 4, 4])
        SUFG = sb("SUFG", [128, 2, 4])
        HTGs = [HTG1, HTG2]
        SCG = [sb(f"SCG{i}", [128, 4, 16]) for i in range(2)]
        ACLG = [sb(f"ACLG{i}", [128, 4, 2]) for i in range(2)]
    SCb = [sb(f"SCb{i}", [128, 16]) for i in range(2)]
    ACL = [sb(f"ACL{i}", [128, 2]) for i in range(2)]
    NGT = sb("NGT", [128, 8])
    XS_B = cv("s2", "B", [128, D], F32)
    HB_B = cv("s2", "B", [128, D])
    HTt_B = cv("s2", "B", [128, 8, 128])
    QT = cv("s2", "B", [128, 8, 128])
    PEX = cv("s2", "B", [128, 4, NMEM])
    PTb = [cv("s2", "B", [128, 8, 128]) for _ in range(2)]
    OB = cv("s2", "B", [128, D])
    XBB = [cv("s2", "B", [128, D], F32)]
    HID = [cv("s2", "C", [128, 8, 512]), ]
    cur = {"XS": None, "HB": HB_A}

    PS = [es.enter_context(nc.psum_tensor(f"ps{i}", [128, 512], F32)) for i in range(8)]
    PSb = [p.bitcast(BF16) for p in PS]
    B_T, B_F0, B_F1, B_K0, B_K1, B_G, B_M0, B_M1 = range(8)

    sems = {}

    def getsem(sk):
        if sk not in sems:
            sems[sk] = es.enter_context(nc.semaphore("s_" + "_".join(str(x) for x in sk)))
        return sems[sk]

    def wload(dst, src_rows_ap, ncols_chunks, key, eng="pool"):
        for (d0, s0, n) in ncols_chunks:
            S.dma(eng,
                  lambda e, d0=d0, s0=s0, n=n: e.dma_start(
                      out=dst[:, :, d0:d0 + n],
                      in_=src_rows_ap.rearrange("(kc p) n -> p kc n", p=128)[:, :, s0:s0 + n]),
                  reads=(), writes=(key,))

    def bcast_load(dst, src1d, n, key):
        S.dma("sp", lambda e: e.dma_start(out=dst, in_=src1d[0:n].partition_broadcast(128)),
              reads=(), writes=(key,))

    def layer_norm(src, dst, keys_r, keys_w, xs=None, xskey="XS", lng=None, lnb=None, gkey="LNG", bkey="LNB",
                   sc=None, sfx=""):
        XS = cur["XS"] if xs is None else xs
        G_ = LNG if lng is None else lng
        B_ = LNB if lnb is None else lnb
        st6, mv, rs = (ST6, MV, RS) if sc is None else sc
        k6, kmv, kr0, kr1 = "ST6" + sfx, "MV" + sfx, "RS0" + sfx, "RS1" + sfx
        S.op("dve", lambda e: (e.bn_stats(out=st6[:, 0, :], in_=src[:, 0:512]),
                               e.bn_stats(out=st6[:, 1, :], in_=src[:, 512:1024]))[-1],
             reads=keys_r, writes=(k6,))
        S.op("dve", lambda e: e.bn_aggr(out=mv[:, :], in_=st6[:, :, :].rearrange("p a b -> p (a b)")),
             reads=(k6,), writes=(kmv,))
        S.op("dve", lambda e: e.tensor_scalar(out=rs[:, 0:1], in0=mv[:, 1:2], scalar1=EPS, scalar2=None,
                                              op0=ALU.add), reads=(kmv,), writes=(kr0,))
        S.op("pool", lambda e: e.tensor_tensor(out=rs[:, 1:2], in0=rs[:, 0:1], in1=negh[:, 0:1], op=ALU.pow),
             reads=(kr0,), writes=(kr1,))
        S.op("dve", lambda e: e.scalar_tensor_tensor(out=XS[:, :], in0=src, scalar=mv[:, 0:1], in1=G_[:, :],
                                                     op0=ALU.subtract, op1=ALU.mult),
             reads=tuple(keys_r) + (kmv, gkey), writes=(xskey,))
        S.op("dve", lambda e: e.scalar_tensor_tensor(out=dst, in0=XS[:, :], scalar=rs[:, 1:2], in1=B_[:, :],
                                                     op0=ALU.mult, op1=ALU.add),
             reads=(xskey, kr1, bkey), writes=keys_w)

    def transpose8(src_bf, src_keys, dst_ap, dst_keys, bank=0, bkey="PS_T"):
        def f(e):
            last = None
            for kc in range(8):
                last = e.transpose(out=PSb[bank][:, kc * 128:(kc + 1) * 128],
                                   in_=src_bf[:, kc * 128:(kc + 1) * 128], identity=ident_bf[:, :])
            return last
        S.op("pe", f, reads=tuple(src_keys), writes=(bkey,))
        S.op("act", lambda e: e.activation(out=dst_ap,
                                           in_=PSb[bank][:, :].rearrange("p (a b) -> p a b", a=8),
                                           func=AF.Copy),
             reads=(bkey,), writes=dst_keys)

    def to_feature_major(src_f32, src_keys, dst_ap, dst_keys, bank=0, bkey="PS_T", hb=None, hbkey="HB"):
        HB = cur["HB"] if hb is None else hb
        S.op("act", lambda e: e.activation(out=HB[:, :], in_=src_f32, func=AF.Copy),
             reads=src_keys, writes=(hbkey,))
        transpose8(HB, (hbkey,), dst_ap, dst_keys, bank, bkey)

    def proj_resid(ti, src_ht, src_keys, W, wkey):
        for half in range(2):
            bank = B_K0 if half == 0 else B_K1
            bkey = "K0" if half == 0 else "K1"
            def f(e, half=half, bank=bank):
                last = None
                for kc in range(8):
                    last = e.matmul(PS[bank][:, 0:512], lhsT=src_ht[:, kc, :],
                                    rhs=W[:, kc, half * 512:(half + 1) * 512], start=(kc == 0), stop=(kc == 7))
                return last
            S.op("pe", f, reads=(wkey,) + tuple(src_keys), writes=(bkey,))
            S.op("dve", lambda e, half=half, bank=bank: e.scalar_tensor_tensor(
                out=H[:, ti, half * 512:(half + 1) * 512], in0=H[:, ti, half * 512:(half + 1) * 512], scalar=ALPHA,
                in1=PS[bank][:, 0:512], op0=ALU.mult, op1=ALU.add), reads=(bkey, f"H{ti}"), writes=(f"H{ti}",))

    def barrier():
        allk = list(S.lastw.keys())
        for eng in ENGS:
            S.fence(eng, allk)

    def finish():
        for e in ENGS:
            for (fn, waits, sk, inc) in S.ops[e]:
                for (wk, val) in waits:
                    getsem(wk)
                if sk is not None:
                    getsem(sk)
        block = es.enter_context(nc.Block())

        class _FirstWait:
            def __init__(self, eng, sem, val):
                self._e, self._s, self._v, self._done = eng, sem, val, False

            def __getattr__(self, name):
                attr = getattr(self._e, name)
                if not callable(attr):
                    return attr

                def f(*a, **k):
                    ins = attr(*a, **k)
                    if not self._done:
                        ins._wait_ge(self._s, self._v)
                        self._done = True
                    return ins
                return f

        EMBED = os.environ.get("EMBED", "1") == "1"

        def emit(engname):
            def body(eng):
                for (fn, waits, sk, inc) in S.ops[engname]:
                    ws = list(waits)
                    emb = None
                    if EMBED and fn is not None and ws and engname in ("act", "dve", "pe", "pool") and sk[0] == "e":
                        emb = ws.pop()
                    for (wk, val) in ws:
                        eng.wait_ge(sems[wk], val)
                    if fn is not None:
                        if emb is not None:
                            ins = fn(_FirstWait(eng, sems[emb[0]], emb[1]))
                        else:
                            ins = fn(eng)
                        ins.then_inc(sems[sk], inc)
            return body

        block.tensor(emit("pe"))
        block.scalar(emit("act"))
        block.vector(emit("dve"))
        block.gpsimd(emit("pool"))
        block.sync(emit("sp"))
        es.close()
        return nc

    def dump_H_and_finish():
        for ti in range(NT_MAIN):
            S.dma("sp", lambda e, ti=ti: e.dma_start(out=out_d[ti * 128:(ti + 1) * 128, :], in_=H[:, ti, :]),
                  reads=(f"H{ti}",), writes=(f"OUT{ti}",))
        S.fence("sp", [f"OUT{t}" for t in range(NT_MAIN)])
        return finish()

    S.op("pool", lambda e: e.memset(ones_f[:, :], 1.0), writes=("c_ones",))
    S.op("pool", lambda e: e.memset(zeros_f[:, :], 0.0), writes=("c_zeros",))
    S.op("pool", lambda e: e.memset(negh[:, :], -0.5), writes=("c_negh",))
    S.op("pool", lambda e: e.affine_select(out=ident_f[:, :], in_=ones_f[:, :], pattern=[[-1, 128]],
                                           compare_op=ALU.is_equal, fill=0.0, base=0, channel_multiplier=1),
         reads=("c_ones",), writes=("c_identf",))
    S.op("pool", lambda e: e.tensor_copy(out=ident_bf[:, :], in_=ident_f[:, :]),
         reads=("c_identf",), writes=("c_identb",))
    S.op("pool", lambda e: e.affine_select(out=TT[:, :], in_=ones_f[:, :], pattern=[[1, 128]],
                                           compare_op=ALU.is_ge, fill=0.0, base=0, channel_multiplier=-1),
         reads=("c_ones",), writes=("TT",))
    S.op("pool", lambda e: e.tensor_copy(out=mask_ut[:, :], in_=TT[:, :]), reads=("TT",), writes=("c_mask",))
    S.op("pool", lambda e: e.memset(VEb[0][:, :, :], 1.0), writes=("VE0",))
    S.op("pool", lambda e: e.memset(VEb[1][:, :, :], 1.0), writes=("VE1",))
    S.op("pool", lambda e: e.memset(Cf[:, :, :], 0.0), writes=("Cf0", "Cf1", "Cf2", "Cf3"))
    S.op("pool", lambda e: e.memset(Cb[:, :, :], 0.0), writes=("Cb0", "Cb1", "Cb2", "Cb3"))
    S.op("pool", lambda e: e.memset(Sf[:, :, :], 0.0), writes=("Sf0", "Sf1"))
    S.op("pool", lambda e: e.memset(Sb_[:, :, :], 0.0), writes=("Sb0", "Sb1"))
    S.op("pool", lambda e: e.memset(RAW[:, :, :], 0.0), writes=("RAWq", "RAWk"))

    bcast_load(LNG[:, :], ln_g["ln_in"], D, "LNG")
    bcast_load(LNB[:, :], ln_b["ln_in"], D, "LNB")
    S.dma("sp", lambda e: e.dma_start(out=NGT[:, 0:4], in_=m_norm_g.rearrange("(c p) -> p c", p=128),
                                      allow_slow_non_contiguous=True), writes=("NGTa",))
    S.dma("sp", lambda e: e.dma_start(out=NGT[:, 4:8], in_=g_norm_g.rearrange("(c p) -> p c", p=128),
                                      allow_slow_non_contiguous=True), writes=("NGTb",))
    S.op("dve", lambda e: e.tensor_scalar(out=NGT[:, :], in0=NGT[:, :], scalar1=0.5, scalar2=None, op0=ALU.mult),
         reads=("NGTa", "NGTb"), writes=("NGT",))
    S.dma("sp", lambda e: e.dma_start(out=PM[:, :], in_=pmask_d[:, :]), writes=("PM",))
    with nc.allow_non_contiguous_dma(reason="tiny param loads"):
        for j in range(4):
            S.dma("sp", lambda e, j=j: e.dma_start(out=CW[:, j, :], in_=conv_w[j, :].rearrange("(c p) -> p c", p=128),
                                                   allow_slow_non_contiguous=True), writes=("CW",))
        S.dma("sp", lambda e: e.dma_start(out=CB[:, :], in_=conv_b.rearrange("(c p) -> p c", p=128), allow_slow_non_contiguous=True),
              writes=("CB",))
        S.dma("sp", lambda e: e.dma_start(out=GB[:, 0:1], in_=m_i_bias.rearrange("(p o) -> p o", o=1), allow_slow_non_contiguous=True),
              writes=("GBa",))
        S.dma("sp", lambda e: e.dma_start(out=GB[:, 1:2], in_=m_f_bias.rearrange("(p o) -> p o", o=1), allow_slow_non_contiguous=True),
              writes=("GBb",))
        S.dma("sp", lambda e: e.dma_start(out=GLB[:, :], in_=g_lr_b.rearrange("(c p) -> p c", p=128), allow_slow_non_contiguous=True),
              writes=("GLBa",))
    S.dma("sp", lambda e: e.dma_start(out=GLW[:, :], in_=g_lr_w[:, :]), writes=("GLW",))
    S.op("dve", lambda e: e.tensor_scalar(out=GB[:, 0:1], in0=GB[:, 0:1],
                                          scalar1=math.log(128.0 ** -0.5 / 4.0), scalar2=None, op0=ALU.add),
         reads=("GBa",), writes=("GB0",))
    S.op("dve", lambda e: e.tensor_scalar(out=GB[:, 1:2], in0=GB[:, 1:2], scalar1=-1.0, scalar2=None,
                                          op0=ALU.mult), reads=("GBb",), writes=("GB1",))
    S.op("dve", lambda e: e.tensor_scalar(out=GLB[:, :], in0=GLB[:, :], scalar1=-1.0, scalar2=None,
                                          op0=ALU.mult), reads=("GLBa",), writes=("GLB",))

    wload(Wf, w_in, [(512, 512, 512), (1024, 2056, 512), (0, 0, 512)], "Wf")
    wload(Ws, w_in, [(0, 2048, 8), (8, 3592, 16)], "Ws")
    wload(WtA, w_in, [(0, 1024, 512), (512, 2568, 512)], "Wt")

    def load_main_weights():
        wload(WtB, w_in, [(0, 1536, 512), (512, 3080, 512)], "WtB")
        wload(Wo_, w_out, [(0, 0, 1024)], "Wo")
    if NG4 == 0:
        load_main_weights()
    pass

    htt = HTt[0]
    hk = "HTt0"

    def ln_A(src, dst, keys_r, keys_w):
        S.op("dve", lambda e: (e.bn_stats(out=ST6[:, 0, :], in_=src[:, 0:512]),
                               e.bn_stats(out=ST6[:, 1, :], in_=src[:, 512:1024]))[-1],
             reads=keys_r, writes=("ST6",))
        S.op("dve", lambda e: e.bn_aggr(out=MV[:, :], in_=ST6[:, :, :].rearrange("p a b -> p (a b)")),
             reads=("ST6",), writes=("MV",))
        S.op("dve", lambda e: e.tensor_scalar(out=RS[:, 0:1], in0=MV[:, 1:2], scalar1=EPS, scalar2=None,
                                              op0=ALU.add), reads=("MV",), writes=("RS0",))
        S.op("pool", lambda e: e.tensor_tensor(out=RS[:, 1:2], in0=RS[:, 0:1], in1=negh[:, 0:1], op=ALU.pow),
             reads=("RS0",), writes=("RS1",))
        S.op("dve", lambda e: e.scalar_tensor_tensor(out=XS_A[:, :], in0=src, scalar=MV[:, 0:1], in1=LNG[:, :],
                                                     op0=ALU.subtract, op1=ALU.mult),
             reads=tuple(keys_r) + ("MV", "LNG"), writes=("XS",))
        S.op("dve", lambda e: e.scalar_tensor_tensor(out=dst, in0=XS_A[:, :], scalar=RS[:, 1:2], in1=LNB[:, :],
                                                     op0=ALU.mult, op1=ALU.add),
             reads=("XS", "RS1", "LNB"), writes=keys_w)

    def stageX(i):
        main = i >= NT_PRE
        ti = i - NT_PRE
        need_q = main or (i == NT_PRE - 1)
        par = i % 2
        xb = XB[0]
        xk = "X0"
        pm = PM[:, i:i + 1]
        QK, VE, GV, KH, QH, SC, OG = QKb[par], VEb[par], GVb[par], KHb[par], QHb[par], SCb[par], OGb[par]
        S.dma("sp", lambda e: e.dma_start(out=xb[:, :], in_=xall[i * 128:(i + 1) * 128, :]), writes=(xk,))
        if main:
            dst = H[:, ti, :]
            dk_ = (f"H{ti}",)
        else:
            dst = xb[:, :]
            dk_ = (xk,)
        ln_A(xb[:, :], dst, (xk,), dk_)
        to_feature_major(dst, dk_, htt[:, :, :], (hk,))

        def proj_feat(bank, col0, nchunk, key):
            def f(e):
                last = None
                for c in range(nchunk):
                    for kc in range(8):
                        last = e.matmul(PS[bank][:, c * 128:(c + 1) * 128],
                                        lhsT=Wf[:, kc, col0 + c * 128: col0 + (c + 1) * 128],
                                        rhs=htt[:, kc, :], start=(kc == 0), stop=(kc == 7))
                return last
            S.op("pe", f, reads=(hk, "Wf"), writes=(key,))

        def proj_tok(bank, g, key):
            W_, c_, wk_ = {0: (WtA, 0, "Wt"), 1: (WtB, 0, "WtB"), 2: (WtA, 512, "Wt"), 3: (WtB, 512, "WtB")}[g]

            def f(e):
                last = None
                for kc in range(8):
                    last = e.matmul(PS[bank][:, 0:512], lhsT=htt[:, kc, :],
                                    rhs=W_[:, kc, c_:c_ + 512], start=(kc == 0), stop=(kc == 7))
                return last
            S.op("pe", f, reads=(hk, wk_), writes=(key,))

        def f_gates(e):
            last = None
            for (o0, n, c0) in ((0, 4, 0), (128, 4, 4)):
                for kc in range(8):
                    last = e.matmul(PS[B_G][0:n, o0:o0 + 128], lhsT=Ws[:, kc, c0:c0 + n], rhs=htt[:, kc, :],
                                    start=(kc == 0), stop=(kc == 7))
            for kc in range(8):
                last = e.matmul(PS[B_G][0:16, 256:384], lhsT=Ws[:, kc, 8:24], rhs=htt[:, kc, :],
                                start=(kc == 0), stop=(kc == 7))
            return last
        S.op("pe", f_gates, reads=(hk, "Ws"), writes=("G_a", "G_b"))
        if need_q:
            proj_feat(B_F0, 0, 4, "F0")
        proj_feat(B_F1, 512, 4, "F1")
        proj_tok(B_K0, 0, "K0")

        S.op("act", lambda e: e.activation(out=EI[:, :], in_=PS[B_G][0:4, 0:128], func=AF.Exp,
                                           bias=GB[:, 0:1], scale=1.0),
             reads=("G_a", "GB0"), writes=("EI",))
        S.op("act", lambda e: e.activation(out=EF[:, :], in_=PS[B_G][0:4, 128:256], func=AF.Exp,
                                           bias=GB[:, 1:2], scale=-1.0),
             reads=("G_a", "GB1"), writes=("EF",))
        S.op("act", lambda e: e.activation(out=GLR[:, :], in_=PS[B_G][0:16, 256:384], func=AF.Copy),
             reads=("G_b",), writes=("GLR",))
        S.op("dve", lambda e: e.tensor_scalar(out=EF[:, :], in0=EF[:, :], scalar1=1.0, scalar2=None, op0=ALU.add),
             reads=("EF",), writes=("EF",))
        S.op("dve", lambda e: e.tensor_tensor_scan(out=Z[:, 128:256], data0=EF[:, :], data1=zeros_f[0:4, :],
                                                   initial=1.0, op0=ALU.mult, op1=ALU.add),
             reads=("EF",), writes=("Zb",))
        S.op("dve", lambda e: e.tensor_tensor(out=Z[:, 0:128], in0=EI[:, :], in1=Z[:, 128:256], op=ALU.mult),
             reads=("EI", "Zb"), writes=("Za",))
        S.op("dve", lambda e: e.reciprocal(out=RL[:, :], in_=Z[:, 255:256]), reads=("Zb",), writes=("RL",))
        S.op("dve", lambda e: e.tensor_scalar(out=D4[:, :], in0=ident_f[0:4, 0:4], scalar1=RL[:, 0:1],
                                              scalar2=None, op0=ALU.mult), reads=("RL",), writes=("D4",))

        def f_sc(e):
            e.matmul(PS[B_G][:, 384:388], lhsT=Z[0:4, 0:128], rhs=ident_f[0:4, 0:4], start=True, stop=True)
            e.matmul(PS[B_G][:, 388:392], lhsT=Z[0:4, 128:256], rhs=ident_f[0:4, 0:4], start=True, stop=True)
            return e.matmul(PS[B_G][:, 392:396], lhsT=ones_f[0:4, :], rhs=D4[0:4, 0:4], start=True, stop=True)
        S.op("pe", f_sc, reads=("Za", "Zb", "D4"), writes=("G_c",))
        S.op("dve", lambda e: e.tensor_copy(out=SC[:, 0:12], in_=PS[B_G][:, 384:396]), reads=("G_c",),
             writes=(f"SC{par}",))
        S.op("dve", lambda e: e.tensor_scalar(out=SC[:, 0:4], in0=SC[:, 0:4], scalar1=pm, scalar2=None,
                                              op0=ALU.mult), reads=(f"SC{par}", "PM"), writes=(f"SC{par}",))
        S.op("dve", lambda e: e.tensor_tensor(out=SC[:, 12:16], in0=SC[:, 0:4], in1=SC[:, 8:12], op=ALU.mult),
             reads=(f"SC{par}",), writes=(f"SC{par}",))

        def f_z(e):
            e.matmul(PS[B_G][:, 0:128], lhsT=GLW[0:16, 0:128], rhs=GLR[0:16, :], start=True, stop=True)
            return e.matmul(PS[B_G][:, 128:256], lhsT=GLW[0:16, 128:256], rhs=GLR[0:16, :], start=True, stop=True)
        S.op("pe", f_z, reads=("GLR", "GLW"), writes=("G_a",))
        S.op("act", lambda e: (e.activation(out=E2[:, 0, :], in_=PS[B_G][:, 0:128], func=AF.Exp,
                                            bias=GLB[:, 0:1], scale=-1.0),
                               e.activation(out=E2[:, 1, :], in_=PS[B_G][:, 128:256], func=AF.Exp,
                                            bias=GLB[:, 1:2], scale=-1.0))[-1],
             reads=("G_a", "GLB"), writes=("E2",))
        S.op("act", lambda e: e.activation(out=E2[:, :, :], in_=E2[:, :, :], func=AF.Ln, bias=1.0, scale=1.0),
             reads=("E2",), writes=("E2",))
        S.op("dve", lambda e: e.tensor_tensor_scan(out=AI[:, 0, :], data0=ones_f[:, :], data1=E2[:, 0, :],
                                                   initial=0.0, op0=ALU.mult, op1=ALU.add),
             reads=("E2",), writes=("BC0",))
        S.op("dve", lambda e: e.tensor_tensor_scan(out=AI[:, 1, :], data0=ones_f[:, :], data1=E2[:, 1, :],
                                                   initial=0.0, op0=ALU.mult, op1=ALU.add),
             reads=("E2",), writes=("BC1",))
        S.op("act", lambda e: e.activation(out=AC[:, :, :], in_=AI[:, :, :], func=AF.Exp, scale=-1.0 / 16.0),
             reads=("BC0", "BC1"), writes=("AC",))
        S.op("act", lambda e: e.activation(out=AI[:, :, :], in_=AI[:, :, :], func=AF.Exp, scale=1.0 / 16.0),
             reads=("BC0", "BC1", "AC"), writes=("AI", "BC0", "BC1"))
        S.op("act", lambda e: e.activation(out=ACL[par][:, :], in_=AC[:, :, 127], func=AF.Copy),
             reads=("AC",), writes=(f"ACL{par}",))

        groups = []
        if need_q:
            groups.append((0, B_F0, "F0", "RAWq"))
        groups.append((4, B_F1, "F1", "RAWk"))
        for (c0, bank, bkey, rk) in groups:
            S.op("act", lambda e, c0=c0, bank=bank: e.activation(
                out=RAW[:, c0:c0 + 4, 3:131], in_=PS[bank][:, :].rearrange("p (a b) -> p a b", a=4),
                func=AF.Identity, scale=pm), reads=(bkey, "PM"), writes=(rk,))
            if c0 == 0:
                proj_feat(B_F0, 1024, 4, "F0")
            else:
                if main:
                    proj_tok(B_F1, 1, "F1")
            if main or c0 == 4:
                for cc in range(4):
                    c = c0 + cc
                    S.op("dve", lambda e, c=c, cc=cc: e.tensor_scalar(out=CV[:, cc, :], in0=RAW[:, c, 3:131],
                                                                      scalar1=CW[:, 3, c:c + 1], scalar2=CB[:, c:c + 1],
                                                                      op0=ALU.mult, op1=ALU.add),
                         reads=(rk, "CW", "CB"), writes=(f"CV{cc}",))
                for j in range(3):
                    for cc in range(4):
                        c = c0 + cc
                        S.op("dve", lambda e, c=c, cc=cc, j=j: e.scalar_tensor_tensor(
                            out=CV[:, cc, :], in0=RAW[:, c, j:j + 128], scalar=CW[:, j, c:c + 1], in1=CV[:, cc, :],
                            op0=ALU.mult, op1=ALU.add), reads=(rk, f"CV{cc}"), writes=(f"CV{cc}",))
                cvk = tuple(f"CV{cc}" for cc in range(4))
                S.op("act", lambda e: e.activation(out=TH[:, :, :], in_=CV[:, :, :], func=AF.Tanh, scale=0.5),
                     reads=cvk, writes=("OGT",))
                S.op("dve", lambda e, c0=c0: e.scalar_tensor_tensor(
                    out=QK[:, c0:c0 + 4, :], in0=TH[:, :, :], scalar=1.0, in1=CV[:, :, :],
                    op0=ALU.add, op1=ALU.mult), reads=("OGT",) + cvk,
                    writes=(f"QKq{par}" if c0 == 0 else f"QKk{par}",))
            S.op("pool", lambda e, c0=c0: e.tensor_copy(out=RAW[:, c0:c0 + 4, 0:3], in_=RAW[:, c0:c0 + 4, 128:131]),
                 reads=(rk,), writes=(rk,))
        if not need_q:
            proj_feat(B_F0, 1024, 4, "F0")

        if main:
            S.op("dve", lambda e: e.scalar_tensor_tensor(
                out=QH[:, :, :], in0=PS[B_F0][:, 0:256].rearrange("p (a b) -> p a b", a=2), scalar=0.125,
                in1=AC[:, :, :], op0=ALU.mult, op1=ALU.mult), reads=("F0", "AC"), writes=(f"QH{par}",))
        S.op("dve", lambda e: e.tensor_tensor(out=KH[:, :, :],
                                              in0=PS[B_F0][:, 256:512].rearrange("p (a b) -> p a b", a=2),
                                              in1=AI[:, :, :], op=ALU.mult), reads=("F0", "AI"), writes=(f"KH{par}",))

        S.op("act", lambda e: e.activation(out=VE[:, :, 0:128],
                                           in_=PS[B_K0][:, :].rearrange("p (a b) -> p a b", a=4), func=AF.Copy),
             reads=("K0",), writes=(f"VE{par}",))
        proj_tok(B_K0, 2, "K0")
        if main:
            S.op("act", lambda e: e.activation(out=OGT[:, :], in_=PS[B_F1][:, :], func=AF.Tanh, scale=0.5),
                 reads=("F1",), writes=("OGT",))
            S.op("dve", lambda e: e.tensor_scalar(out=OG[:, 0:512], in0=OGT[:, :], scalar1=1.0, scalar2=None,
                                                  op0=ALU.add), reads=("OGT",), writes=(f"OGa{par}",))
            proj_tok(B_F1, 3, "F1")
        S.op("act", lambda e: e.activation(out=GV[:, :], in_=PS[B_K0][:, :], func=AF.Copy),
             reads=("K0",), writes=(f"GV{par}",))
        if main:
            S.op("act", lambda e: e.activation(out=OGT[:, :], in_=PS[B_F1][:, :], func=AF.Tanh, scale=0.5),
                 reads=("F1",), writes=("OGT",))
            S.op("dve", lambda e: e.scalar_tensor_tensor(out=OG[:, 512:1024], in0=OGT[:, :], scalar=1.0,
                                                         in1=PS[B_F1][:, :], op0=ALU.add, op1=ALU.mult),
                 reads=("OGT", "F1"), writes=(f"OGb{par}",))

    def stageY(i, grp=None):
        main = i >= NT_PRE
        ti = i - NT_PRE
        par = i % 2
        pm = PM[:, i:i + 1]
        if grp is None:
            QK, VE, GV, KH, QH, SC, OG = QKb[par], VEb[par], GVb[par], KHb[par], QHb[par], SCb[par], OGb[par]
            kq, kk, kve, kgv, kkh, kqh, ksc = (f"QKq{par}", f"QKk{par}", f"VE{par}", f"GV{par}", f"KH{par}",
                                              f"QH{par}", f"SC{par}")
            kT = lambda h: QK[:, 4 + h, :]
            qT = lambda h: QK[:, h, :]
            VEh = lambda h: VE[:, h, :]
            GVh = lambda h: GV[:, h * 128:(h + 1) * 128]
            KHp = lambda p, part=slice(0, 128): KH[part, p, :]
            QHp = lambda p, part=slice(0, 128): QH[part, p, :]
            SCc = lambda j: SC[:, j:j + 1]
            ACLp = lambda p: ACL[par][:, p:p + 1]
            kacl = f"ACL{par}"
        else:
            t = grp
            gp = (i // 4) % 2
            tk = slice(t * 128, (t + 1) * 128)
            kq = kk = f"QKG{gp}"
            kve, kgv, kkh, kqh, ksc, kacl = f"VEG{gp}", f"GVG{gp}", f"KHG{gp}", f"KHG{gp}", f"SCG{gp}", f"ACLG{gp}"
            kT = lambda h: QKG[gp][:, h, tk]
            qT = None
            VEh = lambda h: VEG[gp][:, t * 4 + h, :]
            GVh = lambda h: GVG[gp][:, t, h * 128:(h + 1) * 128]
            KHp = lambda p, part=slice(0, 128): KHG[gp][part, p, tk]
            QHp = None
            SCc = lambda j: SCG[gp][:, t, j:j + 1]
            ACLp = lambda p: ACLG[gp][:, t, p:p + 1]
            OG = None
        if grp is not None:
            SCrow = lambda a, b: SCG[gp][:, t, a:b]

            def f_T(e):
                last = None
                for h in range(4):
                    e.transpose(out=PSb[B_M0][:, h * 128:(h + 1) * 128], in_=kT(h), identity=ident_bf[:, :])
                for p in range(2):
                    last = e.transpose(out=PSb[B_M0][:, 512 + p * 128:512 + (p + 1) * 128], in_=KHp(p),
                                       identity=ident_bf[:, :])
                return last
            S.op("pe", f_T, reads=(kk, kkh), writes=("M0T",))
            S.op("dve", lambda e: e.tensor_tensor(
                out=KW[:, :, :], in0=PSb[B_M0][:, 0:512].rearrange("p (a b) -> p a b", a=4),
                in1=SCrow(12, 16).rearrange("p (h o) -> p h o", o=1).broadcast_to([128, 4, 128]), op=ALU.mult),
                reads=("M0T", ksc), writes=("KW0", "KW1", "KW2", "KW3"))
            S.op("dve", lambda e: e.tensor_scalar(
                out=KHt[:, :, :], in0=PSb[B_M0][:, 512:768].rearrange("p (a b) -> p a b", a=2),
                scalar1=pm, scalar2=None, op0=ALU.mult), reads=("M0T", "PM"), writes=("KHt0", "KHt1"))

            def f_U(e):
                first = (t == 0)
                lastt = (t == 3)
                for h in range(3):
                    e.matmul(PS[B_M1][:, h * 129:(h + 1) * 129], lhsT=KW[:, h, :], rhs=VEh(h),
                             start=(first and h == 0), stop=lastt, skip_group_check=True)
                e.matmul(PS[B_K1][:, 0:129], lhsT=KW[:, 3, :], rhs=VEh(3), start=first, stop=lastt,
                         skip_group_check=True)
                last = None
                for p in range(2):
                    for hh in range(2):
                        part = slice(hh * 64, (hh + 1) * 64)
                        last = e.matmul(PS[B_K1][part, 129 + p * 128:129 + (p + 1) * 128],
                                        lhsT=KHt[:, p, hh * 64:(hh + 1) * 64], rhs=GVh(2 * p + hh),
                                        start=False, stop=lastt, skip_group_check=True)
                return last
            S.op("pe", f_U, reads=("KW0", "KW1", "KW2", "KW3", "KHt0", "KHt1", kve, kgv),
                 writes=("M1U", "K1"))
            if t == 3:
                for h in range(4):
                    U_ = PS[B_K1][:, 0:129] if h == 3 else PS[B_M1][:, h * 129:(h + 1) * 129]
                    S.op("dve", lambda e, h=h, U_=U_: e.scalar_tensor_tensor(
                        out=Cf[:, h, :], in0=Cf[:, h, :], scalar=SCG[gp][:, 0, 8 + h:9 + h], in1=U_,
                        op0=ALU.mult, op1=ALU.add),
                        reads=("K1" if h == 3 else "M1U", ksc, f"Cf{h}"), writes=(f"Cf{h}",))
                S.op("dve", lambda e: e.tensor_tensor(
                    out=Sf[:, :, :], in0=Sf[:, :, :],
                    in1=ACLG[gp][:, 0, :].rearrange("p (h o) -> p h o", o=1).broadcast_to([128, 2, 128]),
                    op=ALU.mult), reads=(kacl, "Sf0", "Sf1"), writes=("Sf0", "Sf1"))
                S.op("dve", lambda e: e.tensor_tensor(
                    out=Sf[:, :, :], in0=Sf[:, :, :],
                    in1=PS[B_K1][:, 129:385].rearrange("p (a b) -> p a b", a=2), op=ALU.add),
                    reads=("K1", "Sf0", "Sf1"), writes=("Sf0", "Sf1"))
                if i == 4 * NG4 - 1:
                    S.op("act", lambda e: e.activation(out=Cb[:, :, :], in_=Cf[:, :, :], func=AF.Copy),
                         reads=("Cf0", "Cf1", "Cf2", "Cf3"), writes=("Cb0", "Cb1", "Cb2", "Cb3"))
                    S.op("act", lambda e: e.activation(out=Sb_[:, :, :], in_=Sf[:, :, :], func=AF.Copy),
                         reads=("Sf0", "Sf1"), writes=("Sb0", "Sb1"))
            return

        if not main:
            if grp is None:
                SCrow = lambda a, b: SC[:, a:b]
                ACLrow = ACL[par][:, 0:2]
            else:
                SCrow = lambda a, b: SCG[gp][:, t, a:b]
                ACLrow = ACLG[gp][:, t, :]

            def f_T(e):
                last = None
                for h in range(4):
                    e.transpose(out=PSb[B_M0][:, h * 128:(h + 1) * 128], in_=kT(h), identity=ident_bf[:, :])
                for p in range(2):
                    last = e.transpose(out=PSb[B_K1][:, p * 128:(p + 1) * 128], in_=KHp(p), identity=ident_bf[:, :])
                return last
            S.op("pe", f_T, reads=(kk, kkh), writes=("M0T", "K1"))
            S.op("dve", lambda e: e.tensor_tensor(
                out=KW[:, :, :], in0=PSb[B_M0][:, 0:512].rearrange("p (a b) -> p a b", a=4),
                in1=SCrow(12, 16).rearrange("p (h o) -> p h o", o=1).broadcast_to([128, 4, 128]), op=ALU.mult),
                reads=("M0T", ksc), writes=("KW0", "KW1", "KW2", "KW3"))
            S.op("dve", lambda e: e.tensor_scalar(
                out=KHt[:, :, :], in0=PSb[B_K1][:, 0:256].rearrange("p (a b) -> p a b", a=2),
                scalar1=pm, scalar2=None, op0=ALU.mult), reads=("K1", "PM"), writes=("KHt0", "KHt1"))

            def f_U(e):
                e.matmul(PS[B_M0][:, 256:385], lhsT=KW[:, 0, :], rhs=VEh(0), start=True, stop=True)
                for h in range(1, 4):
                    e.matmul(PS[B_M1][:, (h - 1) * 129:h * 129], lhsT=KW[:, h, :], rhs=VEh(h), start=True, stop=True)
                last = None
                for p in range(2):
                    for hh in range(2):
                        part = slice(hh * 64, (hh + 1) * 64)
                        last = e.matmul(PS[B_K1][part, 128 + p * 128:128 + (p + 1) * 128],
                                        lhsT=KHt[:, p, hh * 64:(hh + 1) * 64], rhs=GVh(2 * p + hh),
                                        start=True, stop=True)
                return last
            S.op("pe", f_U, reads=("KW0", "KW1", "KW2", "KW3", "KHt0", "KHt1", kve, kgv),
                 writes=("M0U", "M1U", "K1"))
            for h in range(4):
                U_ = PS[B_M0][:, 256:385] if h == 0 else PS[B_M1][:, (h - 1) * 129:h * 129]
                S.op("dve", lambda e, h=h, U_=U_: e.scalar_tensor_tensor(
                    out=Cf[:, h, :], in0=Cf[:, h, :], scalar=SCc(8 + h), in1=U_, op0=ALU.mult, op1=ALU.add),
                    reads=("M0U" if h == 0 else "M1U", ksc, f"Cf{h}"), writes=(f"Cf{h}",))
            S.op("act", lambda e: e.activation(out=Cb[:, :, :], in_=Cf[:, :, :], func=AF.Copy),
                 reads=("Cf0", "Cf1", "Cf2", "Cf3"), writes=("Cb0", "Cb1", "Cb2", "Cb3"))
            S.op("dve", lambda e: e.tensor_tensor(
                out=Sf[:, :, :], in0=Sf[:, :, :], in1=PS[B_K1][:, 128:384].rearrange("p (a b) -> p a b", a=2),
                op=ALU.add), reads=("K1", "Sf0", "Sf1"), writes=("Sf0", "Sf1"))
            S.op("dve", lambda e: e.tensor_tensor(
                out=Sf[:, :, :], in0=Sf[:, :, :],
                in1=ACLrow.rearrange("p (h o) -> p h o", o=1).broadcast_to([128, 2, 128]), op=ALU.mult),
                reads=(kacl, "Sf0", "Sf1"), writes=("Sf0", "Sf1"))
            S.op("act", lambda e: e.activation(out=Sb_[:, :, :], in_=Sf[:, :, :], func=AF.Copy),
                 reads=("Sf0", "Sf1"), writes=("Sb0", "Sb1"))
            return

        assert main
        bc = lambda ap, n, w: ap.rearrange("p (h o) -> p h o", o=1).broadcast_to([128, n, w])
        mask2 = mask_ut[:, :].rearrange("p (o l) -> p o l", o=1).broadcast_to([128, 2, 128])
        for pr in range(2):
            ha, hb = 2 * pr, 2 * pr + 1

            def f1(e, ha=ha, hb=hb):
                e.matmul(PS[B_M0][:, 0:128], lhsT=kT(ha), rhs=qT(ha), start=True, stop=True)
                e.matmul(PS[B_M0][:, 128:256], lhsT=kT(hb), rhs=qT(hb), start=True, stop=True)
                e.transpose(out=PSb[B_M0][:, 512:640], in_=kT(ha), identity=ident_bf[:, :])
                return e.transpose(out=PSb[B_M0][:, 640:768], in_=kT(hb), identity=ident_bf[:, :])
            S.op("pe", f1, reads=(kq, kk), writes=("M0",))
            for j, h in enumerate((ha, hb)):
                S.op("dve", lambda e, j=j, h=h: e.scalar_tensor_tensor(
                    out=SW[j][:, :], in0=PS[B_M0][:, j * 128:(j + 1) * 128], scalar=SCc(h), in1=mask_ut[:, :],
                    op0=ALU.mult, op1=ALU.mult), reads=("M0", ksc), writes=(f"SW{j}",))
            S.op("dve", lambda e, ha=ha: e.tensor_tensor(
                out=KW[:, ha:ha + 2, :], in0=PSb[B_M0][:, 512:768].rearrange("p (a b) -> p a b", a=2),
                in1=bc(SC[:, 12 + ha:14 + ha], 2, 128), op=ALU.mult), reads=("M0", ksc), writes=(f"KW{ha}", f"KW{hb}"))

            def f2(e, ha=ha, hb=hb):
                for j, h in enumerate((ha, hb)):
                    e.matmul(PS[B_M1][:, j * 129:(j + 1) * 129], lhsT=SW[j][:, :], rhs=VEh(h), start=True, stop=False)
                    e.matmul(PS[B_M1][:, j * 129:(j + 1) * 129], lhsT=qT(h), rhs=Cb[:, h, :], start=False, stop=True)
                last = None
                for j, h in enumerate((ha, hb)):
                    last = e.matmul(PS[B_K1][:, j * 129:(j + 1) * 129], lhsT=KW[:, h, :], rhs=VEh(h),
                                    start=True, stop=True)
                return last
            S.op("pe", f2, reads=("SW0", "SW1", kve, kq, f"Cb{ha}", f"Cb{hb}", f"KW{ha}", f"KW{hb}"),
                 writes=("M1", "K1"))
            for j, h in enumerate((ha, hb)):
                S.op("dve", lambda e, j=j, h=h: e.scalar_tensor_tensor(
                    out=Cf[:, h, :], in0=Cf[:, h, :], scalar=SCc(8 + h), in1=PS[B_K1][:, j * 129:(j + 1) * 129],
                    op0=ALU.mult, op1=ALU.add), reads=("K1", ksc, f"Cf{h}"), writes=(f"Cf{h}",))
            S.op("act", lambda e, ha=ha: e.activation(out=Cb[:, ha:ha + 2, :], in_=Cf[:, ha:ha + 2, :], func=AF.Copy),
                 reads=(f"Cf{ha}", f"Cf{hb}"), writes=(f"Cb{ha}", f"Cb{hb}"))
            Pv = PS[B_M1][:, 0:258].rearrange("p (h c) -> p h c", c=129)
            den = Pv[:, :, 128]
            S.op("dve", lambda e, ha=ha: e.tensor_tensor(out=SM[:, 0:2], in0=den, in1=SC[:, 4 + ha:6 + ha], op=ALU.max),
                 reads=("M1", ksc), writes=("SMa",))
            S.op("dve", lambda e: e.scalar_tensor_tensor(out=SM[:, 2:4], in0=den, scalar=-1.0, in1=SM[:, 0:2],
                                                         op0=ALU.mult, op1=ALU.max), reads=("M1", "SMa"), writes=("SMb",))
            S.op("dve", lambda e: e.reciprocal(out=SM[:, 4:6], in_=SM[:, 2:4]), reads=("SMb",), writes=("SMc",))
            for j in range(2):
                S.op("dve", lambda e, j=j: e.bn_stats(out=ST6b2[:, j, :], in_=Pv[:, j, 0:128]), reads=("M1",),
                     writes=(f"ST6b{j}",))
                S.op("dve", lambda e, j=j: e.bn_aggr(out=MVb2[:, j, :], in_=ST6b2[:, j, :]), reads=(f"ST6b{j}",),
                     writes=(f"MVb{j}",))
            S.op("dve", lambda e: e.tensor_tensor(out=SM[:, 6:8], in0=SM[:, 4:6], in1=SM[:, 4:6], op=ALU.mult),
                 reads=("SMc",), writes=("SMd",))
            S.op("dve", lambda e: e.tensor_tensor(out=SM[:, 8:10], in0=MVb2[:, :, 1], in1=SM[:, 6:8], op=ALU.mult),
                 reads=("MVb0", "MVb1", "SMd"), writes=("SMe",))
            S.op("dve", lambda e: e.tensor_scalar(out=SM[:, 8:10], in0=SM[:, 8:10], scalar1=EPS, scalar2=None,
                                                  op0=ALU.add), reads=("SMe",), writes=("SMe",))
            S.op("pool", lambda e: e.tensor_tensor(out=SM[:, 10:12], in0=SM[:, 8:10], in1=negh[:, 0:2], op=ALU.pow),
                 reads=("SMe",), writes=("SMf",))
            S.op("dve", lambda e: e.tensor_tensor(out=SM[:, 12:14], in0=SM[:, 10:12], in1=SM[:, 4:6], op=ALU.mult),
                 reads=("SMf", "SMc"), writes=("SMg",))
            for j in range(2):
                S.op("dve", lambda e, j=j: e.tensor_scalar(out=TT2[:, j, :], in0=Pv[:, j, 0:128], scalar1=MVb2[:, j, 0:1],
                                                           scalar2=SM[:, 12 + j:13 + j], op0=ALU.subtract, op1=ALU.mult),
                     reads=("M1", f"MVb{j}", "SMg"), writes=(f"TT2{j}",))
            S.op("dve", lambda e, ha=ha: e.tensor_tensor(
                out=Y[:, ha * 128:(ha + 2) * 128], in0=TT2[:, :, :].rearrange("p a b -> p (a b)"),
                in1=OG[:, ha * 128:(ha + 2) * 128], op=ALU.mult),
                reads=("TT20", "TT21", f"OGa{par}"), writes=(f"Y{ha}", f"Y{hb}"))

        for p in range(2):
            h0, h1 = 2 * p, 2 * p + 1

            def g1(e, p=p):
                e.matmul(PS[B_M0][:, 0:128], lhsT=KHp(p, slice(0, 64)), rhs=QHp(p, slice(0, 64)), start=True, stop=True)
                e.matmul(PS[B_M1][:, 128:256], lhsT=KHp(p, slice(64, 128)), rhs=QHp(p, slice(64, 128)),
                         start=True, stop=True)
                return e.transpose(out=PSb[B_M0][:, 512:640], in_=KHp(p), identity=ident_bf[:, :])
            S.op("pe", g1, reads=(kkh, kqh), writes=("M0", "M1"))
            S.op("dve", lambda e, h0=h0: e.tensor_tensor(out=ATb[:, h0, :], in0=PS[B_M0][:, 0:128], in1=mask_ut[:, :],
                                                         op=ALU.mult), reads=("M0",), writes=(f"ATb{h0}",))
            S.op("dve", lambda e, h1=h1: e.tensor_tensor(out=ATb[:, h1, :], in0=PS[B_M1][:, 128:256], in1=mask_ut[:, :],
                                                         op=ALU.mult), reads=("M1",), writes=(f"ATb{h1}",))
            S.op("dve", lambda e, p=p: e.tensor_scalar(out=KHt[:, p, :], in0=PSb[B_M0][:, 512:640], scalar1=pm,
                                                       scalar2=None, op0=ALU.mult), reads=("M0", "PM"), writes=(f"KHt{p}",))

            def g2(e, p=p, h0=h0):
                e.matmul(PS[B_M1][:, 0:128], lhsT=ATb[:, h0, :], rhs=GVh(h0), start=True, stop=False)
                e.matmul(PS[B_M1][:, 0:128], lhsT=QHp(p, slice(0, 64)), rhs=Sb_[0:64, p, :], start=False, stop=True)
                e.matmul(PS[B_K1][:, 0:128], lhsT=ATb[:, h0 + 1, :], rhs=GVh(h0 + 1), start=True, stop=False)
                e.matmul(PS[B_K1][:, 0:128], lhsT=QHp(p, slice(64, 128)), rhs=Sb_[64:128, p, :], start=False, stop=True)
                last = None
                for hh in range(2):
                    part = slice(hh * 64, (hh + 1) * 64)
                    last = e.matmul(PS[B_K1][part, 128:256], lhsT=KHt[:, p, hh * 64:(hh + 1) * 64], rhs=GVh(h0 + hh),
                                    start=True, stop=True)
                return last
            S.op("pe", g2, reads=(f"ATb{h0}", f"ATb{h1}", kgv, kqh, f"Sb{p}", f"KHt{p}"), writes=("M1", "K1"))
            Ovs = [PS[B_M1][:, 0:128], PS[B_K1][:, 0:128]]
            Okeys = ["M1", "K1"]
            for j in range(2):
                S.op("dve", lambda e, j=j: e.bn_stats(out=ST6b2[:, j, :], in_=Ovs[j]), reads=(Okeys[j],),
                     writes=(f"ST6b{j}",))
                S.op("dve", lambda e, j=j: e.bn_aggr(out=MVb2[:, j, :], in_=ST6b2[:, j, :]), reads=(f"ST6b{j}",),
                     writes=(f"MVb{j}",))
            S.op("dve", lambda e: e.tensor_tensor(out=SM[:, 0:2], in0=MVb2[:, :, 0], in1=MVb2[:, :, 0], op=ALU.mult),
                 reads=("MVb0", "MVb1"), writes=("SMa",))
            S.op("dve", lambda e: e.scalar_tensor_tensor(out=SM[:, 2:4], in0=SM[:, 0:2], scalar=EPS, in1=MVb2[:, :, 1],
                                                         op0=ALU.add, op1=ALU.add),
                 reads=("SMa", "MVb0", "MVb1"), writes=("SMb",))
            S.op("pool", lambda e: e.tensor_tensor(out=SM[:, 4:6], in0=SM[:, 2:4], in1=negh[:, 0:2], op=ALU.pow),
                 reads=("SMb",), writes=("SMc",))
            for j in range(2):
                h = h0 + j
                S.op("dve", lambda e, j=j, h=h: e.scalar_tensor_tensor(
                    out=Y[:, 512 + h * 128:512 + (h + 1) * 128], in0=Ovs[j], scalar=SM[:, 4 + j:5 + j],
                    in1=OG[:, 512 + h * 128:512 + (h + 1) * 128], op0=ALU.mult, op1=ALU.mult),
                    reads=(Okeys[j], "SMc", f"OGb{par}"), writes=(f"Y{4 + h}",))
            S.op("dve", lambda e, p=p: e.tensor_tensor(out=Sf[:, p, :], in0=Sf[:, p, :], in1=PS[B_K1][:, 128:256],
                                                       op=ALU.add), reads=("K1", f"Sf{p}"), writes=(f"Sf{p}",))
            S.op("dve", lambda e, p=p: e.tensor_scalar(out=Sf[:, p, :], in0=Sf[:, p, :], scalar1=ACLp(p), scalar2=None,
                                                       op0=ALU.mult), reads=(kacl, f"Sf{p}"), writes=(f"Sf{p}",))
            S.op("act", lambda e, p=p: e.activation(out=Sb_[:, p, :], in_=Sf[:, p, :], func=AF.Copy),
                 reads=(f"Sf{p}",), writes=(f"Sb{p}",))

        if main:
            yk = tuple(f"Y{j}" for j in range(8))

            def f_t(e):
                last = None
                for kc in range(8):
                    last = e.transpose(out=PSb[B_K1][:, kc * 128:(kc + 1) * 128],
                                       in_=Y[:, kc * 128:(kc + 1) * 128], identity=ident_bf[:, :])
                return last
            S.op("pe", f_t, reads=yk, writes=("K1",))

            def f_e(e):
                last = None
                for kc in range(8):
                    last = e.activation(out=YTt[:, kc, :], in_=PSb[B_K1][:, kc * 128:(kc + 1) * 128],
                                        func=AF.Identity, scale=NGT[:, kc:kc + 1])
                return last
            S.op("act", f_e, reads=("K1", "NGT"), writes=("YTt",))
            for half in range(2):
                bank = B_M0 if half == 0 else B_M1
                bkey = "M0" if half == 0 else "M1"

                def f(e, half=half, bank=bank):
                    last = None
                    for kc in range(8):
                        last = e.matmul(PS[bank][:, 0:512], lhsT=YTt[:, kc, :],
                                        rhs=Wo_[:, kc, half * 512:(half + 1) * 512], start=(kc == 0), stop=(kc == 7))
                    return last
                S.op("pe", f, reads=("Wo", "YTt"), writes=(bkey,))
                S.op("dve", lambda e, half=half, bank=bank: e.scalar_tensor_tensor(
                    out=H[:, ti, half * 512:(half + 1) * 512], in0=H[:, ti, half * 512:(half + 1) * 512],
                    scalar=ALPHA, in1=PS[bank][:, 0:512], op0=ALU.mult, op1=ALU.add),
                    reads=(bkey, f"H{ti}"), writes=(f"H{ti}",))

    def groupL(g):
        i0 = 4 * g
        gp = g % 2
        HTG = HTGs[gp]
        for t in range(4):
            i = i0 + t
            xb = XG[t % 2]
            xk = f"XG{t % 2}"
            S.dma("sp", lambda e, xb=xb, i=i: e.dma_start(out=xb[:, :], in_=xall[i * 128:(i + 1) * 128, :]),
                  writes=(xk,))
            ln_A(xb[:, :], xb[:, :], (xk,), (xk,))
            to_feature_major(xb[:, :], (xk,), HTG[:, :, t * 128:(t + 1) * 128], (f"HTG{gp}_{t}",))

    def groupX(g):
        i0 = 4 * g
        gp = g % 2
        HTG = HTGs[gp]
        pm = PM[:, i0:i0 + 1]
        htk = tuple(f"HTG{gp}_{t}" for t in range(4))

        def mm512(out_ap, wsel, M=None):
            def f(e):
                last = None
                for kc in range(8):
                    last = e.matmul(out_ap, lhsT=wsel(kc), rhs=HTG[:, kc, :], start=(kc == 0), stop=(kc == 7))
                return last
            return f
        S.op("pe", mm512(PS[B_G][0:4, 0:512], lambda kc: Ws[:, kc, 0:4]), reads=htk + ("Ws",), writes=("G_a",))
        S.op("pe", mm512(PS[B_K0][0:4, 0:512], lambda kc: Ws[:, kc, 4:8]), reads=htk + ("Ws",), writes=("K0",))
        S.op("pe", mm512(PS[B_F0][0:16, 0:512], lambda kc: Ws[:, kc, 8:24]), reads=htk + ("Ws",), writes=("F0",))
        S.op("pe", mm512(PS[B_F1][:, 0:512], lambda kc: Wf[:, kc, 512 + 128: 512 + 256]),
             reads=htk + ("Wf",), writes=("F1",))
        S.op("act", lambda e: e.activation(out=ZG[:, 0, :], in_=PS[B_G][0:4, 0:512], func=AF.Exp,
                                           bias=GB[:, 0:1], scale=1.0), reads=("G_a", "GB0", "ZGa"), writes=("EIG", "ZGa"))
        S.op("act", lambda e: e.activation(out=EFG[:, :], in_=PS[B_K0][0:4, 0:512], func=AF.Exp,
                                           bias=GB[:, 1:2], scale=-1.0), reads=("K0", "GB1"), writes=("EFG",))
        S.op("act", lambda e: e.activation(out=GLRG[:, :], in_=PS[B_F0][0:16, 0:512], func=AF.Copy),
             reads=("F0",), writes=("GLRG",))
        for p in range(2):
            bank = B_G if p == 0 else B_K0
            bkey = "G_a" if p == 0 else "K0"
            S.op("pe", lambda e, p=p, bank=bank: e.matmul(PS[bank][:, 0:512], lhsT=GLW[0:16, p * 128:(p + 1) * 128],
                                                          rhs=GLRG[0:16, :], start=True, stop=True),
                 reads=("GLRG", "GLW"), writes=(bkey,))
            S.op("act", lambda e, p=p, bank=bank: e.activation(out=E2G[:, p, :], in_=PS[bank][:, 0:512], func=AF.Exp,
                                                               bias=GLB[:, p:p + 1], scale=-1.0),
                 reads=(bkey, "GLB"), writes=(f"E2G{p}",))
        S.op("act", lambda e: e.activation(out=E2G[:, :, :], in_=E2G[:, :, :], func=AF.Ln, bias=1.0, scale=1.0),
             reads=("E2G0", "E2G1"), writes=("E2G0", "E2G1"))
        S.op("dve", lambda e: e.tensor_scalar(out=EFG[:, :], in0=EFG[:, :], scalar1=1.0, scalar2=None, op0=ALU.add),
             reads=("EFG",), writes=("EFG",))
        for t in range(4):
            tk = slice(t * 128, (t + 1) * 128)
            S.op("dve", lambda e, tk=tk: e.tensor_tensor_scan(out=ZG[:, 1, tk], data0=EFG[:, tk],
                                                              data1=zeros_f[0:4, :], initial=1.0,
                                                              op0=ALU.mult, op1=ALU.add),
                 reads=("EFG",), writes=(f"ZGb{t}",))
        zgb = tuple(f"ZGb{t}" for t in range(4))
        S.op("dve", lambda e: e.tensor_tensor(out=ZG[:, 0, :], in0=ZG[:, 0, :], in1=ZG[:, 1, :], op=ALU.mult),
             reads=("EIG",) + zgb, writes=("ZGa", "EIG"))
        S.op("dve", lambda e: e.reciprocal(out=RLG[:, :],
                                           in_=ZG[:, 1, :].rearrange("p (t l) -> p t l", l=128)[:, :, 127]),
             reads=zgb, writes=("RLG",))
        for t in (2, 1, 0):
            S.op("dve", lambda e, t=t: e.tensor_tensor(out=RLG[:, t:t + 1], in0=RLG[:, t:t + 1],
                                                       in1=RLG[:, t + 1:t + 2], op=ALU.mult),
                 reads=("RLG",), writes=("RLG",))
        for t in range(4):
            tk = slice(t * 128, (t + 1) * 128)
            S.op("dve", lambda e, t=t: e.tensor_scalar(out=D4G[:, t, :], in0=ident_f[0:4, 0:4], scalar1=RLG[:, t:t + 1],
                                                       scalar2=None, op0=ALU.mult), reads=("RLG",), writes=(f"D4G{t}",))

            def f_sc(e, t=t, tk=tk):
                o = t * 12
                e.matmul(PS[B_G][:, o:o + 4], lhsT=ZG[0:4, 0, tk], rhs=ident_f[0:4, 0:4], start=True, stop=True)
                e.matmul(PS[B_G][:, o + 4:o + 8], lhsT=ZG[0:4, 1, tk], rhs=ident_f[0:4, 0:4], start=True, stop=True)
                return e.matmul(PS[B_G][:, o + 8:o + 12], lhsT=ones_f[0:4, :], rhs=D4G[0:4, t, :],
                                start=True, stop=True)
            S.op("pe", f_sc, reads=("ZGa", f"D4G{t}", "E2G0") + zgb, writes=("G_c",))
        S.op("dve", lambda e: e.tensor_copy(out=SCG[gp][:, :, 0:12],
                                            in_=PS[B_G][:, 0:48].rearrange("p (a b) -> p a b", a=4)),
             reads=("G_c",), writes=(f"SCG{gp}",))
        S.op("dve", lambda e: e.tensor_scalar(out=SCG[gp][:, :, 0:4], in0=SCG[gp][:, :, 0:4], scalar1=pm, scalar2=None,
                                              op0=ALU.mult), reads=(f"SCG{gp}", "PM"), writes=(f"SCG{gp}",))
        S.op("dve", lambda e: e.tensor_tensor(out=SCG[gp][:, :, 12:16], in0=SCG[gp][:, :, 0:4],
                                              in1=SCG[gp][:, :, 8:12], op=ALU.mult),
             reads=(f"SCG{gp}",), writes=(f"SCG{gp}",))
        for p in range(2):
            for t in range(4):
                tk = slice(t * 128, (t + 1) * 128)
                S.op("dve", lambda e, p=p, tk=tk: e.tensor_tensor_scan(out=BCG[:, p, tk], data0=ones_f[:, :],
                                                                       data1=E2G[:, p, tk], initial=0.0,
                                                                       op0=ALU.mult, op1=ALU.add),
                     reads=(f"E2G{p}",), writes=(f"BCG{p}{t}",))
        bck = tuple(f"BCG{p}{t}" for p in range(2) for t in range(4))
        S.op("dve", lambda e: e.tensor_copy(out=SUFG[:, :, :],
                                            in_=BCG[:, :, :].rearrange("p a (t l) -> p a t l", l=128)[:, :, :, 127]),
             reads=bck, writes=("SUFG",))
        for t in (2, 1, 0):
            S.op("dve", lambda e, t=t: e.tensor_tensor(out=SUFG[:, :, t], in0=SUFG[:, :, t], in1=SUFG[:, :, t + 1],
                                                       op=ALU.add), reads=("SUFG",), writes=("SUFG",))
        S.op("dve", lambda e: e.tensor_scalar(out=SUFG[:, :, :], in0=SUFG[:, :, :], scalar1=-1.0 / 16.0, scalar2=None,
                                              op0=ALU.mult), reads=("SUFG",), writes=("SUFG",))

        def f_ai(e):
            last = None
            for p in range(2):
                for t in range(4):
                    tk = slice(t * 128, (t + 1) * 128)
                    last = e.activation(out=E2G[:, p, tk], in_=BCG[:, p, tk], func=AF.Exp,
                                        bias=SUFG[:, p, t:t + 1], scale=1.0 / 16.0)
            return last
        S.op("act", f_ai, reads=bck + ("SUFG",), writes=("E2G0", "E2G1", "AIG"))
        S.op("act", lambda e: e.activation(out=ACLG[gp][:, :, :].rearrange("p t a -> p a t"), in_=SUFG[:, :, :],
                                           func=AF.Exp), reads=("SUFG",), writes=(f"ACLG{gp}",))
        for cp0 in (0, 2):
            for c in (cp0, cp0 + 1):
                bank = B_F0 if c % 2 == 0 else B_F1
                bkey = "F0" if c % 2 == 0 else "F1"
                if c != 1:
                    S.op("pe", mm512(PS[bank][:, 0:512], lambda kc, c=c: Wf[:, kc, 512 + c * 128: 512 + (c + 1) * 128]),
                         reads=htk + ("Wf",), writes=(bkey,))
                S.op("act", lambda e, c=c, bank=bank: e.activation(out=RAWG[:, c, 3:515], in_=PS[bank][:, 0:512],
                                                                   func=AF.Identity, scale=pm),
                     reads=(bkey, "PM"), writes=(f"RAWG{c}",))
            for c in (cp0, cp0 + 1):
                cg = 4 + c
                S.op("dve", lambda e, c=c, cg=cg: e.tensor_scalar(out=CVG[:, c, :], in0=RAWG[:, c, 3:515],
                                                                  scalar1=CW[:, 3, cg:cg + 1], scalar2=CB[:, cg:cg + 1],
                                                                  op0=ALU.mult, op1=ALU.add),
                     reads=(f"RAWG{c}", "CW", "CB"), writes=(f"CVG{c}",))
            for j in range(3):
                for c in (cp0, cp0 + 1):
                    cg = 4 + c
                    S.op("dve", lambda e, c=c, cg=cg, j=j: e.scalar_tensor_tensor(
                        out=CVG[:, c, :], in0=RAWG[:, c, j:j + 512], scalar=CW[:, j, cg:cg + 1], in1=CVG[:, c, :],
                        op0=ALU.mult, op1=ALU.add), reads=(f"RAWG{c}", f"CVG{c}"), writes=(f"CVG{c}",))
            for c in (cp0, cp0 + 1):
                S.op("pool", lambda e, c=c: e.tensor_copy(out=RAWG[:, c, 0:3], in_=RAWG[:, c, 512:515]),
                     reads=(f"RAWG{c}",), writes=(f"RAWG{c}",))
        cvk = tuple(f"CVG{c}" for c in range(4))
        S.op("act", lambda e: e.activation(out=THG[:, :, :], in_=CVG[:, :, :], func=AF.Tanh, scale=0.5),
             reads=cvk, writes=("THG",))
        S.op("dve", lambda e: e.scalar_tensor_tensor(out=QKG[gp][:, :, :], in0=THG[:, :, :], scalar=1.0, in1=CVG[:, :, :],
                                                     op0=ALU.add, op1=ALU.mult), reads=("THG",) + cvk, writes=(f"QKG{gp}",))
        if g == NG4 - 1:
            def f_mq(e):
                last = None
                for c in range(4):
                    for kc in range(8):
                        last = e.matmul(PS[B_F0][:, c * 128:(c + 1) * 128], lhsT=Wf[:, kc, c * 128:(c + 1) * 128],
                                        rhs=HTG[:, kc, 384:512], start=(kc == 0), stop=(kc == 7))
                return last
            S.op("pe", f_mq, reads=htk + ("Wf",), writes=("F0",))
            S.op("act", lambda e: e.activation(out=RAW[:, 0:4, 3:131],
                                               in_=PS[B_F0][:, :].rearrange("p (a b) -> p a b", a=4),
                                               func=AF.Identity, scale=pm), reads=("F0", "PM"), writes=("RAWq",))
            S.op("pool", lambda e: e.tensor_copy(out=RAW[:, 0:4, 0:3], in_=RAW[:, 0:4, 128:131]),
                 reads=("RAWq",), writes=("RAWq",))
        for t in range(4):
            tk = slice(t * 128, (t + 1) * 128)
            bank = B_K0 if t % 2 == 0 else B_G
            bkey = "K0" if t % 2 == 0 else "G_a"

            def f(e, tk=tk, bank=bank):
                last = None
                for kc in range(8):
                    last = e.matmul(PS[bank][:, 0:512], lhsT=HTG[:, kc, tk], rhs=WtA[:, kc, 0:512],
                                    start=(kc == 0), stop=(kc == 7))
                return last
            S.op("pe", f, reads=htk + ("Wt",), writes=(bkey,))
            S.op("act", lambda e, t=t, bank=bank: e.activation(
                out=VEG[gp][:, 4 * t:4 * t + 4, 0:128], in_=PS[bank][:, :].rearrange("p (a b) -> p a b", a=4),
                func=AF.Copy), reads=(bkey,), writes=(f"VEG{gp}",))
        for t in range(4):
            tk = slice(t * 128, (t + 1) * 128)
            bank = B_F0 if t % 2 == 0 else B_F1
            bkey = "F0" if t % 2 == 0 else "F1"

            def f(e, tk=tk, bank=bank):
                last = None
                for kc in range(8):
                    last = e.matmul(PS[bank][:, 0:512], lhsT=HTG[:, kc, tk], rhs=WtA[:, kc, 512:1024],
                                    start=(kc == 0), stop=(kc == 7))
                return last
            S.op("pe", f, reads=htk + ("Wt",), writes=(bkey,))
            S.op("act", lambda e, t=t, bank=bank: e.activation(out=GVG[gp][:, t, :], in_=PS[bank][:, :], func=AF.Copy),
                 reads=(bkey,), writes=(f"GVG{gp}",))

        for p in range(2):
            bank = B_K0 if p == 0 else B_G
            bkey = "K0" if p == 0 else "G_a"
            S.op("pe", mm512(PS[bank][:, 0:512], lambda kc, p=p: Wf[:, kc, 1280 + p * 128: 1280 + (p + 1) * 128]),
                 reads=htk + ("Wf",), writes=(bkey,))
            S.op("dve", lambda e, p=p, bank=bank: e.tensor_tensor(out=KHG[gp][:, p, :], in0=PS[bank][:, 0:512],
                                                                  in1=E2G[:, p, :], op=ALU.mult),
                 reads=(bkey, "AIG"), writes=(f"KHG{gp}",))

    I_START = 4 * NG4
    if NG4 > 0:
        S.op("pool", lambda e: e.memset(VEG[0][:, :, :], 1.0), writes=("VEG0",))
        S.op("pool", lambda e: e.memset(VEG[1][:, :, :], 1.0), writes=("VEG1",))
        S.op("pool", lambda e: e.memset(RAWG[:, :, :], 0.0), writes=tuple(f"RAWG{c}" for c in range(4)))

        def groupY(g):
            for t in range(4):
                stageY(4 * g + t, grp=t)
        S.replay(S.record(groupL, 0))
        lists = [S.record(groupX, 0)]
        if NG4 > 1:
            lists.append(S.record(groupL, 1))
        S.replay(*lists)
        for g in range(NG4):
            lists = [S.record(groupY, g)]
            if g + 1 < NG4:
                lists.append(S.record(groupX, g + 1))
            if g + 2 < NG4:
                lists.append(S.record(groupL, g + 2))
            mode = os.environ.get("GMODE", "prop")
            if mode == "prop":
                S.replay(*lists)
            elif mode == "seq":
                for l_ in lists:
                    S.replay(l_)
            elif mode == "xfirst":
                if len(lists) > 1:
                    S.replay(lists[1])
                S.replay(lists[0], *lists[2:])
            elif mode == "xl":
                S.replay(*lists[1:])
                S.replay(lists[0])
        S.op("pool", lambda e: e.tensor_copy(out=RAW[:, 4:8, 0:3], in_=RAWG[:, :, 512:515]),
             reads=tuple(f"RAWG{c}" for c in range(4)) + ("RAWk",), writes=("RAWk",))
        barrier()
        load_main_weights()

    PIPE = True
    if PIPE:
        S.replay(S.record(stageX, I_START))
        for i in range(I_START, NT):
            ry = S.record(stageY, i)
            if os.environ.get("YSTOP"):
                ry = ry[:int(os.environ["YSTOP"])]
            if i + 1 < NT:
                S.replay(S.record(stageX, i + 1), ry)
            else:
                S.replay(ry)
    else:
        for i in range(I_START, NT):
            stageX(i)
            stageY(i)

    if stop_after == "A":
        return dump_H_and_finish()
    barrier()
    cur["XS"] = XS_B
    cur["HB"] = HB_B
    wload(Wk_, x_wk, [(0, 0, 1024)], "Wk")
    wload(Wq_, x_wq, [(0, 0, 1024)], "Wq")
    wload(Wx_, x_wo, [(0, 0, 1024)], "Wx")
    bcast_load(LNG[:, :], ln_g["ln1"], D, "LNG")
    bcast_load(LNB[:, :], ln_b["ln1"], D, "LNB")

    def pre_kv():
        for mc in range(2):
            xb = XBB[0]
            S.dma("sp", lambda e, xb=xb, mc=mc: e.dma_start(out=xb[:, :], in_=mem_d[mc * 128:(mc + 1) * 128, :]),
                  writes=("XBB",))
            S.op("act", lambda e, xb=xb: e.activation(out=MHB[:, :], in_=xb[:, :], func=AF.Copy),
                 reads=("XBB",), writes=("MHB",))
            transpose8(MHB, ("MHB",), MEMT[:, :, mc * 128:(mc + 1) * 128], (f"MEMT{mc}",))
        for c in range(8):
            bank = B_K0 if c % 2 == 0 else B_K1
            bkey = "K0" if c % 2 == 0 else "K1"

            def f_k(e, c=c, bank=bank):
                last = None
                for kc in range(8):
                    last = e.matmul(PS[bank][:, 0:256], lhsT=Wk_[:, kc, c * 128:(c + 1) * 128], rhs=MEMT[:, kc, :],
                                    start=(kc == 0), stop=(kc == 7))
                return last
            S.op("pe", f_k, reads=("Wk", "MEMT0", "MEMT1"), writes=(bkey,))
            S.op("act", lambda e, c=c, bank=bank: e.activation(out=KT[:, c, :], in_=PS[bank][:, 0:256], func=AF.Copy),
                 reads=(bkey,), writes=(f"KT{c}",))
        wload(Wk_, x_wv, [(0, 0, 1024)], "Wk")
        for mc in range(2):
            for half in range(2):
                bank = B_K0 if half == 0 else B_K1
                bkey = "K0" if half == 0 else "K1"

                def f_v(e, mc=mc, half=half, bank=bank):
                    last = None
                    for kc in range(8):
                        last = e.matmul(PS[bank][:, 0:512], lhsT=MEMT[:, kc, mc * 128:(mc + 1) * 128],
                                        rhs=Wk_[:, kc, half * 512:(half + 1) * 512], start=(kc == 0), stop=(kc == 7))
                    return last
                S.op("pe", f_v, reads=("Wk", "MEMT0", "MEMT1"), writes=(bkey,))
                S.op("act", lambda e, mc=mc, half=half, bank=bank: e.activation(
                    out=VV[:, mc, half * 512:(half + 1) * 512], in_=PS[bank][:, 0:512], func=AF.Copy),
                    reads=(bkey,), writes=(f"VV{mc}{half}",))

    def stageB1(ti):
        tok = slice(ti * 128, (ti + 1) * 128)
        layer_norm(H[:, ti, :], H[:, ti, :], (f"H{ti}",), (f"H{ti}",), xs=XS_B, xskey="XS1", sc=(ST6c, MVc, RSc),
                   sfx="c")
        to_feature_major(H[:, ti, :], (f"H{ti}",), HT[:, :, tok], (f"HT{ti}",), B_T, "PS_T", hb=MHB, hbkey="MHB")

    KTK = tuple(f"KT{c}" for c in range(8))
    VVK = ("VV00", "VV01", "VV10", "VV11")
    S.dma("sp", lambda e: e.dma_start(out=LNG2[:, :], in_=ln_g["ln2"][0:D].partition_broadcast(128)), writes=("LNG2",))
    S.dma("sp", lambda e: e.dma_start(out=LNB2[:, :], in_=ln_b["ln2"][0:D].partition_broadcast(128)), writes=("LNB2",))
    S.replay(S.record(pre_kv))
    S.replay(S.record(stageB1, 0))
    if NT_MAIN > 1:
        S.replay(S.record(stageB1, 1))
    def stageP(ti):
        tok = slice(ti * 128, (ti + 1) * 128)
        par = ti % 2
        PT, RINV = PTb[par], RINVb[par]
        for g in range(2):
            bank = B_F0 if g == 0 else B_F1
            bkey = "F0" if g == 0 else "F1"

            def f_q(e, g=g, bank=bank):
                last = None
                for c in range(4):
                    for kc in range(8):
                        last = e.matmul(PS[bank][:, c * 128:(c + 1) * 128],
                                        lhsT=Wq_[:, kc, (g * 4 + c) * 128:(g * 4 + c + 1) * 128],
                                        rhs=HT[:, kc, tok], start=(kc == 0), stop=(kc == 7))
                return last
            S.op("pe", f_q, reads=("Wq", f"HT{ti}"), writes=(bkey,))
            S.op("act", lambda e, g=g, bank=bank: e.activation(
                out=QT[:, g * 4:(g + 1) * 4, :], in_=PS[bank][:, :].rearrange("p (a b) -> p a b", a=4),
                func=AF.Copy), reads=(bkey,), writes=(f"QT{g}",))

        def f_s(e):
            last = None
            for hd in range(4):
                bank = B_M0 if hd < 2 else B_M1
                for c in range(2):
                    last = e.matmul(PS[bank][:, (hd % 2) * 256:(hd % 2 + 1) * 256], lhsT=QT[:, 2 * hd + c, :],
                                    rhs=KT[:, 2 * hd + c, :], start=(c == 0), stop=(c == 1))
            return last
        S.op("pe", f_s, reads=("QT0", "QT1") + KTK, writes=("M0", "M1"))
        for half in range(2):
            bank = B_M0 if half == 0 else B_M1
            bkey = "M0" if half == 0 else "M1"
            S.op("dve", lambda e, half=half, bank=bank: e.tensor_reduce(
                out=MX[:, 2 * half:2 * half + 2], in_=PS[bank][:, :].rearrange("p (a b) -> p a b", a=2), axis=AX.X,
                op=ALU.max), reads=(bkey,), writes=(f"MX{half}",))
        S.op("dve", lambda e: e.tensor_scalar(out=NB[:, :], in0=MX[:, :], scalar1=-1.0 / 16.0, scalar2=None,
                                              op0=ALU.mult), reads=("MX0", "MX1"), writes=("NB",))
        for hd in range(4):
            bank = B_M0 if hd < 2 else B_M1
            bkey = "M0" if hd < 2 else "M1"
            sc_ap = PS[bank][:, (hd % 2) * 256:(hd % 2 + 1) * 256]
            S.op("act", lambda e, hd=hd, sc_ap=sc_ap: e.activation(
                out=PEX[:, hd, :], in_=sc_ap, func=AF.Exp, bias=NB[:, hd:hd + 1], scale=1.0 / 16.0,
                accum_out=RSUM[:, hd:hd + 1]), reads=(bkey, "NB"), writes=(f"PEX{hd}", f"RSUM{hd}"))
        S.op("dve", lambda e: e.reciprocal(out=RINV[:, :], in_=RSUM[:, :]),
             reads=tuple(f"RSUM{hd}" for hd in range(4)), writes=(f"RINV{par}",))

        def f_pt(e):
            last = None
            for hd in range(4):
                for mc in range(2):
                    j = hd * 2 + mc
                    last = e.transpose(out=PSb[B_F0][:, j * 128:(j + 1) * 128],
                                       in_=PEX[:, hd, mc * 128:(mc + 1) * 128], identity=ident_bf[:, :])
            return last
        S.op("pe", f_pt, reads=tuple(f"PEX{hd}" for hd in range(4)), writes=("F0",))
        S.op("act", lambda e: e.activation(out=PT[:, :, :], in_=PSb[B_F0][:, :].rearrange("p (a b) -> p a b", a=8),
                                           func=AF.Copy), reads=("F0",), writes=(f"PT{par}",))

    def stageQ(ti):
        tok = slice(ti * 128, (ti + 1) * 128)
        par = ti % 2
        PT, RINV = PTb[par], RINVb[par]

        def f_o2(e):
            last = None
            for hd in range(4):
                bank = B_K0 if hd < 2 else B_K1
                for mc in range(2):
                    last = e.matmul(PS[bank][:, (hd % 2) * 256:(hd % 2 + 1) * 256], lhsT=PT[:, hd * 2 + mc, :],
                                    rhs=VV[:, mc, hd * 256:(hd + 1) * 256], start=(mc == 0), stop=(mc == 1))
            return last
        S.op("pe", f_o2, reads=(f"PT{par}",) + VVK, writes=("K0", "K1"))
        for half in range(2):
            bank = B_K0 if half == 0 else B_K1
            bkey = "K0" if half == 0 else "K1"
            S.op("dve", lambda e, half=half, bank=bank: e.tensor_tensor(
                out=OB[:, half * 512:(half + 1) * 512].rearrange("p (a b) -> p a b", a=2),
                in0=PS[bank][:, :].rearrange("p (a b) -> p a b", a=2),
                in1=RINV[:, 2 * half:2 * half + 2].rearrange("p (h o) -> p h o", o=1).broadcast_to([128, 2, 256]),
                op=ALU.mult), reads=(bkey, f"RINV{par}"), writes=(f"OB{half}",))
        transpose8(OB, ("OB0", "OB1"), HTt_B[:, :, :], ("HTtB",), B_G, "G_a")
        proj_resid(ti, HTt_B, ("HTtB",), Wx_, "Wx")
        layer_norm(H[:, ti, :], H[:, ti, :], (f"H{ti}",), (f"H{ti}",), xs=XBB[0], xskey="XBB", lng=LNG2, lnb=LNB2,
                   gkey="LNG2", bkey="LNB2")
        to_feature_major(H[:, ti, :], (f"H{ti}",), HT[:, :, tok], (f"HT{ti}",), B_G, "G_a", hb=HB_B, hbkey="HB")

    S.replay(S.record(stageP, 0))
    for ti in range(NT_MAIN):
        lists = [S.record(stageQ, ti)]
        if ti + 1 < NT_MAIN:
            lists.append(S.record(stageP, ti + 1))
        if ti + 2 < NT_MAIN:
            lists.append(S.record(stageB1, ti + 2))
        S.replay(*lists)

    if stop_after == "B":
        return dump_H_and_finish()
    barrier()
    cur["XS"] = XS_C
    bcast_load(LNG[:, :], ln_g["ln3"], D, "LNG")
    bcast_load(LNB[:, :], ln_b["ln3"], D, "LNB")
    HTK = tuple(f"HT{t}" for t in range(NT_MAIN))
    hid = HID[0]
    for q in range(4):
        W1 = W1s[q % 2]
        W2 = W2s[q % 2]
        k1 = f"W1_{q % 2}"
        k2 = f"W2_{q % 2}"
        wload(W1, w_ff1, [(0, q * 1024, 1024)], k1)
        S.dma("pool", lambda e, W2=W2, q=q: e.dma_start(
            out=W2[:, :, :], in_=w_ff2[q * 1024:(q + 1) * 1024, :].rearrange("(kc p) n -> p kc n", p=128)),
            writes=(k2,))
        for tg in range(TGN):
            for fc in range(8):
                bank = B_F0 if fc % 2 == 0 else B_F1
                bkey = "F0" if fc % 2 == 0 else "F1"
                rl = RL2[fc % 2]
                rlk = f"RL2_{fc % 2}"
                def f1(e, fc=fc, bank=bank, W1=W1, tg=tg):
                    last = None
                    for kc in range(8):
                        last = e.matmul(PS[bank][:, 0:TPG * 128], lhsT=W1[:, kc, fc * 128:(fc + 1) * 128],
                                        rhs=HT[:, kc, tg * TPG * 128:(tg + 1) * TPG * 128], start=(kc == 0), stop=(kc == 7))
                    return last
                S.op("pe", f1, reads=(k1,) + HTK[tg * TPG:(tg + 1) * TPG], writes=(bkey,))
                S.op("act", lambda e, bank=bank, rl=rl: e.activation(out=rl[:, 0:TPG * 128], in_=PS[bank][:, 0:TPG * 128],
                                                                     func=AF.Relu), reads=(bkey,), writes=(rlk,))
                S.op("dve", lambda e, rl=rl, fc=fc: e.tensor_tensor(out=hid[:, fc, 0:TPG * 128], in0=rl[:, 0:TPG * 128],
                                                                    in1=rl[:, 0:TPG * 128], op=ALU.mult),
                     reads=(rlk,), writes=(f"HID_{fc}",))
            hk_all = tuple(f"HID_{fc}" for fc in range(8))
            for t in range(TPG):
                ti = tg * TPG + t
                for half in range(2):
                    bank = B_K0 if half == 0 else B_K1
                    bkey = "K0" if half == 0 else "K1"
                    def f2(e, t=t, half=half, bank=bank, W2=W2):
                        last = None
                        for fc in range(8):
                            last = e.matmul(PS[bank][:, 0:512], lhsT=hid[:, fc, t * 128:(t + 1) * 128],
                                            rhs=W2[:, fc, half * 512:(half + 1) * 512],
                                            start=(fc == 0), stop=(fc == 7))
                        return last
                    S.op("pe", f2, reads=(k2,) + hk_all, writes=(bkey,))
                    hs = H[:, ti, half * 512:(half + 1) * 512]
                    if q == 0:
                        S.op("dve", lambda e, hs=hs, bank=bank: e.scalar_tensor_tensor(
                            out=hs, in0=hs, scalar=ALPHA, in1=PS[bank][:, 0:512], op0=ALU.mult, op1=ALU.add),
                            reads=(bkey, f"H{ti}"), writes=(f"H{ti}",))
                    else:
                        S.op("dve", lambda e, hs=hs, bank=bank: e.tensor_tensor(
                            out=hs, in0=hs, in1=PS[bank][:, 0:512], op=ALU.add),
                            reads=(bkey, f"H{ti}"), writes=(f"H{ti}",))
                if q == 3:
                    layer_norm(H[:, ti, :], H[:, ti, :], (f"H{ti}",), (f"H{ti}",))
                    S.dma("sp", lambda e, ti=ti: e.dma_start(out=out_d[ti * 128:(ti + 1) * 128, :], in_=H[:, ti, :]),
                          reads=(f"H{ti}",), writes=(f"OUT{ti}",))
    S.fence("sp", [f"OUT{t}" for t in range(NT_MAIN)])

    return finish()


_NC_CACHE = {}


def kernel(**inputs):
    f = lambda k: np.ascontiguousarray(np.asarray(inputs[k], dtype=np.float32))
    x = f("x")
    mem = f("mem")
    if "nc" not in _NC_CACHE:
        _NC_CACHE["nc"] = build_program()
    nc = _NC_CACHE["nc"]
    shared = {
        "w_in": f("w_in")[0], "conv_w": f("conv_w")[0], "conv_b": f("conv_b")[0],
        "m_i_bias": f("m_i_bias")[0], "m_f_bias": f("m_f_bias")[0], "m_norm_g": f("m_norm_g")[0],
        "g_lr_w": f("g_lr_w")[0], "g_lr_b": f("g_lr_b")[0], "g_norm_g": f("g_norm_g")[0],
        "w_out": f("w_out")[0], "x_wq": f("x_wq")[0], "x_wk": f("x_wk")[0], "x_wv": f("x_wv")[0],
        "x_wo": f("x_wo")[0], "w_ff1": f("w_ff1")[0], "w_ff2": f("w_ff2")[0],
        "ln_in_g": f("ln_in_g"), "ln_in_b": f("ln_in_b"),
        "ln1_g": f("ln1_g")[0], "ln1_b": f("ln1_b")[0], "ln2_g": f("ln2_g")[0], "ln2_b": f("ln2_b")[0],
        "ln3_g": f("ln3_g")[0], "ln3_b": f("ln3_b")[0],
    }
    in_maps = []
    SEG = NT_MAIN * 128
    for c in range(8):
        b, s = c // 4, c % 4
        start = s * SEG
        xa = np.zeros((NT * 128, D), np.float32)
        pm = np.zeros((128, NT), np.float32)
        npre = NT_PRE * 128
        have = min(start, npre)
        if have > 0:
            xa[npre - have:npre] = x[b, start - have:start]
            pm[:, NT_PRE - have // 128:NT_PRE] = 1.0
        xa[npre:] = x[b, start:start + SEG]
        pm[:, NT_PRE:] = 1.0
        m = dict(shared)
        m["xall"] = xa
        m["pmask"] = pm
        m["mem"] = mem[b]
        in_maps.append(m)
    res = run_bass_kernel_spmd(nc, in_maps, core_ids=list(range(8)))
    out = np.zeros((2, 8192, D), np.float32)
    for c in range(8):
        b, s = c // 4, c % 4
        out[b, s * SEG:(s + 1) * SEG] = np.asarray(res.results[c]["out"], dtype=np.float32)
    return out
```

```python
import math
import os
import numpy as np
import concourse.bass as bass
import concourse.mybir as mybir
from concourse.bass_utils import run_bass_kernel_spmd
from contextlib import ExitStack

F32 = mybir.dt.float32
BF16 = mybir.dt.bfloat16
AF = mybir.ActivationFunctionType
ALU = mybir.AluOpType
AX = mybir.AxisListType

D = 1024
NT_MAIN = 16
NT_PRE = 48
NT = NT_MAIN + NT_PRE
ALPHA = 2.0 ** 0.25
EPS = 1e-5
NMEM = 256
DFF = 4096
KDMA = 6
ENGS = ["pe", "act", "dve", "pool", "sp"]


class Sched:
    def __init__(self):
        self.ops = {e: [] for e in ENGS}
        self.cnt = {e: 0 for e in ENGS}
        self.lastw = {}
        self.readers = {}
        self.waited = {e: {} for e in ENGS}
        self.snap = {}
        self.dma_n = {e: 0 for e in ENGS}
        self.rec = None

    def record(self, f, *a):
        self.rec = []
        f(*a)
        r, self.rec = self.rec, None
        return r

    def replay(self, *lists):
        pos = [0] * len(lists)
        while True:
            best, bf = None, 2.0
            for j, l in enumerate(lists):
                if pos[j] < len(l):
                    fr = pos[j] / len(l)
                    if fr < bf:
                        best, bf = j, fr
            if best is None:
                break
            kind, a = lists[best][pos[best]]
            pos[best] += 1
            getattr(self, kind)(*a)

    class _FakeIns:
        def then_inc(self, *a, **k):
            return self

    class _FakeEng:
        def __init__(self):
            self.calls = []

        def __getattr__(self, name):
            def f(*a, **k):
                self.calls.append((name, a, k))
                return Sched._FakeIns()
            return f

    @staticmethod
    def _free(ap):
        try:
            sh = list(ap.shape)
            n = 1
            for x in sh[1:]:
                n *= int(x)
            return n
        except Exception:
            return 128

    def est_cost(self, kind, eng, fn):
        if fn is None:
            return 0.0
        fe = Sched._FakeEng()
        try:
            fn(fe)
        except Exception:
            return 0.5
        c = 0.0
        for (name, a, k) in fe.calls:
            out = k.get("out", a[0] if a else None)
            n = Sched._free(out) if out is not None else 128
            if name == "matmul":
                rhs = k.get("rhs", a[2] if len(a) > 2 else None)
                nn = Sched._free(rhs) if rhs is not None else 128
                c += max(nn, 64) / 1500.0 + 0.015
            elif name == "transpose":
                c += 0.1
            elif name == "dma_start":
                c += 0.15
            elif eng == "act":
                c += 0.2 + n * 0.00087
            elif eng == "dve":
                c += 0.1 + n * 0.0011
            elif eng == "pool":
                c += 0.3 + n * 0.002
            else:
                c += 0.2
        return c + 0.08

    def schedule(self, *streams):
        if not hasattr(self, "t_eng"):
            self.t_eng = {e: 0.0 for e in ENGS}
            self.t_w = {}
            self.t_r = {}
        streams = [list(x) for x in streams if x]
        wsets, rsets = [], []
        for st in streams:
            ws, rs = set(), set()
            for (kind, a) in st:
                rs.update(self.norm(k) for k in a[2])
                ws.update(self.norm(k) for k in a[3])
            wsets.append(ws)
            rsets.append(rs)
        for i1 in range(len(streams)):
            for i2 in range(len(streams)):
                if i1 != i2:
                    bad = wsets[i1] & (wsets[i2] | rsets[i2])
                    assert not bad, ("streams share scratch", i1, i2, sorted(bad))
        costs = [[self.est_cost(kind, a[0], a[1]) for (kind, a) in st] for st in streams]
        pos = [0] * len(streams)
        while True:
            best, bt = None, None
            for j, st in enumerate(streams):
                if pos[j] >= len(st):
                    continue
                kind, a = st[pos[j]]
                eng, fn, reads, writes = a
                t = self.t_eng[eng]
                for k in reads:
                    t = max(t, self.t_w.get(self.norm(k), 0.0))
                for k in writes:
                    nk = self.norm(k)
                    t = max(t, self.t_w.get(nk, 0.0), self.t_r.get(nk, 0.0))
                if bt is None or t < bt - 1e-9:
                    best, bt = j, t
            if best is None:
                break
            kind, a = streams[best][pos[best]]
            eng, fn, reads, writes = a
            c = costs[best][pos[best]]
            pos[best] += 1
            fin = bt + c
            self.t_eng[eng] = bt + (0.15 if kind == "dma" else c)
            done = fin + (2.0 if kind == "dma" else 0.0)
            for k in writes:
                self.t_w[self.norm(k)] = done
            for k in reads:
                nk = self.norm(k)
                if self.t_r.get(nk, 0.0) < done:
                    self.t_r[nk] = done
            getattr(self, kind)(*a)

    @staticmethod
    def norm(k):
        if k == "PS_T":
            return "ps0"
        if k == "F0":
            return "ps1"
        if k == "F1":
            return "ps2"
        if k == "K0" or k.startswith("K0o"):
            return "ps3"
        if k == "K1" or k.startswith("K1o"):
            return "ps4"
        if k.startswith("G_"):
            return "ps5"
        if k.startswith("M0"):
            return "ps6"
        if k.startswith("M1"):
            return "ps7"
        return k

    def _deps(self, eng, reads, writes):
        deps = {}

        def add(tok, war=False):
            if tok is None:
                return
            sk, val = tok
            if sk == ("e", eng) and eng == "pe":
                return
            if deps.get(sk, 0) < val:
                deps[sk] = val

        for k in reads:
            add(self.lastw.get(k))
        for k in writes:
            add(self.lastw.get(k))
            for sk, val in self.readers.get(k, {}).items():
                add((sk, val), war=True)
        w = self.waited[eng]
        cand = [(sk, val) for sk, val in deps.items() if w.get(sk, 0) < val]
        keep = []
        for (sk, val) in cand:
            implied = False
            for (sk2, val2) in cand:
                if (sk2, val2) != (sk, val) and sk2[0] == "e" and sk2[1] != eng:
                    if self.snap.get((sk2[1], val2), {}).get(sk, 0) >= val:
                        implied = True
                        break
            if not implied:
                keep.append((sk, val))
        for (sk, val) in keep:
            if w.get(sk, 0) < val:
                w[sk] = val
            if sk[0] == "e" and sk[1] != eng:
                for k2, v2 in self.snap.get((sk[1], val), {}).items():
                    if k2 != ("e", eng) and w.get(k2, 0) < v2:
                        w[k2] = v2
        return keep

    def _commit(self, tok, reads, writes):
        for k in writes:
            self.lastw[k] = tok
            self.readers[k] = {}
        for k in reads:
            r = self.readers.setdefault(k, {})
            if r.get(tok[0], 0) < tok[1]:
                r[tok[0]] = tok[1]

    def op(self, eng, fn, reads=(), writes=()):
        if self.rec is not None:
            self.rec.append(("op", (eng, fn, reads, writes)))
            return
        reads = tuple(dict.fromkeys(self.norm(k) for k in reads))
        writes = tuple(dict.fromkeys(self.norm(k) for k in writes))
        waits = self._deps(eng, reads, writes)
        self.cnt[eng] += 1
        tok = (("e", eng), self.cnt[eng])
        sn = dict(self.waited[eng])
        sn.pop(("e", eng), None)
        self.snap[(eng, self.cnt[eng])] = sn
        self.ops[eng].append((fn, waits, ("e", eng), 1))
        self._commit(tok, reads, writes)

    def dma(self, q, fn, reads=(), writes=()):
        if self.rec is not None:
            self.rec.append(("dma", (q, fn, reads, writes)))
            return
        reads = tuple(dict.fromkeys(self.norm(k) for k in reads))
        writes = tuple(dict.fromkeys(self.norm(k) for k in writes))
        j = self.dma_n[q]
        self.dma_n[q] += 1
        slot = j % KDMA
        val = 16 * (j // KDMA + 1)
        sk = ("d", q, slot)
        waits = self._deps(q, reads, writes)
        if val > 16 and self.waited[q].get(sk, 0) < val - 16:
            self.waited[q][sk] = val - 16
            waits.append((sk, val - 16))
        self.ops[q].append((fn, waits, sk, 16))
        self._commit((sk, val), reads, writes)

    def fence(self, eng, keys):
        keys = tuple(dict.fromkeys(self.norm(k) for k in keys))
        waits = self._deps(eng, keys, keys)
        self.ops[eng].append((None, waits, None, 0))


def build_program(nt_pre=48, nt_main=16, stop_after=None):
    global NT_MAIN, NT_PRE, NT
    NT_MAIN, NT_PRE = nt_main, nt_pre
    NT = NT_MAIN + NT_PRE
    TGN = max(1, NT_MAIN // 4)
    TPG = NT_MAIN // TGN
    nc = bass.Bass("TRN2", target_bir_lowering=False)

    def din(name, shape):
        return nc.dram_tensor(name, list(shape), F32, kind="ExternalInput").ap()

    xall = din("xall", [NT * 128, D])
    pmask_d = din("pmask", [128, NT])
    mem_d = din("mem", [NMEM, D])
    ln_g = {k: din(k + "_g", [D]) for k in ("ln_in", "ln1", "ln2", "ln3")}
    ln_b = {k: din(k + "_b", [D]) for k in ("ln_in", "ln1", "ln2", "ln3")}
    w_in = din("w_in", [D, 3608])
    conv_w = din("conv_w", [4, D])
    conv_b = din("conv_b", [D])
    m_i_bias = din("m_i_bias", [4])
    m_f_bias = din("m_f_bias", [4])
    m_norm_g = din("m_norm_g", [512])
    g_lr_w = din("g_lr_w", [16, 256])
    g_lr_b = din("g_lr_b", [256])
    g_norm_g = din("g_norm_g", [512])
    w_out = din("w_out", [D, D])
    x_wq = din("x_wq", [D, D])
    x_wk = din("x_wk", [D, D])
    x_wv = din("x_wv", [D, D])
    x_wo = din("x_wo", [D, D])
    w_ff1 = din("w_ff1", [D, DFF])
    w_ff2 = din("w_ff2", [DFF, D])
    out_d = nc.dram_tensor("out", [NT_MAIN * 128, D], F32, kind="ExternalOutput").ap()

    S = Sched()
    es = ExitStack()

    def sb(name, shape, dt=F32):
        return es.enter_context(nc.sbuf_tensor(name, list(shape), dt))

    NG4 = NT_PRE // 4 if (NT_PRE >= 4 and NT_PRE % 4 == 0) else 0
    HTILES = max(NT_MAIN, 16) if NG4 > 0 else NT_MAIN
    Hflat = sb("Hflat", [128, HTILES * D])
    H = Hflat[:, 0:NT_MAIN * D].rearrange("p (a b) -> p a b", a=NT_MAIN)
    Hbf = Hflat.bitcast(BF16)
    HTflat = sb("HTflat", [128, max(8 * NT_MAIN * 128, 16384)], BF16)
    HT = HTflat[:, 0:8 * NT_MAIN * 128].rearrange("p (a b) -> p a b", a=8)
    ARENA_BYTES = 73 * 1024
    S2_BYTES = 22 * 1024 + 768
    arena = sb("arena", [128, ARENA_BYTES // 2], BF16)
    s2 = sb("s2", [128, S2_BYTES // 2], BF16)
    regions = {"arena": (arena, ARENA_BYTES), "ht": (HTflat, 32 * 1024), "s2": (s2, S2_BYTES),
               "hp": (Hbf, HTILES * D * 4)}
    bump = {}

    def cv(region, phase, shape, dt=BF16):
        t, cap = regions[region]
        off = bump.get((region, phase), 0)
        off = (off + 31) // 32 * 32
        n = int(np.prod(shape[1:]))
        esz = 2 if dt == BF16 else 4
        assert off + n * esz <= cap, (region, phase, off, n * esz, cap)
        bump[(region, phase)] = off + n * esz
        a = t[0:shape[0], off // 2: off // 2 + n * esz // 2]
        if dt != BF16:
            a = a.bitcast(dt)
        if len(shape) == 3:
            a = a.rearrange("p (a b) -> p a b", a=shape[1])
        return a

    Wf = cv("arena", "A", [128, 8, 1536])
    WtA = cv("arena", "A", [128, 8, 1024])
    Ws = cv("arena", "A", [128, 8, 24])
    WTB_OFF = bump[("arena", "A")]
    WtB = cv("arena", "A", [128, 8, 1024])
    Wo_ = cv("arena", "A", [128, 8, 1024])
    Wq_ = cv("arena", "B", [128, 8, 1024])
    Wx_ = cv("arena", "B", [128, 8, 1024])
    Wk_ = cv("arena", "B", [128, 8, 1024])
    MEMT = cv("arena", "B", [128, 8, NMEM])
    KT = cv("arena", "B", [128, 8, NMEM])
    VV = cv("arena", "B", [128, 2, D])
    MHB = cv("arena", "B", [128, D])
    LNG2 = cv("arena", "B", [128, D], F32)
    LNB2 = cv("arena", "B", [128, D], F32)
    W1s, W2s = [], []
    for _ in range(2):
        W1s.append(cv("arena", "C", [128, 8, 1024]))
        W2s.append(cv("arena", "C", [128, 8, 1024]))
    RL2 = [cv("arena", "C", [128, 512], F32) for i in range(2)]
    XS_C = cv("arena", "C", [128, D], F32)

    ident_bf = sb("ident_bf", [128, 128], BF16)
    ident_f = sb("ident_f", [128, 128])
    ones_f = sb("ones_f", [128, 128])
    zeros_f = sb("zeros_f", [128, 128])
    mask_ut = sb("mask_ut", [128, 128], BF16)
    negh = sb("negh", [128, 4])
    LNG = sb("LNG", [128, D])
    LNB = sb("LNB", [128, D])
    CW = sb("CW", [128, 4, 8])
    CB = sb("CB", [128, 8])
    GB = sb("GB", [4, 4])
    GLW = sb("GLW", [16, 256])
    GLB = sb("GLB", [128, 2])
    PM = sb("PM", [128, NT])
    ST6 = sb("ST6", [128, 2, 6])
    MV = sb("MV", [128, 2])
    RS = sb("RS", [128, 4])
    EI = sb("EI", [4, 128])
    EF = sb("EF", [4, 128])
    Z = sb("Z", [4, 256])
    RL = sb("RL", [4, 1])
    D4 = sb("D4", [4, 4])
    SC = sb("SC", [128, 12])
    SM = sb("SM", [128, 16])
    ST6b = sb("ST6b", [128, 6])
    MVb = sb("MVb", [128, 2])
    ST6c = sb("ST6c", [128, 2, 6])
    MVc = sb("MVc", [128, 2])
    RSc = sb("RSc", [128, 4])
    ST6b2 = sb("ST6b2", [128, 2, 6])
    MVb2 = sb("MVb2", [128, 2, 2])
    GLR = sb("GLR", [16, 128])
    MX = sb("MX", [128, 4])
    NB = sb("NB", [128, 4])
    RSUM = sb("RSUM", [128, 4])
    RINVb = [sb(f"RINV{i}", [128, 4]) for i in range(2)]
    XB = [cv("ht", "A", [128, D], F32)]
    HB_A = cv("ht", "A", [128, D])
    XS_A = cv("ht", "A", [128, D], F32)
    RAW = cv("ht", "A", [128, 8, 131], F32)
    CV = cv("ht", "A", [128, 4, 128], F32)
    OGT = cv("ht", "A", [128, 512], F32)
    TH = OGT.rearrange("p (a b) -> p a b", a=4)
    OGb = [cv("ht", "A", [128, D], F32) for _ in range(2)]
    QKb = [cv("s2", "A", [128, 8, 128]), cv("ht", "A", [128, 8, 128])]
    VEb = [cv("s2", "A", [128, 4, 129]), cv("ht", "A", [128, 4, 129])]
    GVb = [cv("s2", "A", [128, 512]), cv("ht", "A", [128, 512])]
    KHb = [cv("s2", "A", [128, 2, 128]), cv("ht", "A", [128, 2, 128])]
    QHb = [cv("s2", "A", [128, 2, 128]), cv("ht", "A", [128, 2, 128])]
    HTt = [cv("s2", "A", [128, 8, 128])]
    YTt = cv("s2", "A", [128, 8, 128])
    KW = cv("s2", "A", [128, 4, 128])
    Cf = cv("s2", "A", [128, 4, 129], F32)
    Cb = cv("s2", "A", [128, 4, 129])
    SW = [cv("s2", "A", [128, 128]) for i in range(2)]
    TT2 = cv("s2", "A", [128, 2, 128], F32)
    TT = TT2[:, 0, :]
    E2 = cv("s2", "A", [128, 2, 128], F32)
    AC = cv("s2", "A", [128, 2, 128], F32)
    AI = cv("s2", "A", [128, 2, 128], F32)
    KHt = cv("s2", "A", [128, 2, 128])
    ATb = cv("s2", "A", [128, 4, 128])
    Sf = cv("s2", "A", [128, 2, 128], F32)
    Sb_ = cv("s2", "A", [128, 2, 128])
    Y = cv("s2", "A", [128, D])
    if NG4 > 0:
        XGall = cv("hp", "P", [128, 2 * D], F32)
        XG = [XGall[:, 0:D], XGall[:, D:2 * D]]
        HTG1 = cv("hp", "P", [128, 8, 512])
        RAWG = cv("hp", "P", [128, 4, 516], F32)
        CVG = cv("hp", "P", [128, 4, 512], F32)
        bump[("arena", "P")] = WTB_OFF
        HTG2 = cv("arena", "P", [128, 8, 512])
        THG = cv("arena", "P", [128, 4, 512], F32)
        QKG = [cv("hp", "P", [128, 4, 512]), cv("arena", "P", [128, 4, 512])]
        VEG = [cv("hp", "P", [128, 16, 129]), cv("arena", "P", [128, 16, 129])]
        GVG = [cv("hp", "P", [128, 4, 512]), cv("arena", "P", [128, 4, 512])]
        KHG = [cv("hp", "P", [128, 2, 512]), cv("arena", "P", [128, 2, 512])]
        E2G = cv("hp", "P", [128, 2, 512], F32)
        BCG = cv("hp", "P", [128, 2, 512], F32)
        ZG = cv("hp", "P", [4, 2, 512], F32)
        EIG = ZG[:, 0, :]
        EFG = cv("hp", "P", [4, 512], F32)
        GLRG = cv("hp", "P", [16, 512], F32)
        RLG = sb("RLG", [4, 4])
        D4G = sb("D4G", [4, 4, 4])
        SUFG = sb("SUFG", [128, 2, 4])
        HTGs = [HTG1, HTG2]
        SCG = [sb(f"SCG{i}", [128, 4, 16]) for i in range(2)]
        ACLG = [sb(f"ACLG{i}", [128, 4, 2]) for i in range(2)]
    SCb = [sb(f"SCb{i}", [128, 16]) for i in range(2)]
    ACL = [sb(f"ACL{i}", [128, 2]) for i in range(2)]
    NGT = sb("NGT", [128, 8])
    XS_B = cv("s2", "B", [128, D], F32)
    HB_B = cv("s2", "B", [128, D])
    HTt_B = cv("s2", "B", [128, 8, 128])
    QT = cv("s2", "B", [128, 8, 128])
    PEX = cv("s2", "B", [128, 4, NMEM])
    PTb = [cv("s2", "B", [128, 8, 128]) for _ in range(2)]
    OB = cv("s2", "B", [128, D])
    XBB = [cv("s2", "B", [128, D], F32)]
    HID = [cv("s2", "C", [128, 8, 512]), ]
    cur = {"XS": None, "HB": HB_A}

    PS = [es.enter_context(nc.psum_tensor(f"ps{i}", [128, 512], F32)) for i in range(8)]
    PSb = [p.bitcast(BF16) for p in PS]
    B_T, B_F0, B_F1, B_K0, B_K1, B_G, B_M0, B_M1 = range(8)

    sems = {}

    def getsem(sk):
        if sk not in sems:
            sems[sk] = es.enter_context(nc.semaphore("s_" + "_".join(str(x) for x in sk)))
        return sems[sk]

    def wload(dst, src_rows_ap, ncols_chunks, key, eng="pool"):
        for (d0, s0, n) in ncols_chunks:
            S.dma(eng,
                  lambda e, d0=d0, s0=s0, n=n: e.dma_start(
                      out=dst[:, :, d0:d0 + n],
                      in_=src_rows_ap.rearrange("(kc p) n -> p kc n", p=128)[:, :, s0:s0 + n]),
                  reads=(), writes=(key,))

    def bcast_load(dst, src1d, n, key):
        S.dma("sp", lambda e: e.dma_start(out=dst, in_=src1d[0:n].partition_broadcast(128)),
              reads=(), writes=(key,))

    def layer_norm(src, dst, keys_r, keys_w, xs=None, xskey="XS", lng=None, lnb=None, gkey="LNG", bkey="LNB",
                   sc=None, sfx=""):
        XS = cur["XS"] if xs is None else xs
        G_ = LNG if lng is None else lng
        B_ = LNB if lnb is None else lnb
        st6, mv, rs = (ST6, MV, RS) if sc is None else sc
        k6, kmv, kr0, kr1 = "ST6" + sfx, "MV" + sfx, "RS0" + sfx, "RS1" + sfx
        S.op("dve", lambda e: (e.bn_stats(out=st6[:, 0, :], in_=src[:, 0:512]),
                               e.bn_stats(out=st6[:, 1, :], in_=src[:, 512:1024]))[-1],
             reads=keys_r, writes=(k6,))
        S.op("dve", lambda e: e.bn_aggr(out=mv[:, :], in_=st6[:, :, :].rearrange("p a b -> p (a b)")),
             reads=(k6,), writes=(kmv,))
        S.op("dve", lambda e: e.tensor_scalar(out=rs[:, 0:1], in0=mv[:, 1:2], scalar1=EPS, scalar2=None,
                                              op0=ALU.add), reads=(kmv,), writes=(kr0,))
        S.op("pool", lambda e: e.tensor_tensor(out=rs[:, 1:2], in0=rs[:, 0:1], in1=negh[:, 0:1], op=ALU.pow),
             reads=(kr0,), writes=(kr1,))
        S.op("dve", lambda e: e.scalar_tensor_tensor(out=XS[:, :], in0=src, scalar=mv[:, 0:1], in1=G_[:, :],
                                                     op0=ALU.subtract, op1=ALU.mult),
             reads=tuple(keys_r) + (kmv, gkey), writes=(xskey,))
        S.op("dve", lambda e: e.scalar_tensor_tensor(out=dst, in0=XS[:, :], scalar=rs[:, 1:2], in1=B_[:, :],
                                                     op0=ALU.mult, op1=ALU.add),
             reads=(xskey, kr1, bkey), writes=keys_w)

    def transpose8(src_bf, src_keys, dst_ap, dst_keys, bank=0, bkey="PS_T"):
        def f(e):
            last = None
            for kc in range(8):
                last = e.transpose(out=PSb[bank][:, kc * 128:(kc + 1) * 128],
                                   in_=src_bf[:, kc * 128:(kc + 1) * 128], identity=ident_bf[:, :])
            return last
        S.op("pe", f, reads=tuple(src_keys), writes=(bkey,))
        S.op("act", lambda e: e.activation(out=dst_ap,
                                           in_=PSb[bank][:, :].rearrange("p (a b) -> p a b", a=8),
                                           func=AF.Copy),
             reads=(bkey,), writes=dst_keys)

    def to_feature_major(src_f32, src_keys, dst_ap, dst_keys, bank=0, bkey="PS_T", hb=None, hbkey="HB"):
        HB = cur["HB"] if hb is None else hb
        S.op("act", lambda e: e.activation(out=HB[:, :], in_=src_f32, func=AF.Copy),
             reads=src_keys, writes=(hbkey,))
        transpose8(HB, (hbkey,), dst_ap, dst_keys, bank, bkey)

    def proj_resid(ti, src_ht, src_keys, W, wkey):
        for half in range(2):
            bank = B_K0 if half == 0 else B_K1
            bkey = "K0" if half == 0 else "K1"
            def f(e, half=half, bank=bank):
                last = None
                for kc in range(8):
                    last = e.matmul(PS[bank][:, 0:512], lhsT=src_ht[:, kc, :],
                                    rhs=W[:, kc, half * 512:(half + 1) * 512], start=(kc == 0), stop=(kc == 7))
                return last
            S.op("pe", f, reads=(wkey,) + tuple(src_keys), writes=(bkey,))
            S.op("dve", lambda e, half=half, bank=bank: e.scalar_tensor_tensor(
                out=H[:, ti, half * 512:(half + 1) * 512], in0=H[:, ti, half * 512:(half + 1) * 512], scalar=ALPHA,
                in1=PS[bank][:, 0:512], op0=ALU.mult, op1=ALU.add), reads=(bkey, f"H{ti}"), writes=(f"H{ti}",))

    def barrier():
        allk = list(S.lastw.keys())
        for eng in ENGS:
            S.fence(eng, allk)

    def finish():
        for e in ENGS:
            for (fn, waits, sk, inc) in S.ops[e]:
                for (wk, val) in waits:
                    getsem(wk)
                if sk is not None:
                    getsem(sk)
        block = es.enter_context(nc.Block())

        class _FirstWait:
            def __init__(self, eng, sem, val):
                self._e, self._s, self._v, self._done = eng, sem, val, False

            def __getattr__(self, name):
                attr = getattr(self._e, name)
                if not callable(attr):
                    return attr

                def f(*a, **k):
                    ins = attr(*a, **k)
                    if not self._done:
                        ins._wait_ge(self._s, self._v)
                        self._done = True
                    return ins
                return f

        EMBED = os.environ.get("EMBED", "1") == "1"

        def emit(engname):
            def body(eng):
                for (fn, waits, sk, inc) in S.ops[engname]:
                    ws = list(waits)
                    emb = None
                    if EMBED and fn is not None and ws and engname in ("act", "dve", "pe", "pool") and sk[0] == "e":
                        emb = ws.pop()
                    for (wk, val) in ws:
                        eng.wait_ge(sems[wk], val)
                    if fn is not None:
                        if emb is not None:
                            ins = fn(_FirstWait(eng, sems[emb[0]], emb[1]))
                        else:
                            ins = fn(eng)
                        ins.then_inc(sems[sk], inc)
            return body

        block.tensor(emit("pe"))
        block.scalar(emit("act"))
        block.vector(emit("dve"))
        block.gpsimd(emit("pool"))
        block.sync(emit("sp"))
        es.close()
        return nc

    def dump_H_and_finish():
        for ti in range(NT_MAIN):
            S.dma("sp", lambda e, ti=ti: e.dma_start(out=out_d[ti * 128:(ti + 1) * 128, :], in_=H[:, ti, :]),
                  reads=(f"H{ti}",), writes=(f"OUT{ti}",))
        S.fence("sp", [f"OUT{t}" for t in range(NT_MAIN)])
        return finish()

    S.op("pool", lambda e: e.memset(ones_f[:, :], 1.0), writes=("c_ones",))
    S.op("pool", lambda e: e.memset(zeros_f[:, :], 0.0), writes=("c_zeros",))
    S.op("pool", lambda e: e.memset(negh[:, :], -0.5), writes=("c_negh",))
    S.op("pool", lambda e: e.affine_select(out=ident_f[:, :], in_=ones_f[:, :], pattern=[[-1, 128]],
                                           compare_op=ALU.is_equal, fill=0.0, base=0, channel_multiplier=1),
         reads=("c_ones",), writes=("c_identf",))
    S.op("pool", lambda e: e.tensor_copy(out=ident_bf[:, :], in_=ident_f[:, :]),
         reads=("c_identf",), writes=("c_identb",))
    S.op("pool", lambda e: e.affine_select(out=TT[:, :], in_=ones_f[:, :], pattern=[[1, 128]],
                                           compare_op=ALU.is_ge, fill=0.0, base=0, channel_multiplier=-1),
         reads=("c_ones",), writes=("TT",))
    S.op("pool", lambda e: e.tensor_copy(out=mask_ut[:, :], in_=TT[:, :]), reads=("TT",), writes=("c_mask",))
    S.op("pool", lambda e: e.memset(VEb[0][:, :, :], 1.0), writes=("VE0",))
    S.op("pool", lambda e: e.memset(VEb[1][:, :, :], 1.0), writes=("VE1",))
    S.op("pool", lambda e: e.memset(Cf[:, :, :], 0.0), writes=("Cf0", "Cf1", "Cf2", "Cf3"))
    S.op("pool", lambda e: e.memset(Cb[:, :, :], 0.0), writes=("Cb0", "Cb1", "Cb2", "Cb3"))
    S.op("pool", lambda e: e.memset(Sf[:, :, :], 0.0), writes=("Sf0", "Sf1"))
    S.op("pool", lambda e: e.memset(Sb_[:, :, :], 0.0), writes=("Sb0", "Sb1"))
    S.op("pool", lambda e: e.memset(RAW[:, :, :], 0.0), writes=("RAWq", "RAWk"))

    bcast_load(LNG[:, :], ln_g["ln_in"], D, "LNG")
    bcast_load(LNB[:, :], ln_b["ln_in"], D, "LNB")
    S.dma("sp", lambda e: e.dma_start(out=NGT[:, 0:4], in_=m_norm_g.rearrange("(c p) -> p c", p=128),
                                      allow_slow_non_contiguous=True), writes=("NGTa",))
    S.dma("sp", lambda e: e.dma_start(out=NGT[:, 4:8], in_=g_norm_g.rearrange("(c p) -> p c", p=128),
                                      allow_slow_non_contiguous=True), writes=("NGTb",))
    S.op("dve", lambda e: e.tensor_scalar(out=NGT[:, :], in0=NGT[:, :], scalar1=0.5, scalar2=None, op0=ALU.mult),
         reads=("NGTa", "NGTb"), writes=("NGT",))
    S.dma("sp", lambda e: e.dma_start(out=PM[:, :], in_=pmask_d[:, :]), writes=("PM",))
    with nc.allow_non_contiguous_dma(reason="tiny param loads"):
        for j in range(4):
            S.dma("sp", lambda e, j=j: e.dma_start(out=CW[:, j, :], in_=conv_w[j, :].rearrange("(c p) -> p c", p=128),
                                                   allow_slow_non_contiguous=True), writes=("CW",))
        S.dma("sp", lambda e: e.dma_start(out=CB[:, :], in_=conv_b.rearrange("(c p) -> p c", p=128), allow_slow_non_contiguous=True),
              writes=("CB",))
        S.dma("sp", lambda e: e.dma_start(out=GB[:, 0:1], in_=m_i_bias.rearrange("(p o) -> p o", o=1), allow_slow_non_contiguous=True),
              writes=("GBa",))
        S.dma("sp", lambda e: e.dma_start(out=GB[:, 1:2], in_=m_f_bias.rearrange("(p o) -> p o", o=1), allow_slow_non_contiguous=True),
              writes=("GBb",))
        S.dma("sp", lambda e: e.dma_start(out=GLB[:, :], in_=g_lr_b.rearrange("(c p) -> p c", p=128), allow_slow_non_contiguous=True),
              writes=("GLBa",))
    S.dma("sp", lambda e: e.dma_start(out=GLW[:, :], in_=g_lr_w[:, :]), writes=("GLW",))
    S.op("dve", lambda e: e.tensor_scalar(out=GB[:, 0:1], in0=GB[:, 0:1],
                                          scalar1=math.log(128.0 ** -0.5 / 4.0), scalar2=None, op0=ALU.add),
         reads=("GBa",), writes=("GB0",))
    S.op("dve", lambda e: e.tensor_scalar(out=GB[:, 1:2], in0=GB[:, 1:2], scalar1=-1.0, scalar2=None,
                                          op0=ALU.mult), reads=("GBb",), writes=("GB1",))
    S.op("dve", lambda e: e.tensor_scalar(out=GLB[:, :], in0=GLB[:, :], scalar1=-1.0, scalar2=None,
                                          op0=ALU.mult), reads=("GLBa",), writes=("GLB",))

    wload(Wf, w_in, [(512, 512, 512), (1024, 2056, 512), (0, 0, 512)], "Wf")
    wload(Ws, w_in, [(0, 2048, 8), (8, 3592, 16)], "Ws")
    wload(WtA, w_in, [(0, 1024, 512), (512, 2568, 512)], "Wt")

    def load_main_weights():
        wload(WtB, w_in, [(0, 1536, 512), (512, 3080, 512)], "WtB")
        wload(Wo_, w_out, [(0, 0, 1024)], "Wo")
    if NG4 == 0:
        load_main_weights()
    pass

    htt = HTt[0]
    hk = "HTt0"

    def ln_A(src, dst, keys_r, keys_w):
        S.op("dve", lambda e: (e.bn_stats(out=ST6[:, 0, :], in_=src[:, 0:512]),
                               e.bn_stats(out=ST6[:, 1, :], in_=src[:, 512:1024]))[-1],
             reads=keys_r, writes=("ST6",))
        S.op("dve", lambda e: e.bn_aggr(out=MV[:, :], in_=ST6[:, :, :].rearrange("p a b -> p (a b)")),
             reads=("ST6",), writes=("MV",))
        S.op("dve", lambda e: e.tensor_scalar(out=RS[:, 0:1], in0=MV[:, 1:2], scalar1=EPS, scalar2=None,
                                              op0=ALU.add), reads=("MV",), writes=("RS0",))
        S.op("pool", lambda e: e.tensor_tensor(out=RS[:, 1:2], in0=RS[:, 0:1], in1=negh[:, 0:1], op=ALU.pow),
             reads=("RS0",), writes=("RS1",))
        S.op("dve", lambda e: e.scalar_tensor_tensor(out=XS_A[:, :], in0=src, scalar=MV[:, 0:1], in1=LNG[:, :],
                                                     op0=ALU.subtract, op1=ALU.mult),
             reads=tuple(keys_r) + ("MV", "LNG"), writes=("XS",))
        S.op("dve", lambda e: e.scalar_tensor_tensor(out=dst, in0=XS_A[:, :], scalar=RS[:, 1:2], in1=LNB[:, :],
                                                     op0=ALU.mult, op1=ALU.add),
             reads=("XS", "RS1", "LNB"), writes=keys_w)

    def stageX(i):
        main = i >= NT_PRE
        ti = i - NT_PRE
        need_q = main or (i == NT_PRE - 1)
        par = i % 2
        xb = XB[0]
        xk = "X0"
        pm = PM[:, i:i + 1]
        QK, VE, GV, KH, QH, SC, OG = QKb[par], VEb[par], GVb[par], KHb[par], QHb[par], SCb[par], OGb[par]
        S.dma("sp", lambda e: e.dma_start(out=xb[:, :], in_=xall[i * 128:(i + 1) * 128, :]), writes=(xk,))
        if main:
            dst = H[:, ti, :]
            dk_ = (f"H{ti}",)
        else:
            dst = xb[:, :]
            dk_ = (xk,)
        ln_A(xb[:, :], dst, (xk,), dk_)
        to_feature_major(dst, dk_, htt[:, :, :], (hk,))

        def proj_feat(bank, col0, nchunk, key):
            def f(e):
                last = None
                for c in range(nchunk):
                    for kc in range(8):
                        last = e.matmul(PS[bank][:, c * 128:(c + 1) * 128],
                                        lhsT=Wf[:, kc, col0 + c * 128: col0 + (c + 1) * 128],
                                        rhs=htt[:, kc, :], start=(kc == 0), stop=(kc == 7))
                return last
            S.op("pe", f, reads=(hk, "Wf"), writes=(key,))

        def proj_tok(bank, g, key):
            W_, c_, wk_ = {0: (WtA, 0, "Wt"), 1: (WtB, 0, "WtB"), 2: (WtA, 512, "Wt"), 3: (WtB, 512, "WtB")}[g]

            def f(e):
                last = None
                for kc in range(8):
                    last = e.matmul(PS[bank][:, 0:512], lhsT=htt[:, kc, :],
                                    rhs=W_[:, kc, c_:c_ + 512], start=(kc == 0), stop=(kc == 7))
                return last
            S.op("pe", f, reads=(hk, wk_), writes=(key,))

        def f_gates(e):
            last = None
            for (o0, n, c0) in ((0, 4, 0), (128, 4, 4)):
                for kc in range(8):
                    last = e.matmul(PS[B_G][0:n, o0:o0 + 128], lhsT=Ws[:, kc, c0:c0 + n], rhs=htt[:, kc, :],
                                    start=(kc == 0), stop=(kc == 7))
            for kc in range(8):
                last = e.matmul(PS[B_G][0:16, 256:384], lhsT=Ws[:, kc, 8:24], rhs=htt[:, kc, :],
                                start=(kc == 0), stop=(kc == 7))
            return last
        S.op("pe", f_gates, reads=(hk, "Ws"), writes=("G_a", "G_b"))
        if need_q:
            proj_feat(B_F0, 0, 4, "F0")
        proj_feat(B_F1, 512, 4, "F1")
        proj_tok(B_K0, 0, "K0")

        S.op("act", lambda e: e.activation(out=EI[:, :], in_=PS[B_G][0:4, 0:128], func=AF.Exp,
                                           bias=GB[:, 0:1], scale=1.0),
             reads=("G_a", "GB0"), writes=("EI",))
        S.op("act", lambda e: e.activation(out=EF[:, :], in_=PS[B_G][0:4, 128:256], func=AF.Exp,
                                           bias=GB[:, 1:2], scale=-1.0),
             reads=("G_a", "GB1"), writes=("EF",))
        S.op("act", lambda e: e.activation(out=GLR[:, :], in_=PS[B_G][0:16, 256:384], func=AF.Copy),
             reads=("G_b",), writes=("GLR",))
        S.op("dve", lambda e: e.tensor_scalar(out=EF[:, :], in0=EF[:, :], scalar1=1.0, scalar2=None, op0=ALU.add),
             reads=("EF",), writes=("EF",))
        S.op("dve", lambda e: e.tensor_tensor_scan(out=Z[:, 128:256], data0=EF[:, :], data1=zeros_f[0:4, :],
                                                   initial=1.0, op0=ALU.mult, op1=ALU.add),
             reads=("EF",), writes=("Zb",))
        S.op("dve", lambda e: e.tensor_tensor(out=Z[:, 0:128], in0=EI[:, :], in1=Z[:, 128:256], op=ALU.mult),
             reads=("EI", "Zb"), writes=("Za",))
        S.op("dve", lambda e: e.reciprocal(out=RL[:, :], in_=Z[:, 255:256]), reads=("Zb",), writes=("RL",))
        S.op("dve", lambda e: e.tensor_scalar(out=D4[:, :], in0=ident_f[0:4, 0:4], scalar1=RL[:, 0:1],
                                              scalar2=None, op0=ALU.mult), reads=("RL",), writes=("D4",))

        def f_sc(e):
            e.matmul(PS[B_G][:, 384:388], lhsT=Z[0:4, 0:128], rhs=ident_f[0:4, 0:4], start=True, stop=True)
            e.matmul(PS[B_G][:, 388:392], lhsT=Z[0:4, 128:256], rhs=ident_f[0:4, 0:4], start=True, stop=True)
            return e.matmul(PS[B_G][:, 392:396], lhsT=ones_f[0:4, :], rhs=D4[0:4, 0:4], start=True, stop=True)
        S.op("pe", f_sc, reads=("Za", "Zb", "D4"), writes=("G_c",))
        S.op("dve", lambda e: e.tensor_copy(out=SC[:, 0:12], in_=PS[B_G][:, 384:396]), reads=("G_c",),
             writes=(f"SC{par}",))
        S.op("dve", lambda e: e.tensor_scalar(out=SC[:, 0:4], in0=SC[:, 0:4], scalar1=pm, scalar2=None,
                                              op0=ALU.mult), reads=(f"SC{par}", "PM"), writes=(f"SC{par}",))
        S.op("dve", lambda e: e.tensor_tensor(out=SC[:, 12:16], in0=SC[:, 0:4], in1=SC[:, 8:12], op=ALU.mult),
             reads=(f"SC{par}",), writes=(f"SC{par}",))

        def f_z(e):
            e.matmul(PS[B_G][:, 0:128], lhsT=GLW[0:16, 0:128], rhs=GLR[0:16, :], start=True, stop=True)
            return e.matmul(PS[B_G][:, 128:256], lhsT=GLW[0:16, 128:256], rhs=GLR[0:16, :], start=True, stop=True)
        S.op("pe", f_z, reads=("GLR", "GLW"), writes=("G_a",))
        S.op("act", lambda e: (e.activation(out=E2[:, 0, :], in_=PS[B_G][:, 0:128], func=AF.Exp,
                                            bias=GLB[:, 0:1], scale=-1.0),
                               e.activation(out=E2[:, 1, :], in_=PS[B_G][:, 128:256], func=AF.Exp,
                                            bias=GLB[:, 1:2], scale=-1.0))[-1],
             reads=("G_a", "GLB"), writes=("E2",))
        S.op("act", lambda e: e.activation(out=E2[:, :, :], in_=E2[:, :, :], func=AF.Ln, bias=1.0, scale=1.0),
             reads=("E2",), writes=("E2",))
        S.op("dve", lambda e: e.tensor_tensor_scan(out=AI[:, 0, :], data0=ones_f[:, :], data1=E2[:, 0, :],
                                                   initial=0.0, op0=ALU.mult, op1=ALU.add),
             reads=("E2",), writes=("BC0",))
        S.op("dve", lambda e: e.tensor_tensor_scan(out=AI[:, 1, :], data0=ones_f[:, :], data1=E2[:, 1, :],
                                                   initial=0.0, op0=ALU.mult, op1=ALU.add),
             reads=("E2",), writes=("BC1",))
        S.op("act", lambda e: e.activation(out=AC[:, :, :], in_=AI[:, :, :], func=AF.Exp, scale=-1.0 / 16.0),
             reads=("BC0", "BC1"), writes=("AC",))
        S.op("act", lambda e: e.activation(out=AI[:, :, :], in_=AI[:, :, :], func=AF.Exp, scale=1.0 / 16.0),
             reads=("BC0", "BC1", "AC"), writes=("AI", "BC0", "BC1"))
        S.op("act", lambda e: e.activation(out=ACL[par][:, :], in_=AC[:, :, 127], func=AF.Copy),
             reads=("AC",), writes=(f"ACL{par}",))

        groups = []
        if need_q:
            groups.append((0, B_F0, "F0", "RAWq"))
        groups.append((4, B_F1, "F1", "RAWk"))
        for (c0, bank, bkey, rk) in groups:
            S.op("act", lambda e, c0=c0, bank=bank: e.activation(
                out=RAW[:, c0:c0 + 4, 3:131], in_=PS[bank][:, :].rearrange("p (a b) -> p a b", a=4),
                func=AF.Identity, scale=pm), reads=(bkey, "PM"), writes=(rk,))
            if c0 == 0:
                proj_feat(B_F0, 1024, 4, "F0")
            else:
                if main:
                    proj_tok(B_F1, 1, "F1")
            if main or c0 == 4:
                for cc in range(4):
                    c = c0 + cc
                    S.op("dve", lambda e, c=c, cc=cc: e.tensor_scalar(out=CV[:, cc, :], in0=RAW[:, c, 3:131],
                                                                      scalar1=CW[:, 3, c:c + 1], scalar2=CB[:, c:c + 1],
                                                                      op0=ALU.mult, op1=ALU.add),
                         reads=(rk, "CW", "CB"), writes=(f"CV{cc}",))
                for j in range(3):
                    for cc in range(4):
                        c = c0 + cc
                        S.op("dve", lambda e, c=c, cc=cc, j=j: e.scalar_tensor_tensor(
                            out=CV[:, cc, :], in0=RAW[:, c, j:j + 128], scalar=CW[:, j, c:c + 1], in1=CV[:, cc, :],
                            op0=ALU.mult, op1=ALU.add), reads=(rk, f"CV{cc}"), writes=(f"CV{cc}",))
                cvk = tuple(f"CV{cc}" for cc in range(4))
                S.op("act", lambda e: e.activation(out=TH[:, :, :], in_=CV[:, :, :], func=AF.Tanh, scale=0.5),
                     reads=cvk, writes=("OGT",))
                S.op("dve", lambda e, c0=c0: e.scalar_tensor_tensor(
                    out=QK[:, c0:c0 + 4, :], in0=TH[:, :, :], scalar=1.0, in1=CV[:, :, :],
                    op0=ALU.add, op1=ALU.mult), reads=("OGT",) + cvk,
                    writes=(f"QKq{par}" if c0 == 0 else f"QKk{par}",))
            S.op("pool", lambda e, c0=c0: e.tensor_copy(out=RAW[:, c0:c0 + 4, 0:3], in_=RAW[:, c0:c0 + 4, 128:131]),
                 reads=(rk,), writes=(rk,))
        if not need_q:
            proj_feat(B_F0, 1024, 4, "F0")

        if main:
            S.op("dve", lambda e: e.scalar_tensor_tensor(
                out=QH[:, :, :], in0=PS[B_F0][:, 0:256].rearrange("p (a b) -> p a b", a=2), scalar=0.125,
                in1=AC[:, :, :], op0=ALU.mult, op1=ALU.mult), reads=("F0", "AC"), writes=(f"QH{par}",))
        S.op("dve", lambda e: e.tensor_tensor(out=KH[:, :, :],
                                              in0=PS[B_F0][:, 256:512].rearrange("p (a b) -> p a b", a=2),
                                              in1=AI[:, :, :], op=ALU.mult), reads=("F0", "AI"), writes=(f"KH{par}",))

        S.op("act", lambda e: e.activation(out=VE[:, :, 0:128],
                                           in_=PS[B_K0][:, :].rearrange("p (a b) -> p a b", a=4), func=AF.Copy),
             reads=("K0",), writes=(f"VE{par}",))
        proj_tok(B_K0, 2, "K0")
        if main:
            S.op("act", lambda e: e.activation(out=OGT[:, :], in_=PS[B_F1][:, :], func=AF.Tanh, scale=0.5),
                 reads=("F1",), writes=("OGT",))
            S.op("dve", lambda e: e.tensor_scalar(out=OG[:, 0:512], in0=OGT[:, :], scalar1=1.0, scalar2=None,
                                                  op0=ALU.add), reads=("OGT",), writes=(f"OGa{par}",))
            proj_tok(B_F1, 3, "F1")
        S.op("act", lambda e: e.activation(out=GV[:, :], in_=PS[B_K0][:, :], func=AF.Copy),
             reads=("K0",), writes=(f"GV{par}",))
        if main:
            S.op("act", lambda e: e.activation(out=OGT[:, :], in_=PS[B_F1][:, :], func=AF.Tanh, scale=0.5),
                 reads=("F1",), writes=("OGT",))
            S.op("dve", lambda e: e.scalar_tensor_tensor(out=OG[:, 512:1024], in0=OGT[:, :], scalar=1.0,
                                                         in1=PS[B_F1][:, :], op0=ALU.add, op1=ALU.mult),
                 reads=("OGT", "F1"), writes=(f"OGb{par}",))

    def stageY(i, grp=None):
        main = i >= NT_PRE
        ti = i - NT_PRE
        par = i % 2
        pm = PM[:, i:i + 1]
        if grp is None:
            QK, VE, GV, KH, QH, SC, OG = QKb[par], VEb[par], GVb[par], KHb[par], QHb[par], SCb[par], OGb[par]
            kq, kk, kve, kgv, kkh, kqh, ksc = (f"QKq{par}", f"QKk{par}", f"VE{par}", f"GV{par}", f"KH{par}",
                                              f"QH{par}", f"SC{par}")
            kT = lambda h: QK[:, 4 + h, :]
            qT = lambda h: QK[:, h, :]
            VEh = lambda h: VE[:, h, :]
            GVh = lambda h: GV[:, h * 128:(h + 1) * 128]
            KHp = lambda p, part=slice(0, 128): KH[part, p, :]
            QHp = lambda p, part=slice(0, 128): QH[part, p, :]
            SCc = lambda j: SC[:, j:j + 1]
            ACLp = lambda p: ACL[par][:, p:p + 1]
            kacl = f"ACL{par}"
        else:
            t = grp
            gp = (i // 4) % 2
            tk = slice(t * 128, (t + 1) * 128)
            kq = kk = f"QKG{gp}"
            kve, kgv, kkh, kqh, ksc, kacl = f"VEG{gp}", f"GVG{gp}", f"KHG{gp}", f"KHG{gp}", f"SCG{gp}", f"ACLG{gp}"
            kT = lambda h: QKG[gp][:, h, tk]
            qT = None
            VEh = lambda h: VEG[gp][:, t * 4 + h, :]
            GVh = lambda h: GVG[gp][:, t, h * 128:(h + 1) * 128]
            KHp = lambda p, part=slice(0, 128): KHG[gp][part, p, tk]
            QHp = None
            SCc = lambda j: SCG[gp][:, t, j:j + 1]
            ACLp = lambda p: ACLG[gp][:, t, p:p + 1]
            OG = None
        if grp is not None:
            SCrow = lambda a, b: SCG[gp][:, t, a:b]

            def f_T(e):
                last = None
                for h in range(4):
                    e.transpose(out=PSb[B_M0][:, h * 128:(h + 1) * 128], in_=kT(h), identity=ident_bf[:, :])
                for p in range(2):
                    last = e.transpose(out=PSb[B_M0][:, 512 + p * 128:512 + (p + 1) * 128], in_=KHp(p),
                                       identity=ident_bf[:, :])
                return last
            S.op("pe", f_T, reads=(kk, kkh), writes=("M0T",))
            S.op("dve", lambda e: e.tensor_tensor(
                out=KW[:, :, :], in0=PSb[B_M0][:, 0:512].rearrange("p (a b) -> p a b", a=4),
                in1=SCrow(12, 16).rearrange("p (h o) -> p h o", o=1).broadcast_to([128, 4, 128]), op=ALU.mult),
                reads=("M0T", ksc), writes=("KW0", "KW1", "KW2", "KW3"))
            S.op("dve", lambda e: e.tensor_scalar(
                out=KHt[:, :, :], in0=PSb[B_M0][:, 512:768].rearrange("p (a b) -> p a b", a=2),
                scalar1=pm, scalar2=None, op0=ALU.mult), reads=("M0T", "PM"), writes=("KHt0", "KHt1"))

            def f_U(e):
                first = (t == 0)
                lastt = (t == 3)
                for h in range(3):
                    e.matmul(PS[B_M1][:, h * 129:(h + 1) * 129], lhsT=KW[:, h, :], rhs=VEh(h),
                             start=(first and h == 0), stop=lastt, skip_group_check=True)
                e.matmul(PS[B_K1][:, 0:129], lhsT=KW[:, 3, :], rhs=VEh(3), start=first, stop=lastt,
                         skip_group_check=True)
                last = None
                for p in range(2):
                    for hh in range(2):
                        part = slice(hh * 64, (hh + 1) * 64)
                        last = e.matmul(PS[B_K1][part, 129 + p * 128:129 + (p + 1) * 128],
                                        lhsT=KHt[:, p, hh * 64:(hh + 1) * 64], rhs=GVh(2 * p + hh),
                                        start=False, stop=lastt, skip_group_check=True)
                return last
            S.op("pe", f_U, reads=("KW0", "KW1", "KW2", "KW3", "KHt0", "KHt1", kve, kgv),
                 writes=("M1U", "K1"))
            if t == 3:
                for h in range(4):
                    U_ = PS[B_K1][:, 0:129] if h == 3 else PS[B_M1][:, h * 129:(h + 1) * 129]
                    S.op("dve", lambda e, h=h, U_=U_: e.scalar_tensor_tensor(
                        out=Cf[:, h, :], in0=Cf[:, h, :], scalar=SCG[gp][:, 0, 8 + h:9 + h], in1=U_,
                        op0=ALU.mult, op1=ALU.add),
                        reads=("K1" if h == 3 else "M1U", ksc, f"Cf{h}"), writes=(f"Cf{h}",))
                S.op("dve", lambda e: e.tensor_tensor(
                    out=Sf[:, :, :], in0=Sf[:, :, :],
                    in1=ACLG[gp][:, 0, :].rearrange("p (h o) -> p h o", o=1).broadcast_to([128, 2, 128]),
                    op=ALU.mult), reads=(kacl, "Sf0", "Sf1"), writes=("Sf0", "Sf1"))
                S.op("dve", lambda e: e.tensor_tensor(
                    out=Sf[:, :, :], in0=Sf[:, :, :],
                    in1=PS[B_K1][:, 129:385].rearrange("p (a b) -> p a b", a=2), op=ALU.add),
                    reads=("K1", "Sf0", "Sf1"), writes=("Sf0", "Sf1"))
                if i == 4 * NG4 - 1:
                    S.op("act", lambda e: e.activation(out=Cb[:, :, :], in_=Cf[:, :, :], func=AF.Copy),
                         reads=("Cf0", "Cf1", "Cf2", "Cf3"), writes=("Cb0", "Cb1", "Cb2", "Cb3"))
                    S.op("act", lambda e: e.activation(out=Sb_[:, :, :], in_=Sf[:, :, :], func=AF.Copy),
                         reads=("Sf0", "Sf1"), writes=("Sb0", "Sb1"))
            return

        if not main:
            if grp is None:
                SCrow = lambda a, b: SC[:, a:b]
                ACLrow = ACL[par][:, 0:2]
            else:
                SCrow = lambda a, b: SCG[gp][:, t, a:b]
                ACLrow = ACLG[gp][:, t, :]

            def f_T(e):
                last = None
                for h in range(4):
                    e.transpose(out=PSb[B_M0][:, h * 128:(h + 1) * 128], in_=kT(h), identity=ident_bf[:, :])
                for p in range(2):
                    last = e.transpose(out=PSb[B_K1][:, p * 128:(p + 1) * 128], in_=KHp(p), identity=ident_bf[:, :])
                return last
            S.op("pe", f_T, reads=(kk, kkh), writes=("M0T", "K1"))
            S.op("dve", lambda e: e.tensor_tensor(
                out=KW[:, :, :], in0=PSb[B_M0][:, 0:512].rearrange("p (a b) -> p a b", a=4),
                in1=SCrow(12, 16).rearrange("p (h o) -> p h o", o=1).broadcast_to([128, 4, 128]), op=ALU.mult),
                reads=("M0T", ksc), writes=("KW0", "KW1", "KW2", "KW3"))
            S.op("dve", lambda e: e.tensor_scalar(
                out=KHt[:, :, :], in0=PSb[B_K1][:, 0:256].rearrange("p (a b) -> p a b", a=2),
                scalar1=pm, scalar2=None, op0=ALU.mult), reads=("K1", "PM"), writes=("KHt0", "KHt1"))

            def f_U(e):
                e.matmul(PS[B_M0][:, 256:385], lhsT=KW[:, 0, :], rhs=VEh(0), start=True, stop=True)
                for h in range(1, 4):
                    e.matmul(PS[B_M1][:, (h - 1) * 129:h * 129], lhsT=KW[:, h, :], rhs=VEh(h), start=True, stop=True)
                last = None
                for p in range(2):
                    for hh in range(2):
                        part = slice(hh * 64, (hh + 1) * 64)
                        last = e.matmul(PS[B_K1][part, 128 + p * 128:128 + (p + 1) * 128],
                                        lhsT=KHt[:, p, hh * 64:(hh + 1) * 64], rhs=GVh(2 * p + hh),
                                        start=True, stop=True)
                return last
            S.op("pe", f_U, reads=("KW0", "KW1", "KW2", "KW3", "KHt0", "KHt1", kve, kgv),
                 writes=("M0U", "M1U", "K1"))
            for h in range(4):
                U_ = PS[B_M0][:, 256:385] if h == 0 else PS[B_M1][:, (h - 1) * 129:h * 129]
                S.op("dve", lambda e, h=h, U_=U_: e.scalar_tensor_tensor(
                    out=Cf[:, h, :], in0=Cf[:, h, :], scalar=SCc(8 + h), in1=U_, op0=ALU.mult, op1=ALU.add),
                    reads=("M0U" if h == 0 else "M1U", ksc, f"Cf{h}"), writes=(f"Cf{h}",))
            S.op("act", lambda e: e.activation(out=Cb[:, :, :], in_=Cf[:, :, :], func=AF.Copy),
                 reads=("Cf0", "Cf1", "Cf2", "Cf3"), writes=("Cb0", "Cb1", "Cb2", "Cb3"))
            S.op("dve", lambda e: e.tensor_tensor(
                out=Sf[:, :, :], in0=Sf[:, :, :], in1=PS[B_K1][:, 128:384].rearrange("p (a b) -> p a b", a=2),
                op=ALU.add), reads=("K1", "Sf0", "Sf1"), writes=("Sf0", "Sf1"))
            S.op("dve", lambda e: e.tensor_tensor(
                out=Sf[:, :, :], in0=Sf[:, :, :],
                in1=ACLrow.rearrange("p (h o) -> p h o", o=1).broadcast_to([128, 2, 128]), op=ALU.mult),
                reads=(kacl, "Sf0", "Sf1"), writes=("Sf0", "Sf1"))
            S.op("act", lambda e: e.activation(out=Sb_[:, :, :], in_=Sf[:, :, :], func=AF.Copy),
                 reads=("Sf0", "Sf1"), writes=("Sb0", "Sb1"))
            return

        assert main
        bc = lambda ap, n, w: ap.rearrange("p (h o) -> p h o", o=1).broadcast_to([128, n, w])
        mask2 = mask_ut[:, :].rearrange("p (o l) -> p o l", o=1).broadcast_to([128, 2, 128])
        for pr in range(2):
            ha, hb = 2 * pr, 2 * pr + 1

            def f1(e, ha=ha, hb=hb):
                e.matmul(PS[B_M0][:, 0:128], lhsT=kT(ha), rhs=qT(ha), start=True, stop=True)
                e.matmul(PS[B_M0][:, 128:256], lhsT=kT(hb), rhs=qT(hb), start=True, stop=True)
                e.transpose(out=PSb[B_M0][:, 512:640], in_=kT(ha), identity=ident_bf[:, :])
                return e.transpose(out=PSb[B_M0][:, 640:768], in_=kT(hb), identity=ident_bf[:, :])
            S.op("pe", f1, reads=(kq, kk), writes=("M0",))
            for j, h in enumerate((ha, hb)):
                S.op("dve", lambda e, j=j, h=h: e.scalar_tensor_tensor(
                    out=SW[j][:, :], in0=PS[B_M0][:, j * 128:(j + 1) * 128], scalar=SCc(h), in1=mask_ut[:, :],
                    op0=ALU.mult, op1=ALU.mult), reads=("M0", ksc), writes=(f"SW{j}",))
            S.op("dve", lambda e, ha=ha: e.tensor_tensor(
                out=KW[:, ha:ha + 2, :], in0=PSb[B_M0][:, 512:768].rearrange("p (a b) -> p a b", a=2),
                in1=bc(SC[:, 12 + ha:14 + ha], 2, 128), op=ALU.mult), reads=("M0", ksc), writes=(f"KW{ha}", f"KW{hb}"))

            def f2(e, ha=ha, hb=hb):
                for j, h in enumerate((ha, hb)):
                    e.matmul(PS[B_M1][:, j * 129:(j + 1) * 129], lhsT=SW[j][:, :], rhs=VEh(h), start=True, stop=False)
                    e.matmul(PS[B_M1][:, j * 129:(j + 1) * 129], lhsT=qT(h), rhs=Cb[:, h, :], start=False, stop=True)
                last = None
                for j, h in enumerate((ha, hb)):
                    last = e.matmul(PS[B_K1][:, j * 129:(j + 1) * 129], lhsT=KW[:, h, :], rhs=VEh(h),
                                    start=True, stop=True)
                return last
            S.op("pe", f2, reads=("SW0", "SW1", kve, kq, f"Cb{ha}", f"Cb{hb}", f"KW{ha}", f"KW{hb}"),
                 writes=("M1", "K1"))
            for j, h in enumerate((ha, hb)):
                S.op("dve", lambda e, j=j, h=h: e.scalar_tensor_tensor(
                    out=Cf[:, h, :], in0=Cf[:, h, :], scalar=SCc(8 + h), in1=PS[B_K1][:, j * 129:(j + 1) * 129],
                    op0=ALU.mult, op1=ALU.add), reads=("K1", ksc, f"Cf{h}"), writes=(f"Cf{h}",))
            S.op("act", lambda e, ha=ha: e.activation(out=Cb[:, ha:ha + 2, :], in_=Cf[:, ha:ha + 2, :], func=AF.Copy),
                 reads=(f"Cf{ha}", f"Cf{hb}"), writes=(f"Cb{ha}", f"Cb{hb}"))
            Pv = PS[B_M1][:, 0:258].rearrange("p (h c) -> p h c", c=129)
            den = Pv[:, :, 128]
            S.op("dve", lambda e, ha=ha: e.tensor_tensor(out=SM[:, 0:2], in0=den, in1=SC[:, 4 + ha:6 + ha], op=ALU.max),
                 reads=("M1", ksc), writes=("SMa",))
            S.op("dve", lambda e: e.scalar_tensor_tensor(out=SM[:, 2:4], in0=den, scalar=-1.0, in1=SM[:, 0:2],
                                                         op0=ALU.mult, op1=ALU.max), reads=("M1", "SMa"), writes=("SMb",))
            S.op("dve", lambda e: e.reciprocal(out=SM[:, 4:6], in_=SM[:, 2:4]), reads=("SMb",), writes=("SMc",))
            for j in range(2):
                S.op("dve", lambda e, j=j: e.bn_stats(out=ST6b2[:, j, :], in_=Pv[:, j, 0:128]), reads=("M1",),
                     writes=(f"ST6b{j}",))
                S.op("dve", lambda e, j=j: e.bn_aggr(out=MVb2[:, j, :], in_=ST6b2[:, j, :]), reads=(f"ST6b{j}",),
                     writes=(f"MVb{j}",))
            S.op("dve", lambda e: e.tensor_tensor(out=SM[:, 6:8], in0=SM[:, 4:6], in1=SM[:, 4:6], op=ALU.mult),
                 reads=("SMc",), writes=("SMd",))
            S.op("dve", lambda e: e.tensor_tensor(out=SM[:, 8:10], in0=MVb2[:, :, 1], in1=SM[:, 6:8], op=ALU.mult),
                 reads=("MVb0", "MVb1", "SMd"), writes=("SMe",))
            S.op("dve", lambda e: e.tensor_scalar(out=SM[:, 8:10], in0=SM[:, 8:10], scalar1=EPS, scalar2=None,
                                                  op0=ALU.add), reads=("SMe",), writes=("SMe",))
            S.op("pool", lambda e: e.tensor_tensor(out=SM[:, 10:12], in0=SM[:, 8:10], in1=negh[:, 0:2], op=ALU.pow),
                 reads=("SMe",), writes=("SMf",))
            S.op("dve", lambda e: e.tensor_tensor(out=SM[:, 12:14], in0=SM[:, 10:12], in1=SM[:, 4:6], op=ALU.mult),
                 reads=("SMf", "SMc"), writes=("SMg",))
            for j in range(2):
                S.op("dve", lambda e, j=j: e.tensor_scalar(out=TT2[:, j, :], in0=Pv[:, j, 0:128], scalar1=MVb2[:, j, 0:1],
                                                           scalar2=SM[:, 12 + j:13 + j], op0=ALU.subtract, op1=ALU.mult),
                     reads=("M1", f"MVb{j}", "SMg"), writes=(f"TT2{j}",))
            S.op("dve", lambda e, ha=ha: e.tensor_tensor(
                out=Y[:, ha * 128:(ha + 2) * 128], in0=TT2[:, :, :].rearrange("p a b -> p (a b)"),
                in1=OG[:, ha * 128:(ha + 2) * 128], op=ALU.mult),
                reads=("TT20", "TT21", f"OGa{par}"), writes=(f"Y{ha}", f"Y{hb}"))

        for p in range(2):
            h0, h1 = 2 * p, 2 * p + 1

            def g1(e, p=p):
                e.matmul(PS[B_M0][:, 0:128], lhsT=KHp(p, slice(0, 64)), rhs=QHp(p, slice(0, 64)), start=True, stop=True)
                e.matmul(PS[B_M1][:, 128:256], lhsT=KHp(p, slice(64, 128)), rhs=QHp(p, slice(64, 128)),
                         start=True, stop=True)
                return e.transpose(out=PSb[B_M0][:, 512:640], in_=KHp(p), identity=ident_bf[:, :])
            S.op("pe", g1, reads=(kkh, kqh), writes=("M0", "M1"))
            S.op("dve", lambda e, h0=h0: e.tensor_tensor(out=ATb[:, h0, :], in0=PS[B_M0][:, 0:128], in1=mask_ut[:, :],
                                                         op=ALU.mult), reads=("M0",), writes=(f"ATb{h0}",))
            S.op("dve", lambda e, h1=h1: e.tensor_tensor(out=ATb[:, h1, :], in0=PS[B_M1][:, 128:256], in1=mask_ut[:, :],
                                                         op=ALU.mult), reads=("M1",), writes=(f"ATb{h1}",))
            S.op("dve", lambda e, p=p: e.tensor_scalar(out=KHt[:, p, :], in0=PSb[B_M0][:, 512:640], scalar1=pm,
                                                       scalar2=None, op0=ALU.mult), reads=("M0", "PM"), writes=(f"KHt{p}",))

            def g2(e, p=p, h0=h0):
                e.matmul(PS[B_M1][:, 0:128], lhsT=ATb[:, h0, :], rhs=GVh(h0), start=True, stop=False)
                e.matmul(PS[B_M1][:, 0:128], lhsT=QHp(p, slice(0, 64)), rhs=Sb_[0:64, p, :], start=False, stop=True)
                e.matmul(PS[B_K1][:, 0:128], lhsT=ATb[:, h0 + 1, :], rhs=GVh(h0 + 1), start=True, stop=False)
                e.matmul(PS[B_K1][:, 0:128], lhsT=QHp(p, slice(64, 128)), rhs=Sb_[64:128, p, :], start=False, stop=True)
                last = None
                for hh in range(2):
                    part = slice(hh * 64, (hh + 1) * 64)
                    last = e.matmul(PS[B_K1][part, 128:256], lhsT=KHt[:, p, hh * 64:(hh + 1) * 64], rhs=GVh(h0 + hh),
                                    start=True, stop=True)
                return last
            S.op("pe", g2, reads=(f"ATb{h0}", f"ATb{h1}", kgv, kqh, f"Sb{p}", f"KHt{p}"), writes=("M1", "K1"))
            Ovs = [PS[B_M1][:, 0:128], PS[B_K1][:, 0:128]]
            Okeys = ["M1", "K1"]
            for j in range(2):
                S.op("dve", lambda e, j=j: e.bn_stats(out=ST6b2[:, j, :], in_=Ovs[j]), reads=(Okeys[j],),
                     writes=(f"ST6b{j}",))
                S.op("dve", lambda e, j=j: e.bn_aggr(out=MVb2[:, j, :], in_=ST6b2[:, j, :]), reads=(f"ST6b{j}",),
                     writes=(f"MVb{j}",))
            S.op("dve", lambda e: e.tensor_tensor(out=SM[:, 0:2], in0=MVb2[:, :, 0], in1=MVb2[:, :, 0], op=ALU.mult),
                 reads=("MVb0", "MVb1"), writes=("SMa",))
            S.op("dve", lambda e: e.scalar_tensor_tensor(out=SM[:, 2:4], in0=SM[:, 0:2], scalar=EPS, in1=MVb2[:, :, 1],
                                                         op0=ALU.add, op1=ALU.add),
                 reads=("SMa", "MVb0", "MVb1"), writes=("SMb",))
            S.op("pool", lambda e: e.tensor_tensor(out=SM[:, 4:6], in0=SM[:, 2:4], in1=negh[:, 0:2], op=ALU.pow),
                 reads=("SMb",), writes=("SMc",))
            for j in range(2):
                h = h0 + j
                S.op("dve", lambda e, j=j, h=h: e.scalar_tensor_tensor(
                    out=Y[:, 512 + h * 128:512 + (h + 1) * 128], in0=Ovs[j], scalar=SM[:, 4 + j:5 + j],
                    in1=OG[:, 512 + h * 128:512 + (h + 1) * 128], op0=ALU.mult, op1=ALU.mult),
                    reads=(Okeys[j], "SMc", f"OGb{par}"), writes=(f"Y{4 + h}",))
            S.op("dve", lambda e, p=p: e.tensor_tensor(out=Sf[:, p, :], in0=Sf[:, p, :], in1=PS[B_K1][:, 128:256],
                                                       op=ALU.add), reads=("K1", f"Sf{p}"), writes=(f"Sf{p}",))
            S.op("dve", lambda e, p=p: e.tensor_scalar(out=Sf[:, p, :], in0=Sf[:, p, :], scalar1=ACLp(p), scalar2=None,
                                                       op0=ALU.mult), reads=(kacl, f"Sf{p}"), writes=(f"Sf{p}",))
            S.op("act", lambda e, p=p: e.activation(out=Sb_[:, p, :], in_=Sf[:, p, :], func=AF.Copy),
                 reads=(f"Sf{p}",), writes=(f"Sb{p}",))

        if main:
            yk = tuple(f"Y{j}" for j in range(8))

            def f_t(e):
                last = None
                for kc in range(8):
                    last = e.transpose(out=PSb[B_K1][:, kc * 128:(kc + 1) * 128],
                                       in_=Y[:, kc * 128:(kc + 1) * 128], identity=ident_bf[:, :])
                return last
            S.op("pe", f_t, reads=yk, writes=("K1",))

            def f_e(e):
                last = None
                for kc in range(8):
                    last = e.activation(out=YTt[:, kc, :], in_=PSb[B_K1][:, kc * 128:(kc + 1) * 128],
                                        func=AF.Identity, scale=NGT[:, kc:kc + 1])
                return last
            S.op("act", f_e, reads=("K1", "NGT"), writes=("YTt",))
            for half in range(2):
                bank = B_M0 if half == 0 else B_M1
                bkey = "M0" if half == 0 else "M1"

                def f(e, half=half, bank=bank):
                    last = None
                    for kc in range(8):
                        last = e.matmul(PS[bank][:, 0:512], lhsT=YTt[:, kc, :],
                                        rhs=Wo_[:, kc, half * 512:(half + 1) * 512], start=(kc == 0), stop=(kc == 7))
                    return last
                S.op("pe", f, reads=("Wo", "YTt"), writes=(bkey,))
                S.op("dve", lambda e, half=half, bank=bank: e.scalar_tensor_tensor(
                    out=H[:, ti, half * 512:(half + 1) * 512], in0=H[:, ti, half * 512:(half + 1) * 512],
                    scalar=ALPHA, in1=PS[bank][:, 0:512], op0=ALU.mult, op1=ALU.add),
                    reads=(bkey, f"H{ti}"), writes=(f"H{ti}",))

    def groupL(g):
        i0 = 4 * g
        gp = g % 2
        HTG = HTGs[gp]
        for t in range(4):
            i = i0 + t
            xb = XG[t % 2]
            xk = f"XG{t % 2}"
            S.dma("sp", lambda e, xb=xb, i=i: e.dma_start(out=xb[:, :], in_=xall[i * 128:(i + 1) * 128, :]),
                  writes=(xk,))
            ln_A(xb[:, :], xb[:, :], (xk,), (xk,))
            to_feature_major(xb[:, :], (xk,), HTG[:, :, t * 128:(t + 1) * 128], (f"HTG{gp}_{t}",))

    def groupX(g):
        i0 = 4 * g
        gp = g % 2
        HTG = HTGs[gp]
        pm = PM[:, i0:i0 + 1]
        htk = tuple(f"HTG{gp}_{t}" for t in range(4))

        def mm512(out_ap, wsel, M=None):
            def f(e):
                last = None
                for kc in range(8):
                    last = e.matmul(out_ap, lhsT=wsel(kc), rhs=HTG[:, kc, :], start=(kc == 0), stop=(kc == 7))
                return last
            return f
        S.op("pe", mm512(PS[B_G][0:4, 0:512], lambda kc: Ws[:, kc, 0:4]), reads=htk + ("Ws",), writes=("G_a",))
        S.op("pe", mm512(PS[B_K0][0:4, 0:512], lambda kc: Ws[:, kc, 4:8]), reads=htk + ("Ws",), writes=("K0",))
        S.op("pe", mm512(PS[B_F0][0:16, 0:512], lambda kc: Ws[:, kc, 8:24]), reads=htk + ("Ws",), writes=("F0",))
        S.op("pe", mm512(PS[B_F1][:, 0:512], lambda kc: Wf[:, kc, 512 + 128: 512 + 256]),
             reads=htk + ("Wf",), writes=("F1",))
        S.op("act", lambda e: e.activation(out=ZG[:, 0, :], in_=PS[B_G][0:4, 0:512], func=AF.Exp,
                                           bias=GB[:, 0:1], scale=1.0), reads=("G_a", "GB0", "ZGa"), writes=("EIG", "ZGa"))
        S.op("act", lambda e: e.activation(out=EFG[:, :], in_=PS[B_K0][0:4, 0:512], func=AF.Exp,
                                           bias=GB[:, 1:2], scale=-1.0), reads=("K0", "GB1"), writes=("EFG",))
        S.op("act", lambda e: e.activation(out=GLRG[:, :], in_=PS[B_F0][0:16, 0:512], func=AF.Copy),
             reads=("F0",), writes=("GLRG",))
        for p in range(2):
            bank = B_G if p == 0 else B_K0
            bkey = "G_a" if p == 0 else "K0"
            S.op("pe", lambda e, p=p, bank=bank: e.matmul(PS[bank][:, 0:512], lhsT=GLW[0:16, p * 128:(p + 1) * 128],
                                                          rhs=GLRG[0:16, :], start=True, stop=True),
                 reads=("GLRG", "GLW"), writes=(bkey,))
            S.op("act", lambda e, p=p, bank=bank: e.activation(out=E2G[:, p, :], in_=PS[bank][:, 0:512], func=AF.Exp,
                                                               bias=GLB[:, p:p + 1], scale=-1.0),
                 reads=(bkey, "GLB"), writes=(f"E2G{p}",))
        S.op("act", lambda e: e.activation(out=E2G[:, :, :], in_=E2G[:, :, :], func=AF.Ln, bias=1.0, scale=1.0),
             reads=("E2G0", "E2G1"), writes=("E2G0", "E2G1"))
        S.op("dve", lambda e: e.tensor_scalar(out=EFG[:, :], in0=EFG[:, :], scalar1=1.0, scalar2=None, op0=ALU.add),
             reads=("EFG",), writes=("EFG",))
        for t in range(4):
            tk = slice(t * 128, (t + 1) * 128)
            S.op("dve", lambda e, tk=tk: e.tensor_tensor_scan(out=ZG[:, 1, tk], data0=EFG[:, tk],
                                                              data1=zeros_f[0:4, :], initial=1.0,
                                                              op0=ALU.mult, op1=ALU.add),
                 reads=("EFG",), writes=(f"ZGb{t}",))
        zgb = tuple(f"ZGb{t}" for t in range(4))
        S.op("dve", lambda e: e.tensor_tensor(out=ZG[:, 0, :], in0=ZG[:, 0, :], in1=ZG[:, 1, :], op=ALU.mult),
             reads=("EIG",) + zgb, writes=("ZGa", "EIG"))
        S.op("dve", lambda e: e.reciprocal(out=RLG[:, :],
                                           in_=ZG[:, 1, :].rearrange("p (t l) -> p t l", l=128)[:, :, 127]),
             reads=zgb, writes=("RLG",))
        for t in (2, 1, 0):
            S.op("dve", lambda e, t=t: e.tensor_tensor(out=RLG[:, t:t + 1], in0=RLG[:, t:t + 1],
                                                       in1=RLG[:, t + 1:t + 2], op=ALU.mult),
                 reads=("RLG",), writes=("RLG",))
        for t in range(4):
            tk = slice(t * 128, (t + 1) * 128)
            S.op("dve", lambda e, t=t: e.tensor_scalar(out=D4G[:, t, :], in0=ident_f[0:4, 0:4], scalar1=RLG[:, t:t + 1],
                                                       scalar2=None, op0=ALU.mult), reads=("RLG",), writes=(f"D4G{t}",))

            def f_sc(e, t=t, tk=tk):
                o = t * 12
                e.matmul(PS[B_G][:, o:o + 4], lhsT=ZG[0:4, 0, tk], rhs=ident_f[0:4, 0:4], start=True, stop=True)
                e.matmul(PS[B_G][:, o + 4:o + 8], lhsT=ZG[0:4, 1, tk], rhs=ident_f[0:4, 0:4], start=True, stop=True)
                return e.matmul(PS[B_G][:, o + 8:o + 12], lhsT=ones_f[0:4, :], rhs=D4G[0:4, t, :],
                                start=True, stop=True)
            S.op("pe", f_sc, reads=("ZGa", f"D4G{t}", "E2G0") + zgb, writes=("G_c",))
        S.op("dve", lambda e: e.tensor_copy(out=SCG[gp][:, :, 0:12],
                                            in_=PS[B_G][:, 0:48].rearrange("p (a b) -> p a b", a=4)),
             reads=("G_c",), writes=(f"SCG{gp}",))
        S.op("dve", lambda e: e.tensor_scalar(out=SCG[gp][:, :, 0:4], in0=SCG[gp][:, :, 0:4], scalar1=pm, scalar2=None,
                                              op0=ALU.mult), reads=(f"SCG{gp}", "PM"), writes=(f"SCG{gp}",))
        S.op("dve", lambda e: e.tensor_tensor(out=SCG[gp][:, :, 12:16], in0=SCG[gp][:, :, 0:4],
                                              in1=SCG[gp][:, :, 8:12], op=ALU.mult),
             reads=(f"SCG{gp}",), writes=(f"SCG{gp}",))
        for p in range(2):
            for t in range(4):
                tk = slice(t * 128, (t + 1) * 128)
                S.op("dve", lambda e, p=p, tk=tk: e.tensor_tensor_scan(out=BCG[:, p, tk], data0=ones_f[:, :],
                                                                       data1=E2G[:, p, tk], initial=0.0,
                                                                       op0=ALU.mult, op1=ALU.add),
                     reads=(f"E2G{p}",), writes=(f"BCG{p}{t}",))
        bck = tuple(f"BCG{p}{t}" for p in range(2) for t in range(4))
        S.op("dve", lambda e: e.tensor_copy(out=SUFG[:, :, :],
                                            in_=BCG[:, :, :].rearrange("p a (t l) -> p a t l", l=128)[:, :, :, 127]),
             reads=bck, writes=("SUFG",))
        for t in (2, 1, 0):
            S.op("dve", lambda e, t=t: e.tensor_tensor(out=SUFG[:, :, t], in0=SUFG[:, :, t], in1=SUFG[:, :, t + 1],
                                                       op=ALU.add), reads=("SUFG",), writes=("SUFG",))
        S.op("dve", lambda e: e.tensor_scalar(out=SUFG[:, :, :], in0=SUFG[:, :, :], scalar1=-1.0 / 16.0, scalar2=None,
                                              op0=ALU.mult), reads=("SUFG",), writes=("SUFG",))

        def f_ai(e):
            last = None
            for p in range(2):
                for t in range(4):
                    tk = slice(t * 128, (t + 1) * 128)
                    last = e.activation(out=E2G[:, p, tk], in_=BCG[:, p, tk], func=AF.Exp,
                                        bias=SUFG[:, p, t:t + 1], scale=1.0 / 16.0)
            return last
        S.op("act", f_ai, reads=bck + ("SUFG",), writes=("E2G0", "E2G1", "AIG"))
        S.op("act", lambda e: e.activation(out=ACLG[gp][:, :, :].rearrange("p t a -> p a t"), in_=SUFG[:, :, :],
                                           func=AF.Exp), reads=("SUFG",), writes=(f"ACLG{gp}",))
        for cp0 in (0, 2):
            for c in (cp0, cp0 + 1):
                bank = B_F0 if c % 2 == 0 else B_F1
                bkey = "F0" if c % 2 == 0 else "F1"
                if c != 1:
                    S.op("pe", mm512(PS[bank][:, 0:512], lambda kc, c=c: Wf[:, kc, 512 + c * 128: 512 + (c + 1) * 128]),
                         reads=htk + ("Wf",), writes=(bkey,))
                S.op("act", lambda e, c=c, bank=bank: e.activation(out=RAWG[:, c, 3:515], in_=PS[bank][:, 0:512],
                                                                   func=AF.Identity, scale=pm),
                     reads=(bkey, "PM"), writes=(f"RAWG{c}",))
            for c in (cp0, cp0 + 1):
                cg = 4 + c
                S.op("dve", lambda e, c=c, cg=cg: e.tensor_scalar(out=CVG[:, c, :], in0=RAWG[:, c, 3:515],
                                                                  scalar1=CW[:, 3, cg:cg + 1], scalar2=CB[:, cg:cg + 1],
                                                                  op0=ALU.mult, op1=ALU.add),
                     reads=(f"RAWG{c}", "CW", "CB"), writes=(f"CVG{c}",))
            for j in range(3):
                for c in (cp0, cp0 + 1):
                    cg = 4 + c
                    S.op("dve", lambda e, c=c, cg=cg, j=j: e.scalar_tensor_tensor(
                        out=CVG[:, c, :], in0=RAWG[:, c, j:j + 512], scalar=CW[:, j, cg:cg + 1], in1=CVG[:, c, :],
                        op0=ALU.mult, op1=ALU.add), reads=(f"RAWG{c}", f"CVG{c}"), writes=(f"CVG{c}",))
            for c in (cp0, cp0 + 1):
                S.op("pool", lambda e, c=c: e.tensor_copy(out=RAWG[:, c, 0:3], in_=RAWG[:, c, 512:515]),
                     reads=(f"RAWG{c}",), writes=(f"RAWG{c}",))
        cvk = tuple(f"CVG{c}" for c in range(4))
        S.op("act", lambda e: e.activation(out=THG[:, :, :], in_=CVG[:, :, :], func=AF.Tanh, scale=0.5),
             reads=cvk, writes=("THG",))
        S.op("dve", lambda e: e.scalar_tensor_tensor(out=QKG[gp][:, :, :], in0=THG[:, :, :], scalar=1.0, in1=CVG[:, :, :],
                                                     op0=ALU.add, op1=ALU.mult), reads=("THG",) + cvk, writes=(f"QKG{gp}",))
        if g == NG4 - 1:
            def f_mq(e):
                last = None
                for c in range(4):
                    for kc in range(8):
                        last = e.matmul(PS[B_F0][:, c * 128:(c + 1) * 128], lhsT=Wf[:, kc, c * 128:(c + 1) * 128],
                                        rhs=HTG[:, kc, 384:512], start=(kc == 0), stop=(kc == 7))
                return last
            S.op("pe", f_mq, reads=htk + ("Wf",), writes=("F0",))
            S.op("act", lambda e: e.activation(out=RAW[:, 0:4, 3:131],
                                               in_=PS[B_F0][:, :].rearrange("p (a b) -> p a b", a=4),
                                               func=AF.Identity, scale=pm), reads=("F0", "PM"), writes=("RAWq",))
            S.op("pool", lambda e: e.tensor_copy(out=RAW[:, 0:4, 0:3], in_=RAW[:, 0:4, 128:131]),
                 reads=("RAWq",), writes=("RAWq",))
        for t in range(4):
            tk = slice(t * 128, (t + 1) * 128)
            bank = B_K0 if t % 2 == 0 else B_G
            bkey = "K0" if t % 2 == 0 else "G_a"

            def f(e, tk=tk, bank=bank):
                last = None
                for kc in range(8):
                    last = e.matmul(PS[bank][:, 0:512], lhsT=HTG[:, kc, tk], rhs=WtA[:, kc, 0:512],
                                    start=(kc == 0), stop=(kc == 7))
                return last
            S.op("pe", f, reads=htk + ("Wt",), writes=(bkey,))
            S.op("act", lambda e, t=t, bank=bank: e.activation(
                out=VEG[gp][:, 4 * t:4 * t + 4, 0:128], in_=PS[bank][:, :].rearrange("p (a b) -> p a b", a=4),
                func=AF.Copy), reads=(bkey,), writes=(f"VEG{gp}",))
        for t in range(4):
            tk = slice(t * 128, (t + 1) * 128)
            bank = B_F0 if t % 2 == 0 else B_F1
            bkey = "F0" if t % 2 == 0 else "F1"

            def f(e, tk=tk, bank=bank):
                last = None
                for kc in range(8):
                    last = e.matmul(PS[bank][:, 0:512], lhsT=HTG[:, kc, tk], rhs=WtA[:, kc, 512:1024],
                                    start=(kc == 0), stop=(kc == 7))
                return last
            S.op("pe", f, reads=htk + ("Wt",), writes=(bkey,))
            S.op("act", lambda e, t=t, bank=bank: e.activation(out=GVG[gp][:, t, :], in_=PS[bank][:, :], func=AF.Copy),
                 reads=(bkey,), writes=(f"GVG{gp}",))

        for p in range(2):
            bank = B_K0 if p == 0 else B_G
            bkey = "K0" if p == 0 else "G_a"
            S.op("pe", mm512(PS[bank][:, 0:512], lambda kc, p=p: Wf[:, kc, 1280 + p * 128: 1280 + (p + 1) * 128]),
                 reads=htk + ("Wf",), writes=(bkey,))
            S.op("dve", lambda e, p=p, bank=bank: e.tensor_tensor(out=KHG[gp][:, p, :], in0=PS[bank][:, 0:512],
                                                                  in1=E2G[:, p, :], op=ALU.mult),
                 reads=(bkey, "AIG"), writes=(f"KHG{gp}",))

    I_START = 4 * NG4
    if NG4 > 0:
        S.op("pool", lambda e: e.memset(VEG[0][:, :, :], 1.0), writes=("VEG0",))
        S.op("pool", lambda e: e.memset(VEG[1][:, :, :], 1.0), writes=("VEG1",))
        S.op("pool", lambda e: e.memset(RAWG[:, :, :], 0.0), writes=tuple(f"RAWG{c}" for c in range(4)))

        def groupY(g):
            for t in range(4):
                stageY(4 * g + t, grp=t)
        S.replay(S.record(groupL, 0))
        lists = [S.record(groupX, 0)]
        if NG4 > 1:
            lists.append(S.record(groupL, 1))
        S.replay(*lists)
        for g in range(NG4):
            lists = [S.record(groupY, g)]
            if g + 1 < NG4:
                lists.append(S.record(groupX, g + 1))
            if g + 2 < NG4:
                lists.append(S.record(groupL, g + 2))
            mode = os.environ.get("GMODE", "prop")
            if mode == "prop":
                S.replay(*lists)
            elif mode == "seq":
                for l_ in lists:
                    S.replay(l_)
            elif mode == "xfirst":
                if len(lists) > 1:
                    S.replay(lists[1])
                S.replay(lists[0], *lists[2:])
            elif mode == "xl":
                S.replay(*lists[1:])
                S.replay(lists[0])
        S.op("pool", lambda e: e.tensor_copy(out=RAW[:, 4:8, 0:3], in_=RAWG[:, :, 512:515]),
             reads=tuple(f"RAWG{c}" for c in range(4)) + ("RAWk",), writes=("RAWk",))
        barrier()
        load_main_weights()

    PIPE = True
    if PIPE:
        S.replay(S.record(stageX, I_START))
        for i in range(I_START, NT):
            ry = S.record(stageY, i)
            if os.environ.get("YSTOP"):
                ry = ry[:int(os.environ["YSTOP"])]
            if i + 1 < NT:
                S.replay(S.record(stageX, i + 1), ry)
            else:
                S.replay(ry)
    else:
        for i in range(I_START, NT):
            stageX(i)
            stageY(i)

    if stop_after == "A":
        return dump_H_and_finish()
    barrier()
    cur["XS"] = XS_B
    cur["HB"] = HB_B
    wload(Wk_, x_wk, [(0, 0, 1024)], "Wk")
    wload(Wq_, x_wq, [(0, 0, 1024)], "Wq")
    wload(Wx_, x_wo, [(0, 0, 1024)], "Wx")
    bcast_load(LNG[:, :], ln_g["ln1"], D, "LNG")
    bcast_load(LNB[:, :], ln_b["ln1"], D, "LNB")

    def pre_kv():
        for mc in range(2):
            xb = XBB[0]
            S.dma("sp", lambda e, xb=xb, mc=mc: e.dma_start(out=xb[:, :], in_=mem_d[mc * 128:(mc + 1) * 128, :]),
                  writes=("XBB",))
            S.op("act", lambda e, xb=xb: e.activation(out=MHB[:, :], in_=xb[:, :], func=AF.Copy),
                 reads=("XBB",), writes=("MHB",))
            transpose8(MHB, ("MHB",), MEMT[:, :, mc * 128:(mc + 1) * 128], (f"MEMT{mc}",))
        for c in range(8):
            bank = B_K0 if c % 2 == 0 else B_K1
            bkey = "K0" if c % 2 == 0 else "K1"

            def f_k(e, c=c, bank=bank):
                last = None
                for kc in range(8):
                    last = e.matmul(PS[bank][:, 0:256], lhsT=Wk_[:, kc, c * 128:(c + 1) * 128], rhs=MEMT[:, kc, :],
                                    start=(kc == 0), stop=(kc == 7))
                return last
            S.op("pe", f_k, reads=("Wk", "MEMT0", "MEMT1"), writes=(bkey,))
            S.op("act", lambda e, c=c, bank=bank: e.activation(out=KT[:, c, :], in_=PS[bank][:, 0:256], func=AF.Copy),
                 reads=(bkey,), writes=(f"KT{c}",))
        wload(Wk_, x_wv, [(0, 0, 1024)], "Wk")
        for mc in range(2):
            for half in range(2):
                bank = B_K0 if half == 0 else B_K1
                bkey = "K0" if half == 0 else "K1"

                def f_v(e, mc=mc, half=half, bank=bank):
                    last = None
                    for kc in range(8):
                        last = e.matmul(PS[bank][:, 0:512], lhsT=MEMT[:, kc, mc * 128:(mc + 1) * 128],
                                        rhs=Wk_[:, kc, half * 512:(half + 1) * 512], start=(kc == 0), stop=(kc == 7))
                    return last
                S.op("pe", f_v, reads=("Wk", "MEMT0", "MEMT1"), writes=(bkey,))
                S.op("act", lambda e, mc=mc, half=half, bank=bank: e.activation(
                    out=VV[:, mc, half * 512:(half + 1) * 512], in_=PS[bank][:, 0:512], func=AF.Copy),
                    reads=(bkey,), writes=(f"VV{mc}{half}",))

    def stageB1(ti):
        tok = slice(ti * 128, (ti + 1) * 128)
        layer_norm(H[:, ti, :], H[:, ti, :], (f"H{ti}",), (f"H{ti}",), xs=XS_B, xskey="XS1", sc=(ST6c, MVc, RSc),
                   sfx="c")
        to_feature_major(H[:, ti, :], (f"H{ti}",), HT[:, :, tok], (f"HT{ti}",), B_T, "PS_T", hb=MHB, hbkey="MHB")

    KTK = tuple(f"KT{c}" for c in range(8))
    VVK = ("VV00", "VV01", "VV10", "VV11")
    S.dma("sp", lambda e: e.dma_start(out=LNG2[:, :], in_=ln_g["ln2"][0:D].partition_broadcast(128)), writes=("LNG2",))
    S.dma("sp", lambda e: e.dma_start(out=LNB2[:, :], in_=ln_b["ln2"][0:D].partition_broadcast(128)), writes=("LNB2",))
    S.replay(S.record(pre_kv))
    S.replay(S.record(stageB1, 0))
    if NT_MAIN > 1:
        S.replay(S.record(stageB1, 1))
    def stageP(ti):
        tok = slice(ti * 128, (ti + 1) * 128)
        par = ti % 2
        PT, RINV = PTb[par], RINVb[par]
        for g in range(2):
            bank = B_F0 if g == 0 else B_F1
            bkey = "F0" if g == 0 else "F1"

            def f_q(e, g=g, bank=bank):
                last = None
                for c in range(4):
                    for kc in range(8):
                        last = e.matmul(PS[bank][:, c * 128:(c + 1) * 128],
                                        lhsT=Wq_[:, kc, (g * 4 + c) * 128:(g * 4 + c + 1) * 128],
                                        rhs=HT[:, kc, tok], start=(kc == 0), stop=(kc == 7))
                return last
            S.op("pe", f_q, reads=("Wq", f"HT{ti}"), writes=(bkey,))
            S.op("act", lambda e, g=g, bank=bank: e.activation(
                out=QT[:, g * 4:(g + 1) * 4, :], in_=PS[bank][:, :].rearrange("p (a b) -> p a b", a=4),
                func=AF.Copy), reads=(bkey,), writes=(f"QT{g}",))

        def f_s(e):
            last = None
            for hd in range(4):
                bank = B_M0 if hd < 2 else B_M1
                for c in range(2):
                    last = e.matmul(PS[bank][:, (hd % 2) * 256:(hd % 2 + 1) * 256], lhsT=QT[:, 2 * hd + c, :],
                                    rhs=KT[:, 2 * hd + c, :], start=(c == 0), stop=(c == 1))
            return last
        S.op("pe", f_s, reads=("QT0", "QT1") + KTK, writes=("M0", "M1"))
        for half in range(2):
            bank = B_M0 if half == 0 else B_M1
            bkey = "M0" if half == 0 else "M1"
            S.op("dve", lambda e, half=half, bank=bank: e.tensor_reduce(
                out=MX[:, 2 * half:2 * half + 2], in_=PS[bank][:, :].rearrange("p (a b) -> p a b", a=2), axis=AX.X,
                op=ALU.max), reads=(bkey,), writes=(f"MX{half}",))
        S.op("dve", lambda e: e.tensor_scalar(out=NB[:, :], in0=MX[:, :], scalar1=-1.0 / 16.0, scalar2=None,
                                              op0=ALU.mult), reads=("MX0", "MX1"), writes=("NB",))
        for hd in range(4):
            bank = B_M0 if hd < 2 else B_M1
            bkey = "M0" if hd < 2 else "M1"
            sc_ap = PS[bank][:, (hd % 2) * 256:(hd % 2 + 1) * 256]
            S.op("act", lambda e, hd=hd, sc_ap=sc_ap: e.activation(
                out=PEX[:, hd, :], in_=sc_ap, func=AF.Exp, bias=NB[:, hd:hd + 1], scale=1.0 / 16.0,
                accum_out=RSUM[:, hd:hd + 1]), reads=(bkey, "NB"), writes=(f"PEX{hd}", f"RSUM{hd}"))
        S.op("dve", lambda e: e.reciprocal(out=RINV[:, :], in_=RSUM[:, :]),
             reads=tuple(f"RSUM{hd}" for hd in range(4)), writes=(f"RINV{par}",))

        def f_pt(e):
            last = None
            for hd in range(4):
                for mc in range(2):
                    j = hd * 2 + mc
                    last = e.transpose(out=PSb[B_F0][:, j * 128:(j + 1) * 128],
                                       in_=PEX[:, hd, mc * 128:(mc + 1) * 128], identity=ident_bf[:, :])
            return last
        S.op("pe", f_pt, reads=tuple(f"PEX{hd}" for hd in range(4)), writes=("F0",))
        S.op("act", lambda e: e.activation(out=PT[:, :, :], in_=PSb[B_F0][:, :].rearrange("p (a b) -> p a b", a=8),
                                           func=AF.Copy), reads=("F0",), writes=(f"PT{par}",))

    def stageQ(ti):
        tok = slice(ti * 128, (ti + 1) * 128)
        par = ti % 2
        PT, RINV = PTb[par], RINVb[par]

        def f_o2(e):
            last = None
            for hd in range(4):
                bank = B_K0 if hd < 2 else B_K1
                for mc in range(2):
                    last = e.matmul(PS[bank][:, (hd % 2) * 256:(hd % 2 + 1) * 256], lhsT=PT[:, hd * 2 + mc, :],
                                    rhs=VV[:, mc, hd * 256:(hd + 1) * 256], start=(mc == 0), stop=(mc == 1))
            return last
        S.op("pe", f_o2, reads=(f"PT{par}",) + VVK, writes=("K0", "K1"))
        for half in range(2):
            bank = B_K0 if half == 0 else B_K1
            bkey = "K0" if half == 0 else "K1"
            S.op("dve", lambda e, half=half, bank=bank: e.tensor_tensor(
                out=OB[:, half * 512:(half + 1) * 512].rearrange("p (a b) -> p a b", a=2),
                in0=PS[bank][:, :].rearrange("p (a b) -> p a b", a=2),
                in1=RINV[:, 2 * half:2 * half + 2].rearrange("p (h o) -> p h o", o=1).broadcast_to([128, 2, 256]),
                op=ALU.mult), reads=(bkey, f"RINV{par}"), writes=(f"OB{half}",))
        transpose8(OB, ("OB0", "OB1"), HTt_B[:, :, :], ("HTtB",), B_G, "G_a")
        proj_resid(ti, HTt_B, ("HTtB",), Wx_, "Wx")
        layer_norm(H[:, ti, :], H[:, ti, :], (f"H{ti}",), (f"H{ti}",), xs=XBB[0], xskey="XBB", lng=LNG2, lnb=LNB2,
                   gkey="LNG2", bkey="LNB2")
        to_feature_major(H[:, ti, :], (f"H{ti}",), HT[:, :, tok], (f"HT{ti}",), B_G, "G_a", hb=HB_B, hbkey="HB")

    S.replay(S.record(stageP, 0))
    for ti in range(NT_MAIN):
        lists = [S.record(stageQ, ti)]
        if ti + 1 < NT_MAIN:
            lists.append(S.record(stageP, ti + 1))
        if ti + 2 < NT_MAIN:
            lists.append(S.record(stageB1, ti + 2))
        S.replay(*lists)

    if stop_after == "B":
        return dump_H_and_finish()
    barrier()
    cur["XS"] = XS_C
    bcast_load(LNG[:, :], ln_g["ln3"], D, "LNG")
    bcast_load(LNB[:, :], ln_b["ln3"], D, "LNB")
    HTK = tuple(f"HT{t}" for t in range(NT_MAIN))
    hid = HID[0]
    for q in range(4):
        W1 = W1s[q % 2]
        W2 = W2s[q % 2]
        k1 = f"W1_{q % 2}"
        k2 = f"W2_{q % 2}"
        wload(W1, w_ff1, [(0, q * 1024, 1024)], k1)
        S.dma("pool", lambda e, W2=W2, q=q: e.dma_start(
            out=W2[:, :, :], in_=w_ff2[q * 1024:(q + 1) * 1024, :].rearrange("(kc p) n -> p kc n", p=128)),
            writes=(k2,))
        for tg in range(TGN):
            for fc in range(8):
                bank = B_F0 if fc % 2 == 0 else B_F1
                bkey = "F0" if fc % 2 == 0 else "F1"
                rl = RL2[fc % 2]
                rlk = f"RL2_{fc % 2}"
                def f1(e, fc=fc, bank=bank, W1=W1, tg=tg):
                    last = None
                    for kc in range(8):
                        last = e.matmul(PS[bank][:, 0:TPG * 128], lhsT=W1[:, kc, fc * 128:(fc + 1) * 128],
                                        rhs=HT[:, kc, tg * TPG * 128:(tg + 1) * TPG * 128], start=(kc == 0), stop=(kc == 7))
                    return last
                S.op("pe", f1, reads=(k1,) + HTK[tg * TPG:(tg + 1) * TPG], writes=(bkey,))
                S.op("act", lambda e, bank=bank, rl=rl: e.activation(out=rl[:, 0:TPG * 128], in_=PS[bank][:, 0:TPG * 128],
                                                                     func=AF.Relu), reads=(bkey,), writes=(rlk,))
                S.op("dve", lambda e, rl=rl, fc=fc: e.tensor_tensor(out=hid[:, fc, 0:TPG * 128], in0=rl[:, 0:TPG * 128],
                                                                    in1=rl[:, 0:TPG * 128], op=ALU.mult),
                     reads=(rlk,), writes=(f"HID_{fc}",))
            hk_all = tuple(f"HID_{fc}" for fc in range(8))
            for t in range(TPG):
                ti = tg * TPG + t
                for half in range(2):
                    bank = B_K0 if half == 0 else B_K1
                    bkey = "K0" if half == 0 else "K1"
                    def f2(e, t=t, half=half, bank=bank, W2=W2):
                        last = None
                        for fc in range(8):
                            last = e.matmul(PS[bank][:, 0:512], lhsT=hid[:, fc, t * 128:(t + 1) * 128],
                                            rhs=W2[:, fc, half * 512:(half + 1) * 512],
                                            start=(fc == 0), stop=(fc == 7))
                        return last
                    S.op("pe", f2, reads=(k2,) + hk_all, writes=(bkey,))
                    hs = H[:, ti, half * 512:(half + 1) * 512]
                    if q == 0:
                        S.op("dve", lambda e, hs=hs, bank=bank: e.scalar_tensor_tensor(
                            out=hs, in0=hs, scalar=ALPHA, in1=PS[bank][:, 0:512], op0=ALU.mult, op1=ALU.add),
                            reads=(bkey, f"H{ti}"), writes=(f"H{ti}",))
                    else:
                        S.op("dve", lambda e, hs=hs, bank=bank: e.tensor_tensor(
                            out=hs, in0=hs, in1=PS[bank][:, 0:512], op=ALU.add),
                            reads=(bkey, f"H{ti}"), writes=(f"H{ti}",))
                if q == 3:
                    layer_norm(H[:, ti, :], H[:, ti, :], (f"H{ti}",), (f"H{ti}",))
                    S.dma("sp", lambda e, ti=ti: e.dma_start(out=out_d[ti * 128:(ti + 1) * 128, :], in_=H[:, ti, :]),
                          reads=(f"H{ti}",), writes=(f"OUT{ti}",))
    S.fence("sp", [f"OUT{t}" for t in range(NT_MAIN)])

    return finish()


_NC_CACHE = {}


def kernel(**inputs):
    f = lambda k: np.ascontiguousarray(np.asarray(inputs[k], dtype=np.float32))
    x = f("x")
    mem = f("mem")
    if "nc" not in _NC_CACHE:
        _NC_CACHE["nc"] = build_program()
    nc = _NC_CACHE["nc"]
    shared = {
        "w_in": f("w_in")[0], "conv_w": f("conv_w")[0], "conv_b": f("conv_b")[0],
        "m_i_bias": f("m_i_bias")[0], "m_f_bias": f("m_f_bias")[0], "m_norm_g": f("m_norm_g")[0],
        "g_lr_w": f("g_lr_w")[0], "g_lr_b": f("g_lr_b")[0], "g_norm_g": f("g_norm_g")[0],
        "w_out": f("w_out")[0], "x_wq": f("x_wq")[0], "x_wk": f("x_wk")[0], "x_wv": f("x_wv")[0],
        "x_wo": f("x_wo")[0], "w_ff1": f("w_ff1")[0], "w_ff2": f("w_ff2")[0],
        "ln_in_g": f("ln_in_g"), "ln_in_b": f("ln_in_b"),
        "ln1_g": f("ln1_g")[0], "ln1_b": f("ln1_b")[0], "ln2_g": f("ln2_g")[0], "ln2_b": f("ln2_b")[0],
        "ln3_g": f("ln3_g")[0], "ln3_b": f("ln3_b")[0],
    }
    in_maps = []
    SEG = NT_MAIN * 128
    for c in range(8):
        b, s = c // 4, c % 4
        start = s * SEG
        xa = np.zeros((NT * 128, D), np.float32)
        pm = np.zeros((128, NT), np.float32)
        npre = NT_PRE * 128
        have = min(start, npre)
        if have > 0:
            xa[npre - have:npre] = x[b, start - have:start]
            pm[:, NT_PRE - have // 128:NT_PRE] = 1.0
        xa[npre:] = x[b, start:start + SEG]
        pm[:, NT_PRE:] = 1.0
        m = dict(shared)
        m["xall"] = xa
        m["pmask"] = pm
        m["mem"] = mem[b]
        in_maps.append(m)
    res = run_bass_kernel_spmd(nc, in_maps, core_ids=list(range(8)))
    out = np.zeros((2, 8192, D), np.float32)
    for c in range(8):
        b, s = c // 4, c % 4
        out[b, s * SEG:(s + 1) * SEG] = np.asarray(res.results[c]["out"], dtype=np.float32)
    return out
```
